# Optimizing a Trainium2 kernel written in Bass

```python
import jax, jax.numpy as jnp
from jax import lax
import numpy as np

D_MODEL = 1024
BATCH = 4
SEQ = 4096
DEPTH = 1

HG_WIDTH = D_MODEL // 2
HG_HEAD_DIM = 128
HG_HEADS = HG_WIDTH // HG_HEAD_DIM
CONV_WIDTH = D_MODEL // 2
CONV_GROUPS = 8
CONV_K = 3
D_FF = ((-(-8 * D_MODEL // 3) + 255) // 256) * 256
CHUNK = 32
EPS = 1e-6
SPLIT_SIZES = (HG_WIDTH, HG_WIDTH, HG_WIDTH, HG_WIDTH, CONV_WIDTH, CONV_WIDTH, CONV_WIDTH, D_MODEL, D_MODEL)
N_IN = 4 * HG_WIDTH + 3 * CONV_WIDTH + 2 * D_MODEL

kernel_name = 'hybrid_hgrn2_shortconv_gated_block'


def rmsnorm(x, g):
    xf = x.astype(jnp.float32)
    y = xf * lax.rsqrt(jnp.mean(xf * xf, axis=-1, keepdims=True) + EPS)
    return (y * g.astype(jnp.float32)).astype(x.dtype)


def hgrn2_chunkwise(q, k, v, log_f):
    B, L, H, dk = q.shape
    dv = v.shape[-1]
    n = L // CHUNK

    def to_chunks(t):
        return t.reshape(B, n, CHUNK, H, t.shape[-1]).transpose(1, 0, 3, 2, 4)

    qc, kc, vc, gc = to_chunks(q), to_chunks(k), to_chunks(v), to_chunks(log_f)
    b = jnp.cumsum(gc, axis=3)
    anchor = b[:, :, :, CHUNK // 2 - 1:CHUNK // 2, :]
    q_hat = qc * jnp.exp(b - anchor)
    k_hat = kc * jnp.exp(anchor - b)
    scores = jnp.einsum('nbhid,nbhjd->nbhij', q_hat, k_hat)
    causal = jnp.tril(jnp.ones((CHUNK, CHUNK), dtype=bool))
    scores = jnp.where(causal, scores, 0.0)
    o_intra = jnp.einsum('nbhij,nbhjv->nbhiv', scores, vc)
    b_last = b[:, :, :, -1:, :]
    q_in = qc * jnp.exp(b)
    k_out = kc * jnp.exp(b_last - b)
    chunk_decay = jnp.exp(b_last[:, :, :, 0, :])

    def step(S, inp):
        q_i, k_o, v_c, dec = inp
        o = jnp.einsum('bhid,bhdv->bhiv', q_i, S)
        S = dec[..., None] * S + jnp.einsum('bhjd,bhjv->bhdv', k_o, v_c)
        return S, o

    S0 = jnp.zeros((B, H, dk, dv), jnp.float32)
    _, o_inter = lax.scan(step, S0, (q_in, k_out, vc, chunk_decay))
    o = o_intra + o_inter
    return o.transpose(1, 0, 3, 2, 4).reshape(B, L, H, dv)


def hgrn2_mixer(q_raw, f_raw, i_raw, g_raw, lb, norm_g):
    B, L, _ = q_raw.shape

    def heads(t):
        return t.reshape(B, L, HG_HEADS, HG_HEAD_DIM).astype(jnp.float32)

    q = jax.nn.silu(heads(q_raw)) * (HG_HEAD_DIM ** -0.5)
    lbh = lb.astype(jnp.float32).reshape(HG_HEADS, HG_HEAD_DIM)
    f = lbh + (1.0 - lbh) * jax.nn.sigmoid(heads(f_raw))
    o = hgrn2_chunkwise(q, 1.0 - f, heads(i_raw), jnp.log(f))
    o = rmsnorm(o, norm_g) * jax.nn.silu(heads(g_raw))
    return o.reshape(B, L, HG_WIDTH).astype(q_raw.dtype)


def short_conv_mixer(c_gate, b_gate, xb, conv_w):
    u = c_gate * xb
    rhs = conv_w.astype(u.dtype)[:, None, :]
    y = lax.conv_general_dilated(u, rhs, window_strides=(1,), padding=[(CONV_K - 1, 0)],
                                 dimension_numbers=('NWC', 'WIO', 'NWC'),
                                 feature_group_count=CONV_WIDTH)
    return b_gate * y


def swiglu(h, w_gate, w_up, w_down):
    return (jax.nn.silu(h @ w_gate) * (h @ w_up)) @ w_down


def setup_inputs(seed: int = 0) -> dict:
    key = jax.random.key(seed)
    ks = jax.random.split(key, 14)

    def nrm(k, shape, fan):
        return jax.random.normal(k, shape, jnp.float32) * (fan ** -0.5)

    def gain(k, shape):
        return 1.0 + 0.02 * jax.random.normal(k, shape, jnp.float32)

    return {
        'x': jax.random.normal(ks[0], (BATCH, SEQ, D_MODEL), jnp.float32),
        'norm_mix_g': gain(ks[1], (DEPTH, D_MODEL)),
        'w_in': nrm(ks[2], (DEPTH, D_MODEL, N_IN), D_MODEL),
        'lower_bounds': 0.1 * jax.random.normal(ks[3], (DEPTH + 1, HG_WIDTH), jnp.float32),
        'hg_norm_g': gain(ks[4], (DEPTH, HG_HEAD_DIM)),
        'conv_w': nrm(ks[5], (DEPTH, CONV_K, CONV_WIDTH), CONV_K),
        'w_branch_a': nrm(ks[6], (DEPTH, HG_WIDTH, D_MODEL), HG_WIDTH),
        'w_branch_b': nrm(ks[7], (DEPTH, CONV_WIDTH, D_MODEL), CONV_WIDTH),
        'w_out': nrm(ks[8], (DEPTH, D_MODEL, D_MODEL), D_MODEL),
        'norm_ffn_g': gain(ks[9], (DEPTH, D_MODEL)),
        'w_ffn_gate': nrm(ks[10], (DEPTH, D_MODEL, D_FF), D_MODEL),
        'w_ffn_up': nrm(ks[11], (DEPTH, D_MODEL, D_FF), D_MODEL),
        'w_ffn_down': nrm(ks[12], (DEPTH, D_FF, D_MODEL), D_FF),
        'norm_final_g': gain(ks[13], (D_MODEL,)),
    }


def reference(x, norm_mix_g, w_in, lower_bounds, hg_norm_g, conv_w, w_branch_a, w_branch_b,
              w_out, norm_ffn_g, w_ffn_gate, w_ffn_up, w_ffn_down, norm_final_g):
    offsets = [int(o) for o in np.cumsum(SPLIT_SIZES)[:-1]]
    lb_all = jnp.cumsum(jax.nn.softmax(lower_bounds.astype(jnp.float32), axis=0), axis=0)
    for l in range(DEPTH):
        h = rmsnorm(x, norm_mix_g[l])
        proj = h @ w_in[l]
        q_raw, f_raw, i_raw, g_raw, c_gate, b_gate, xb, gate_a, gate_b = jnp.split(proj, offsets, axis=-1)
        y_a = hgrn2_mixer(q_raw, f_raw, i_raw, g_raw, lb_all[l], hg_norm_g[l]) @ w_branch_a[l]
        y_b = short_conv_mixer(c_gate, b_gate, xb, conv_w[l]) @ w_branch_b[l]
        merged = jax.nn.sigmoid(gate_a) * y_a + jax.nn.sigmoid(gate_b) * y_b
        x = x + merged @ w_out[l]
        h2 = rmsnorm(x, norm_ffn_g[l])
        x = x + swiglu(h2, w_ffn_gate[l], w_ffn_up[l], w_ffn_down[l])
    return rmsnorm(x, norm_final_g)
```

```python
import numpy as np
from contextlib import ExitStack
import concourse.bass as bass
import concourse.mybir as mybir
from concourse.bass_utils import run_bass_kernel_spmd

F32 = mybir.dt.float32
BF16 = mybir.dt.bfloat16
ALU = mybir.AluOpType
AF = mybir.ActivationFunctionType

ENGS = ("pe", "act", "dve", "pool", "sp")

D = 1024
NIN = 5632
DFF = 2816
T = 512
NT = T // 128
CH = 64
NCH = T // CH
NMAIN = 4
NPRE = 4
TOK = NMAIN * T
R = 8
NTMP = 14
NXS = 4

G1, G2, L0, L1, HGN, CW, EPS, ONE, GF = 0, 8, 16, 20, 24, 25, 37, 38, 39
NCST = GF + D


class Buf:
    __slots__ = ("name", "w", "r", "dsem")

    def __init__(self, name):
        self.name = name
        self.w = None
        self.r = {}
        self.dsem = None


class Prog:
    def __init__(self, nc, stack):
        self.nc = nc
        self.stack = stack
        self.streams = {e: [] for e in ENGS}
        self.sems = {}
        self.count = {}
        self.known = {e: {} for e in ENGS}
        for e in ENGS:
            self._newsem("P_" + e)

    def _newsem(self, key):
        h = self.stack.enter_context(self.nc.semaphore(key))
        self.sems[key] = h
        self.count[key] = 0
        return h

    def _deps(self, eng, reads, writes):
        deps = {}
        for b in reads:
            if b.w is not None:
                k, v = b.w
                if deps.get(k, 0) < v:
                    deps[k] = v
        for b in writes:
            if b.w is not None:
                k, v = b.w
                if deps.get(k, 0) < v:
                    deps[k] = v
            for k, v in b.r.items():
                if deps.get(k, 0) < v:
                    deps[k] = v
        own = "P_" + eng
        kn = self.known[eng]
        for k, v in deps.items():
            if k == own and eng == "pe":
                continue
            if kn.get(k, 0) >= v:
                continue
            kn[k] = v
            self.streams[eng].append(("wait", self.sems[k], v))

    def op(self, eng, fn, reads=(), writes=()):
        self._deps(eng, reads, writes)
        key = "P_" + eng
        self.count[key] += 1
        t = self.count[key]
        self.streams[eng].append(("inc", fn, self.sems[key]))
        for b in reads:
            if b.r.get(key, 0) < t:
                b.r[key] = t
        for b in writes:
            b.w = (key, t)
            b.r = {}
        return t

    def dma(self, eng, fn, reads=(), writes=(), chan=None):
        self._deps(eng, reads, writes)
        if chan.dsem is None:
            chan.dsem = "D_" + chan.name
            self._newsem(chan.dsem)
        key = chan.dsem
        self.count[key] += 16
        t = self.count[key]
        self.streams[eng].append(("dma", fn, self.sems[key]))
        for b in reads:
            if b.r.get(key, 0) < t:
                b.r[key] = t
        for b in writes:
            b.w = (key, t)
            b.r = {}
        return t

    def final_wait(self, eng, bufs):
        self._deps(eng, (), bufs)

    def emit(self, block):
        def run(E, stream):
            for item in stream:
                kind = item[0]
                if kind == "wait":
                    E.wait_ge(item[1], item[2])
                elif kind == "inc":
                    item[1](E).then_inc(item[2], 1)
                else:
                    item[1](E).then_inc(item[2], 16)

        @block.tensor
        def _(E):
            run(E, self.streams["pe"])

        @block.scalar
        def _(E):
            run(E, self.streams["act"])

        @block.vector
        def _(E):
            run(E, self.streams["dve"])

        @block.gpsimd
        def _(E):
            run(E, self.streams["pool"])

        @block.sync
        def _(E):
            run(E, self.streams["sp"])


def slab_specs():
    sp = {}
    names = ["q", "f", "i", "g", "c", "bg", "xb", "ga0", "ga1", "gb0", "gb1"]
    for n, nm in enumerate(names):
        sp[nm] = ("w_in", 0, 8, n * 512, 512)
    for n in range(2):
        sp[f"wa{n}"] = ("w_a", 0, 4, n * 512, 512)
        sp[f"wb{n}"] = ("w_b", 0, 4, n * 512, 512)
        sp[f"wo{n}"] = ("w_out", 0, 8, n * 512, 512)
    for j in range(6):
        nc_ = 512 if j < 5 else 256
        sp[f"fg{j}"] = ("w_g", 0, 8, j * 512, nc_)
        sp[f"fu{j}"] = ("w_u", 0, 8, j * 512, nc_)
    for n in range(2):
        for k in range(3):
            kc = 8 if k < 2 else 6
            sp[f"fd{n}{k}"] = ("w_d", k * 1024, kc, n * 512, 512)
    return sp


MAIN_ORDER = (["f", "i", "q", "g", "c", "xb", "bg",
               "ga0", "wa0", "gb0", "wb0", "ga1", "wa1", "gb1", "wb1", "wo0", "wo1"]
              + [x for j in range(6) for x in (f"fg{j}", f"fu{j}")]
              + [f"fd{n}{k}" for n in range(2) for k in range(3)])


def build_nc():
    nc = bass.Bass("TRN2", target_bir_lowering=False)
    x_main = nc.dram_tensor("x_main", [TOK, D], F32, kind="ExternalInput").ap()
    x_prev = nc.dram_tensor("x_prev", [NPRE * T, D], F32, kind="ExternalInput").ap()
    cst_d = nc.dram_tensor("cst", [128, NCST], F32, kind="ExternalInput").ap()
    W = {
        "w_in": nc.dram_tensor("w_in", [D, NIN], F32, kind="ExternalInput").ap(),
        "w_a": nc.dram_tensor("w_a", [512, D], F32, kind="ExternalInput").ap(),
        "w_b": nc.dram_tensor("w_b", [512, D], F32, kind="ExternalInput").ap(),
        "w_out": nc.dram_tensor("w_out", [D, D], F32, kind="ExternalInput").ap(),
        "w_g": nc.dram_tensor("w_g", [D, DFF], F32, kind="ExternalInput").ap(),
        "w_u": nc.dram_tensor("w_u", [D, DFF], F32, kind="ExternalInput").ap(),
        "w_d": nc.dram_tensor("w_d", [DFF, D], F32, kind="ExternalInput").ap(),
    }
    out_d = nc.dram_tensor("out", [TOK, D], F32, kind="ExternalOutput").ap()
    dbg_out = {}
    if DEBUG:
        for nm, shp, dt in [("d_qin", [128, 4 * T], BF16), ("d_khat", [128, 4 * T], BF16), ("d_v", [128, NT * 512], BF16),
                            ("d_og", [128, 4 * T], BF16), ("d_gsil", [128, 4 * T], BF16), ("d_kout", [128, NT * 512], BF16),
                            ("d_hT", [128, 8 * T], BF16), ("d_S", [128, 2 * 4 * 128], F32), ("d_yb", [128, 4 * T], BF16),
                            ("d_merged", [128, 8 * T], BF16)]:
            dbg_out[nm] = nc.dram_tensor(nm, shp, dt, kind="ExternalOutput").ap()
    SPECS = slab_specs()
    cache_names = list(SPECS.keys())
    wcache = nc.dram_tensor("wcache", [len(cache_names), 128, 8 * 512], BF16).ap()
    cache_idx = {n: i for i, n in enumerate(cache_names)}

    with ExitStack() as st:
        def sb(name, shape, dt):
            return st.enter_context(nc.sbuf_tensor(name, shape, dt))

        def ps(name, shape, dt):
            return st.enter_context(nc.psum_tensor(name, shape, dt))

        cst = sb("cst_sb", [128, NCST], F32)
        identf = sb("identf", [128, 128], F32)
        ident = sb("ident", [128, 128], BF16)
        ones_f = sb("ones_f", [128, 128], F32)
        maskT = sb("maskT", [128, 128], F32)
        smask = sb("smask", [128, T], F32)
        diagw = sb("diagw", [128, 12, 128], BF16)
        sv = sb("sv", [128, 4, 4], F32)
        S = sb("S", [128, 2, 4, 128], F32)
        Sb = sb("Sb", [128, 2, 4, 128], BF16)
        decs = sb("decs", [128, 2, 4, NCH], F32)
        xt = [sb(f"xt{i}", [128, D], F32) for i in range(2)]
        xs = [sb(f"xs{i}", [128, D], BF16) for i in range(NXS)]
        stat = [sb(f"stat{i}", [128, 4], F32) for i in range(NXS)]
        junk = sb("junk", [128, 2 * T], BF16)
        omask = sb("omask", [128, T], F32)
        gdec = sb("gdec", [128, 4, 4], F32)
        arenaA = sb("arenaA", [128, 22 * T], BF16)
        act = arenaA[:, :].rearrange("p (j t) -> p j t", t=T)
        qin = arenaA[:, 0:4 * T].rearrange("p (h t) -> p h t", t=T)
        khat = arenaA[:, 4 * T:8 * T].rearrange("p (h t) -> p h t", t=T)
        kout = arenaA[:, 8 * T:12 * T].rearrange("p (t d) -> p t d", d=512)
        v = arenaA[:, 12 * T:16 * T].rearrange("p (t d) -> p t d", d=512)
        gsil = arenaA[:, 16 * T:20 * T].rearrange("p (h t) -> p h t", t=T)
        koutT = [arenaA[:, 20 * T:21 * T], arenaA[:, 21 * T:22 * T]]
        arenaB = sb("arenaB", [128, 8 * T], BF16)
        h2T = arenaB[:, :].rearrange("p (c t) -> p c t", t=T)
        og = arenaB[:, 0:4 * T].rearrange("p (h t) -> p h t", t=T)
        yb = arenaB[:, 4 * T:8 * T].rearrange("p (h t) -> p h t", t=T)
        hTs = [sb(f"hT{i}", [128, 8, T], BF16) for i in range(2)]
        u = sb("u", [128, 4, T + 2], BF16)
        merged = sb("merged", [128, 8, T], BF16)
        x1 = sb("x1", [128, NT, D], F32)
        AT = [sb(f"AT{i}", [128, 128], BF16) for i in range(4)]
        tmp = [sb(f"tmp{i}", [128, T], F32) for i in range(NTMP)]
        ring = [sb(f"ring{i}", [128, 8, 512], BF16) for i in range(R)]
        FB = [ps(f"pf{i}", [128, 512], F32) for i in range(6)]
        TBs = [ps(f"ptb{i}", [128, 1024], BF16) for i in range(2)]

        block = st.enter_context(nc.Block())
        P = Prog(nc, st)

        b_cst, b_ident, b_identf, b_ones, b_maskT, b_smask, b_diagw, b_sv = (Buf(n) for n in
            ["cst", "ident", "identf", "ones", "maskT", "smask", "diagw", "sv"])
        b_S = [[Buf(f"S{p}{h}") for h in range(4)] for p in range(2)]
        b_Sb = [[Buf(f"Sb{p}{h}") for h in range(4)] for p in range(2)]
        b_decs = [[Buf(f"decs{p}{h}") for h in range(4)] for p in range(2)]
        b_xt = [Buf(f"xt{i}") for i in range(2)]
        b_xs = [Buf(f"xs{i}") for i in range(NXS)]
        b_stat = [Buf(f"stat{i}") for i in range(NXS)]
        b_act = [Buf(f"act{j}") for j in range(22)]
        b_hTs = [Buf("hT0"), Buf("hT1")]
        b_qin = [Buf(f"qin{h}") for h in range(4)]
        b_khat = [Buf(f"khat{h}") for h in range(4)]
        b_kout = [Buf(f"kout{h}") for h in range(4)]
        b_koutT = [Buf(f"koutT{i}") for i in range(2)]
        b_v = [Buf(f"v{t}") for t in range(NT)]
        b_gsil = [Buf(f"gsil{h}") for h in range(4)]
        b_og = [Buf(f"og{h}") for h in range(4)]
        b_u = Buf("u")
        b_omask = Buf("omask")
        b_gdecs = [Buf(f"gdec{h}") for h in range(4)]
        b_yb = [Buf(f"yb{m}") for m in range(4)]
        b_merged = [Buf(f"merged{m}") for m in range(8)]
        b_x1 = [Buf(f"x1{t}") for t in range(NT)]
        b_h2T = Buf("h2T")
        b_AT = [Buf(f"AT{i}") for i in range(4)]
        b_tmp = [Buf(f"tmp{i}") for i in range(NTMP)]
        b_ring = [Buf(f"ring{i}") for i in range(R)]
        b_FB = [Buf(f"FB{i}") for i in range(6)]
        b_pS = [[Buf(f"pS{k}{h}") for h in range(4)] for k in range(2)]
        b_TBs = [Buf("TB0"), Buf("TB1")]
        b_cache = {n: Buf("wc_" + n) for n in cache_names}
        A_CH = b_qin + b_khat + b_kout + b_koutT + b_v + b_gsil
        B_CH = b_og + b_yb

        cnt = {"tmp": 0, "fb": 0, "xt": 0, "at": 0, "kt": 0, "st": 0, "os": 0, "tb": 0, "fbmod": 6}

        def TB_():
            i = cnt["tb"] % 2
            cnt["tb"] += 1
            return TBs[i], b_TBs[i]

        def T_():
            i = cnt["tmp"] % NTMP
            cnt["tmp"] += 1
            return tmp[i], b_tmp[i]

        def F_():
            i = cnt["fb"] % cnt["fbmod"]
            cnt["fb"] += 1
            return FB[i], b_FB[i]

        NPC = 4
        pc_list = [n for n in MAIN_ORDER if n not in ("f", "i", "c", "xb")][:NPC * NPRE]
        sched = []
        for pb in range(NPRE):
            sched += ["f", "i"]
            if pb == 0:
                sched += ["c", "xb"]
            sched += ["PC:" + n for n in pc_list[pb * NPC:(pb + 1) * NPC]]
        for mb in range(NMAIN):
            sched += MAIN_ORDER
        ring_state = {"issued": 0, "acq": 0}
        cached = set()
        ch_sw = [Buf(f"rsw{i}") for i in range(R)]
        ch_hw = [Buf(f"rhw{i}") for i in range(R)]

        pending = {"store": None}

        def flush_store():
            if pending["store"] is not None:
                fn_, reads_, writes_, chan_ = pending["store"]
                P.dma("pool", fn_, reads=reads_, writes=writes_, chan=chan_)
                pending["store"] = None

        def issue_next():
            g = ring_state["issued"]
            if g >= len(sched):
                flush_store()
                return
            ring_state["issued"] += 1
            name = sched[g]
            if name.startswith("PC:"):
                name = name[3:]
            s = g % R
            wkey, r0, kc, c0, ncols = SPECS[name]
            dst = ring[s][:, 0:kc, 0:ncols]
            ci = cache_idx[name]
            cview = wcache[ci].rearrange("p (c n) -> p c n", n=512)[:, 0:kc, 0:ncols]
            if name in cached:
                flush_store()
                P.dma("sp", lambda E: E.dma_start(out=dst, in_=cview), reads=[b_cache[name]],
                      writes=[b_ring[s]], chan=ch_hw[s])
            else:
                src = W[wkey][r0:r0 + kc * 128, :].rearrange("(c p) n -> p c n", p=128)[:, :, c0:c0 + ncols]
                P.dma("pool", lambda E: E.dma_start(out=dst, in_=src), writes=[b_ring[s]], chan=ch_sw[s])
                cached.add(name)
                flush_store()
                pending["store"] = (lambda E: E.dma_start(out=cview, in_=dst), [b_ring[s]], [b_cache[name]], ch_sw[s])

        def acquire(name):
            g = ring_state["acq"]
            assert sched[g] == name, (g, sched[g], name)
            ring_state["acq"] += 1
            assert g < ring_state["issued"], "ring underflow"
            return ring[g % R], b_ring[g % R]

        def precache(n):
            for _ in range(n):
                g = ring_state["acq"]
                assert sched[g].startswith("PC:"), (g, sched[g])
                ring_state["acq"] += 1
                issue_next()

        def release(n=1):
            for _ in range(n):
                issue_next()

        P.dma("sp", lambda E: E.dma_start(out=cst[:], in_=cst_d), writes=[b_cst], chan=b_cst)
        P.op("dve", lambda E: E.memset(identf[:], 0.0), writes=[b_identf])
        P.op("pool", lambda E: E.affine_select(out=identf[:], in_=identf[:], pattern=[[-1, 128]],
                                               compare_op=ALU.not_equal, fill=1.0, base=0, channel_multiplier=1),
             reads=[b_identf], writes=[b_identf])
        P.op("dve", lambda E: E.tensor_copy(out=ident[:], in_=identf[:]), reads=[b_identf], writes=[b_ident])
        P.op("dve", lambda E: E.memset(ones_f[:], 1.0), writes=[b_ones])
        P.op("dve", lambda E: E.memset(maskT[:], 1.0), writes=[b_maskT])
        P.op("pool", lambda E: E.affine_select(out=maskT[:], in_=maskT[:], pattern=[[1, 128]],
                                               compare_op=ALU.is_ge, fill=0.0, base=0, channel_multiplier=-1),
             reads=[b_maskT], writes=[b_maskT])
        P.op("dve", lambda E: E.memset(maskT[0:64, 64:128], 0.0), reads=[b_maskT], writes=[b_maskT])
        P.op("dve", lambda E: E.memset(smask[:], 1.0), writes=[b_smask])
        P.op("dve", lambda E: E.memset(smask[:].rearrange("p (c j) -> p c j", j=CH)[:, :, 0:1], 0.0),
             reads=[b_smask], writes=[b_smask])
        for k in range(12):
            P.op("dve", lambda E, k=k: E.tensor_scalar(out=diagw[:, k, :], in0=identf[:], scalar1=cst[:, CW + k:CW + k + 1],
                                                       scalar2=0.0, op0=ALU.mult, op1=ALU.add),
                 reads=[b_identf, b_cst], writes=[b_diagw])
        P.op("dve", lambda E: E.tensor_tensor(out=sv[:, :, 3], in0=cst[:, L0:L0 + 4], in1=cst[:, L1:L1 + 4], op=ALU.subtract),
             reads=[b_cst], writes=[b_sv])
        P.op("act", lambda E: E.activation(out=sv[:, :, 1], in_=sv[:, :, 3], func=AF.Sigmoid), reads=[b_sv], writes=[b_sv])
        P.op("act", lambda E: E.activation(out=sv[:, :, 0], in_=sv[:, :, 3], func=AF.Sigmoid, scale=-1.0),
             reads=[b_sv], writes=[b_sv])
        P.op("dve", lambda E: E.tensor_scalar(out=sv[:, :, 2], in0=sv[:, :, 0], scalar1=-1.0, scalar2=0.0, op0=ALU.mult, op1=ALU.add),
             reads=[b_sv], writes=[b_sv])
        P.op("dve", lambda E: E.memset(omask[:], 1.0), writes=[b_omask])
        P.op("dve", lambda E: E.memset(gdec[:], 1.0), writes=b_gdecs)
        P.op("dve", lambda E: E.memset(S[:, 0, :, :], 0.0), writes=b_S[0])
        P.op("dve", lambda E: E.memset(Sb[:, 0, :, :], 0.0), writes=b_Sb[0])
        P.op("dve", lambda E: E.memset(u[:], 0.0), writes=[b_u])
        chain = {"step": 0}

        def norm_stat(src_tile_ap, src_reads):
            k = cnt["st"] % NXS
            cnt["st"] += 1
            P.op("act", lambda E: E.activation(out=junk[:, :], in_=src_tile_ap, func=AF.Square, accum_out=stat[k][:, 2:3]),
                 reads=src_reads, writes=[b_stat[k]])
            P.op("act", lambda E: E.activation(out=stat[k][:, 3:4], in_=stat[k][:, 2:3], func=AF.Ln, scale=1.0 / D,
                                               bias=cst[:, EPS:EPS + 1]),
                 reads=[b_stat[k], b_cst], writes=[b_stat[k]])
            P.op("act", lambda E: E.activation(out=stat[k][:, 3:4], in_=stat[k][:, 3:4], func=AF.Exp, scale=-0.5),
                 reads=[b_stat[k]], writes=[b_stat[k]])
            return k

        def norm_xs(k, src_tile_ap, src_reads):
            P.op("dve", lambda E: E.tensor_scalar(out=xs[k][:], in0=src_tile_ap, scalar1=stat[k][:, 3:4], scalar2=0.0,
                                                  op0=ALU.mult, op1=ALU.add),
                 reads=list(src_reads) + [b_stat[k]], writes=[b_xs[k]])

        def norm_a(src_tile_ap, src_reads):
            k = norm_stat(src_tile_ap, src_reads)
            norm_xs(k, src_tile_ap, src_reads)
            return k

        def norm_b(k, gcol, dstT, b_dst, t, extra_w=()):
            TB, b_TB = TB_()

            def tr(E):
                last = None
                for c in range(8):
                    last = E.transpose(out=TB[:, c * 128:(c + 1) * 128], in_=xs[k][:, c * 128:(c + 1) * 128], identity=ident[:])
                return last
            P.op("pe", tr, reads=[b_xs[k], b_ident], writes=[b_TB])
            P.op("dve", lambda E: E.tensor_tensor(
                out=dstT[:, :, t * 128:(t + 1) * 128],
                in0=TB[:].rearrange("p (c t) -> p c t", t=128),
                in1=cst[:, gcol:gcol + 8].rearrange("p (c o) -> p c o", o=1).to_broadcast([128, 8, 128]),
                op=ALU.mult), reads=[b_TB, b_cst], writes=[b_dst] + list(extra_w))

        def load_x(src, r0):
            k = cnt["xt"] % 2
            cnt["xt"] += 1
            P.dma("sp", lambda E: E.dma_start(out=xt[k][:], in_=src[r0:r0 + 128, :]), writes=[b_xt[k]], chan=b_xt[k])
            return k

        def proj_fm(slab, b_slab, kc, col0, rhsT, b_rhs):
            pf, b_pf = F_()

            def mm(E):
                last = None
                for c in range(kc):
                    last = E.matmul(pf[:, :], lhsT=slab[:, c, col0:col0 + 128], rhs=rhsT[:, c, :],
                                    start=(c == 0), stop=(c == kc - 1))
                return last
            P.op("pe", mm, reads=[b_slab] + list(b_rhs), writes=[b_pf])
            return pf, b_pf

        def stage0_load(src, row0, tiles, st):
            for t in tiles:
                st["xk"][t] = load_x(src, row0 + t * 128)

        def stage0_norm(tiles, st):
            for t in tiles:
                k = st["xk"][t]
                st["ks"][t] = norm_stat(xt[k][:, :], [b_xt[k]])
            for t in tiles:
                k = st["xk"][t]
                norm_xs(st["ks"][t], xt[k][:, :], [b_xt[k]])

        def stage0_a(src, row0):
            st = {"xk": {}, "ks": {}}
            stage0_load(src, row0, [0, 1], st)
            stage0_norm([0, 1], st)
            stage0_load(src, row0, [2, 3], st)
            stage0_norm([2, 3], st)
            return [st["ks"][t] for t in range(NT)]

        def stage0_b(ks, hp):
            for t in range(NT):
                norm_b(ks[t], G1, hTs[hp], b_hTs[hp], t)

        def lockstep(*lists):
            n = max(len(l) for l in lists)
            for i in range(n):
                for l in lists:
                    if i < len(l):
                        l[i]()

        def make_vtile(hT, b_hT, si, b_si, on_dve=False):
            def vtile(t):
                pv, b_pv = F_()

                def mmv(E):
                    last = None
                    for c in range(8):
                        last = E.matmul(pv[:, :], lhsT=hT[:, c, t * 128:(t + 1) * 128], rhs=si[:, c, :],
                                        start=(c == 0), stop=(c == 7))
                    return last
                P.op("pe", mmv, reads=[b_si, b_hT], writes=[b_pv])
                if on_dve:
                    P.op("dve", lambda E: E.tensor_copy(out=v[:, t, :], in_=pv[:, :]), reads=[b_pv], writes=[b_v[t]] + b_act)
                else:
                    P.op("act", lambda E: E.activation(out=v[:, t, :], in_=pv[:, :], func=AF.Copy),
                         reads=[b_pv], writes=[b_v[t]] + b_act)
            return vtile

        def gate_p1(h, pf, b_pf, r_e, rb_e, r_sg, rb_sg, r_km, rb_km, r_lf, rb_lf):
            return [
                lambda: P.op("act", lambda E: E.activation(out=r_e[:, :], in_=pf[:, :], func=AF.Exp, scale=-1.0),
                             reads=[b_pf], writes=[rb_e]),
                lambda: P.op("act", lambda E: E.activation(out=r_e[:, :], in_=r_e[:, :], func=AF.Ln, bias=cst[:, ONE:ONE + 1]),
                             reads=[rb_e, b_cst], writes=[rb_e]),
                lambda: P.op("act", lambda E: E.activation(out=r_sg[:, :], in_=r_e[:, :], func=AF.Exp, scale=-1.0),
                             reads=[rb_e], writes=[rb_sg]),
                lambda: P.op("act", lambda E: E.activation(out=r_km[:, :], in_=r_sg[:, :], func=AF.Identity, scale=sv[:, h, 2:3],
                                                           bias=sv[:, h, 0:1]),
                             reads=[rb_sg, b_sv], writes=[rb_km]),
                lambda: P.op("act", lambda E: E.activation(out=r_lf[:, :], in_=r_sg[:, :], func=AF.Ln, scale=sv[:, h, 0:1],
                                                           bias=sv[:, h, 1:2]),
                             reads=[rb_sg, b_sv], writes=[rb_lf]),
            ]

        def kout_tail(h, kk):
            hc = h * 128
            st_ = {}

            def t1():
                TB, b_TB = TB_()
                st_["tb"] = (TB, b_TB)

                def trk(E):
                    last = None
                    for t in range(NT):
                        last = E.transpose(out=TB[:, t * 128:(t + 1) * 128], in_=koutT[kk][:, t * 128:(t + 1) * 128],
                                           identity=ident[:])
                    return last
                P.op("pe", trk, reads=[b_koutT[kk], b_ident], writes=[b_TB])

            def t2():
                TB, b_TB = st_["tb"]
                P.op("act", lambda E: E.activation(out=kout[:, :, hc:hc + 128],
                                                   in_=TB[:, 0:512].rearrange("p (t d) -> p t d", d=128), func=AF.Copy),
                     reads=[b_TB], writes=[b_kout[h]] + b_act)
            return [t1, t2]

        def stage1(blk_par, hp, main, hooks=None):
            hT = hTs[hp]
            b_hT = b_hTs[hp]
            sf, b_sf = acquire("f")
            si, b_si = acquire("i")
            vtile = make_vtile(hT, b_hT, si, b_si)
            r_e, rb_e = [tmp[0], tmp[1]], [b_tmp[0], b_tmp[1]]
            r_sg, rb_sg = [tmp[2], tmp[3]], [b_tmp[2], b_tmp[3]]
            r_km, rb_km = [tmp[4], tmp[5]], [b_tmp[4], b_tmp[5]]
            r_lf, rb_lf = [tmp[6], tmp[7]], [b_tmp[6], b_tmp[7]]
            r_ee, rb_ee = tmp[8:12], b_tmp[8:12]
            pfs = [proj_fm(sf, b_sf, 8, h * 128, hT, [b_hT]) for h in range(4)]

            def p1(h):
                p = h % 2
                return gate_p1(h, pfs[h][0], pfs[h][1], r_e[p], rb_e[p], r_sg[p], rb_sg[p], r_km[p], rb_km[p], r_lf[p], rb_lf[p])

            def p2(h):
                p = h % 2
                bb, b_bb = r_e[p], rb_e[p]
                ei, b_ei = r_sg[p], rb_sg[p]
                ee, b_ee = r_ee[h], rb_ee[h]
                kk = p
                ops = [
                    lambda: P.op("dve", lambda E: E.tensor_tensor_scan(out=bb[:, :], data0=smask[:, :], data1=r_lf[p][:, :], initial=0.0,
                                                                       op0=ALU.mult, op1=ALU.add),
                                 reads=[b_smask, rb_lf[p]], writes=[b_bb]),
                    lambda: P.op("act", lambda E: E.activation(out=ee[:, :], in_=bb[:, :], func=AF.Exp), reads=[b_bb], writes=[b_ee]),
                    lambda: P.op("act", lambda E: E.activation(out=ei[:, :], in_=bb[:, :], func=AF.Exp, scale=-1.0),
                                 reads=[b_bb], writes=[b_ei]),
                    lambda: P.op("dve", lambda E: E.tensor_tensor(out=khat[:, h, :], in0=r_km[p][:, :], in1=ei[:, :], op=ALU.mult),
                                 reads=[rb_km[p], b_ei], writes=[b_khat[h]] + b_act),
                    lambda: P.op("dve", lambda E: E.tensor_copy(
                        out=decs[:, blk_par, h, :],
                        in_=ee[:, :].rearrange("p (c j) -> p c j", j=CH)[:, :, CH - 1]),
                        reads=[b_ee], writes=[b_decs[blk_par][h]]),
                    lambda: P.op("dve", lambda E: E.tensor_tensor(
                        out=koutT[kk].rearrange("p (c j) -> p c j", j=CH),
                        in0=khat[:, h, :].rearrange("p (c j) -> p c j", j=CH),
                        in1=decs[:, blk_par, h, :].rearrange("p (c o) -> p c o", o=1).to_broadcast([128, NCH, CH]),
                        op=ALU.mult), reads=[b_khat[h], b_decs[blk_par][h]], writes=[b_koutT[kk]] + b_act),
                ]
                return ops + kout_tail(h, kk)

            hooks = hooks or {}
            hk = lambda name: hooks[name]() if name in hooks else None
            hk("start")
            lockstep(p1(0), p1(1))
            vtile(0)
            lockstep(p2(0), p2(1))
            hk("mid1")
            lockstep(p1(2), p1(3))
            vtile(1)
            vtile(2)
            hk("mid2")
            lockstep(p2(2), p2(3))
            vtile(3)
            release(2)
            hk("end")
            if main:
                sq_, b_sq = acquire("q")
                sg_, b_sg = acquire("g")

                def QG(h):
                    hc = h * 128
                    qs, b_qs = (tmp[0], b_tmp[0]) if h % 2 == 0 else (tmp[1], b_tmp[1])
                    pq, b_pq = proj_fm(sq_, b_sq, 8, hc, hT, [b_hT])
                    P.op("act", lambda E: E.activation(out=qs[:, :], in_=pq[:, :], func=AF.Silu), reads=[b_pq], writes=[b_qs])
                    pg, b_pg = proj_fm(sg_, b_sg, 8, hc, hT, [b_hT])
                    P.op("act", lambda E: E.activation(out=gsil[:, h, :], in_=pg[:, :], func=AF.Silu),
                         reads=[b_pg], writes=[b_gsil[h]] + b_act)
                    P.op("dve", lambda E: E.scalar_tensor_tensor(out=qin[:, h, :], in0=qs[:, :], scalar=float(128 ** -0.5),
                                                                 in1=r_ee[h][:, :], op0=ALU.mult, op1=ALU.mult),
                         reads=[b_qs, rb_ee[h]], writes=[b_qin[h]] + b_act)
                for h in range(4):
                    QG(h)
                release(2)

        def stage1_pre(hp, first, hooks=None):
            hT = hTs[hp]
            b_hT = b_hTs[hp]
            sf, b_sf = acquire("f")
            si, b_si = acquire("i")
            vtile = make_vtile(hT, b_hT, si, b_si, on_dve=True)
            ie, ikm, ilf = [0, 1, 8, 9], [4, 5, 10, 11], [6, 7, 12, 13]
            r_e, rb_e = [tmp[i] for i in ie], [b_tmp[i] for i in ie]
            r_sg, rb_sg = [tmp[2], tmp[3]], [b_tmp[2], b_tmp[3]]
            r_km, rb_km = [tmp[i] for i in ikm], [b_tmp[i] for i in ikm]
            r_lf, rb_lf = [tmp[i] for i in ilf], [b_tmp[i] for i in ilf]
            pS_, b_pS_ = FB[5], b_FB[5]
            pfs = [proj_fm(sf, b_sf, 8, h * 128, hT, [b_hT]) for h in range(4)]

            def p1(h):
                p = h % 2
                return gate_p1(h, pfs[h][0], pfs[h][1], r_e[h], rb_e[h], r_sg[p], rb_sg[p], r_km[h], rb_km[h], r_lf[h], rb_lf[h])

            def p2(h):
                p = h % 2
                bb, b_bb = r_e[h], rb_e[h]
                lf, b_lf = r_lf[h], rb_lf[h]
                bg = b_gdecs[h]
                kk = p
                ops = [
                    lambda: P.op("dve", lambda E: E.tensor_tensor_scan(out=bb[:, :], data0=omask[:, :], data1=lf[:, :], initial=0.0,
                                                                       op0=ALU.mult, op1=ALU.add),
                                 reads=[b_omask, b_lf], writes=[b_bb]),
                    lambda: P.op("dve", lambda E: E.tensor_scalar(out=gdec[:, h, 2:3], in0=bb[:, T - 1:T], scalar1=-40.0, scalar2=0.0,
                                                                  op0=ALU.max, op1=ALU.add), reads=[b_bb, bg], writes=[bg]),
                    lambda: P.op("dve", lambda E: E.tensor_scalar(out=lf[:, :], in0=bb[:, :], scalar1=bb[:, T - 1:T], scalar2=40.0,
                                                                  op0=ALU.subtract, op1=ALU.min), reads=[b_bb], writes=[b_lf]),
                    lambda: P.op("act", lambda E: E.activation(out=lf[:, :], in_=lf[:, :], func=AF.Exp, scale=-1.0),
                                 reads=[b_lf], writes=[b_lf]),
                    lambda: P.op("dve", lambda E: E.tensor_scalar(out=lf[:, :], in0=lf[:, :], scalar1=gdec[:, h, 0:1], scalar2=1e-30,
                                                                  op0=ALU.mult, op1=ALU.max),
                                 reads=[b_lf, bg], writes=[b_lf]),
                    lambda: P.op("dve", lambda E: E.tensor_tensor(out=koutT[kk], in0=r_km[h][:, :], in1=lf[:, :], op=ALU.mult),
                                 reads=[rb_km[h], b_lf], writes=[b_koutT[kk]] + b_act),
                    lambda: P.op("act", lambda E: E.activation(out=gdec[:, h, 1:2], in_=gdec[:, h, 2:3], func=AF.Exp),
                                 reads=[bg], writes=[bg]),
                    lambda: P.op("dve", lambda E: E.tensor_scalar(out=gdec[:, h, 0:1], in0=gdec[:, h, 0:1], scalar1=gdec[:, h, 1:2],
                                                                  scalar2=1e-30, op0=ALU.mult, op1=ALU.max),
                                 reads=[bg], writes=[bg]),
                ]
                return ops + kout_tail(h, kk)

            hooks = hooks or {}
            hk = lambda name: hooks[name]() if name in hooks else None
            hk("start")
            lockstep(p1(0), p1(1))
            vtile(0)
            hk("mid1")
            lockstep(p2(0), p2(1), p1(2), p1(3))
            vtile(1)
            vtile(2)
            hk("mid2")
            lockstep(p2(2), p2(3))
            vtile(3)
            release(2)
            hk("end")
            for h in range(4):
                hc = h * 128

                def mms(E, h=h, hc=hc):
                    last = None
                    for t in range(NT):
                        last = E.matmul(pS_[:, hc:hc + 128], lhsT=kout[:, t, hc:hc + 128], rhs=v[:, t, hc:hc + 128],
                                        start=(first and h == 0 and t == 0), stop=False, skip_group_check=True)
                    return last
                P.op("pe", mms, reads=[b_kout[h]] + b_v, writes=[b_pS_])

        def prefix_finish():
            pS_, b_pS_ = FB[5], b_FB[5]
            dS_t, dS_b = tmp[0], b_tmp[0]
            P.op("dve", lambda E: E.tensor_copy(out=dS_t[:, :], in_=pS_[:, :]), reads=[b_pS_], writes=[dS_b])
            P.op("dve", lambda E: E.tensor_copy(out=S[:, 0, :, :].rearrange("p h d -> p (h d)"), in_=dS_t[:, :]),
                 reads=[dS_b], writes=b_S[0])
            P.op("act", lambda E: E.activation(out=Sb[:, 0, :, :].rearrange("p h d -> p (h d)"), in_=dS_t[:, :], func=AF.Copy),
                 reads=[dS_b], writes=b_Sb[0])

        def core(blk_par, main, last_pre):
            po = FB[0:4]
            b_po = b_FB[0:4]
            if main:
                items = [(h, t) for h in range(4) for t in range(NT)]

                def emit_psc(i):
                    h, t = items[i]
                    tc_ = slice(t * 128, (t + 1) * 128)
                    psc, b_psc = FB[4 + i % 2], b_FB[4 + i % 2]
                    P.op("pe", lambda E: E.matmul(psc[:, 0:128], lhsT=khat[:, h, tc_], rhs=qin[:, h, tc_], start=True, stop=True),
                         reads=[b_khat[h], b_qin[h]], writes=[b_psc])

                def emit_po(i):
                    h, t = items[i]
                    hc = h * 128
                    tc_ = slice(t * 128, (t + 1) * 128)
                    psc, b_psc = FB[4 + i % 2], b_FB[4 + i % 2]
                    ai = i % 4
                    P.op("dve", lambda E: E.tensor_tensor(out=AT[ai][:, :], in0=psc[:, 0:128], in1=maskT[:, :], op=ALU.mult),
                         reads=[b_psc, b_maskT], writes=[b_AT[ai]])
                    P.op("pe", lambda E: E.matmul(po[h][:, tc_], lhsT=v[:, t, hc:hc + 128], rhs=AT[ai][:, :],
                                                  start=(t == 0), stop=False, skip_group_check=True),
                         reads=[b_v[t], b_AT[ai]], writes=[b_po[h]])
                emit_psc(0)
                for i in range(len(items)):
                    if i + 1 < len(items):
                        emit_psc(i + 1)
                    emit_po(i)

            def emit_pS(cc):
                t = cc // 2
                r0 = (cc % 2) * CH
                bank = 4 + cc % 2
                for h in range(4):
                    hc = h * 128
                    P.op("pe", lambda E, h=h, hc=hc: E.matmul(
                        FB[bank][:, hc:hc + 128], lhsT=kout[r0:r0 + CH, t, hc:hc + 128], rhs=v[r0:r0 + CH, t, hc:hc + 128],
                        start=True, stop=True),
                         reads=[b_kout[h], b_v[t]], writes=[b_pS[cc % 2][h], b_FB[bank]])
            emit_pS(0)
            for cc in range(NCH):
                if cc + 1 < NCH:
                    emit_pS(cc + 1)
                bank = 4 + cc % 2
                par = chain["step"] % 2
                npar = 1 - par
                need_bf = main or (last_pre and cc == NCH - 1)
                for h in range(4):
                    hc = h * 128
                    if main:
                        P.op("pe", lambda E, h=h, cc=cc, par=par: E.matmul(
                            po[h][:, cc * CH:(cc + 1) * CH], lhsT=Sb[:, par, h, :], rhs=qin[:, h, cc * CH:(cc + 1) * CH],
                            start=False, stop=True, skip_group_check=True),
                             reads=[b_Sb[par][h], b_qin[h]], writes=[b_po[h]])
                for h in range(4):
                    hc = h * 128
                    P.op("dve", lambda E, h=h, hc=hc, cc=cc, par=par, npar=npar, bank=bank: E.scalar_tensor_tensor(
                        out=S[:, npar, h, :], in0=S[:, par, h, :], scalar=decs[:, blk_par, h, cc:cc + 1],
                        in1=FB[bank][:, hc:hc + 128], op0=ALU.mult, op1=ALU.add),
                         reads=[b_S[par][h], b_decs[blk_par][h], b_pS[cc % 2][h], b_FB[bank]], writes=[b_S[npar][h]])
                    if need_bf:
                        P.op("act", lambda E, h=h, npar=npar: E.activation(out=Sb[:, npar, h, :], in_=S[:, npar, h, :], func=AF.Copy),
                             reads=[b_S[npar][h]], writes=[b_Sb[npar][h]])
                chain["step"] += 1
            if main:
                sq2s = [T_() for _ in range(4)]
                rss = [T_() for _ in range(4)]
                pS_deps = [x for k3 in range(2) for x in b_pS[k3]]
                for h in range(4):
                    sq2, b_sq2 = sq2s[h]
                    P.op("act", lambda E, h=h, sq2=sq2: E.activation(out=sq2[:, :], in_=po[h][:, :], func=AF.Square),
                         reads=[b_po[h]], writes=[b_sq2])
                for hh in range(2):
                    for h in (2 * hh, 2 * hh + 1):
                        sq2, b_sq2 = sq2s[h]
                        rs, b_rs = rss[h]
                        pss, b_pss = (FB[4], b_FB[4]) if (h % 2 == 0) else (FB[5], b_FB[5])
                        P.op("pe", lambda E, pss=pss, sq2=sq2: E.matmul(pss[:, :], lhsT=ones_f[:, :], rhs=sq2[:, :], start=True, stop=True),
                             reads=[b_ones, b_sq2], writes=[b_pss] + pS_deps)
                        P.op("act", lambda E, pss=pss, rs=rs: E.activation(out=rs[:, :], in_=pss[:, :], func=AF.Ln, scale=1.0 / 128,
                                                                           bias=cst[:, EPS:EPS + 1]),
                             reads=[b_pss, b_cst], writes=[b_rs])
                    for h in (2 * hh, 2 * hh + 1):
                        rs, b_rs = rss[h]
                        P.op("act", lambda E, rs=rs: E.activation(out=rs[:, :], in_=rs[:, :], func=AF.Exp, scale=-0.5),
                             reads=[b_rs], writes=[b_rs])
                for h in range(4):
                    rs, b_rs = rss[h]
                    t1, b_t1 = sq2s[h]
                    P.op("dve", lambda E, h=h, t1=t1, rs=rs: E.scalar_tensor_tensor(
                        out=t1[:, :], in0=po[h][:, :], scalar=cst[:, HGN:HGN + 1], in1=rs[:, :], op0=ALU.mult, op1=ALU.mult),
                         reads=[b_po[h], b_cst, b_rs], writes=[b_t1])
                    P.op("dve", lambda E, h=h, t1=t1: E.tensor_tensor(out=og[:, h, :], in0=t1[:, :], in1=gsil[:, h, :], op=ALU.mult),
                         reads=[b_t1, b_gsil[h]], writes=[b_og[h], b_h2T])

        def conv_u(hp):
            hT = hTs[hp]
            b_hT = b_hTs[hp]
            sc_, b_sc = acquire("c")
            sx_, b_sx = acquire("xb")
            for m in range(4):
                pc, b_pc = proj_fm(sc_, b_sc, 8, m * 128, hT, [b_hT])
                cs, b_cs = T_()
                P.op("act", lambda E, pc=pc, cs=cs: E.activation(out=cs[:, :], in_=pc[:, :], func=AF.Copy),
                     reads=[b_pc], writes=[b_cs])
                px, b_px = proj_fm(sx_, b_sx, 8, m * 128, hT, [b_hT])
                P.op("dve", lambda E, m=m, cs=cs, px=px: E.tensor_tensor(out=u[:, m, 2:T + 2], in0=cs[:, :], in1=px[:, :], op=ALU.mult),
                     reads=[b_cs, b_px], writes=[b_u])

        def halo():
            P.op("dve", lambda E: E.tensor_copy(out=u[:, :, 0:2], in_=u[:, :, T:T + 2]), reads=[b_u], writes=[b_u])

        def stage2(hp):
            hT = hTs[hp]
            b_hT = b_hTs[hp]
            conv_u(hp)
            sbg, b_sbg = acquire("bg")
            for m in range(4):
                py, b_py = F_()

                def mmc(E, m=m, py=py):
                    last = None
                    for k in range(3):
                        last = E.matmul(py[:, :], lhsT=diagw[:, k * 4 + m, :], rhs=u[:, m, k:k + T], start=(k == 0), stop=(k == 2))
                    return last
                P.op("pe", mmc, reads=[b_diagw, b_u], writes=[b_py])
                pb, b_pb = proj_fm(sbg, b_sbg, 8, m * 128, hT, [b_hT])
                bs, b_bs = T_()
                P.op("act", lambda E, pb=pb, bs=bs: E.activation(out=bs[:, :], in_=pb[:, :], func=AF.Copy),
                     reads=[b_pb], writes=[b_bs])
                P.op("dve", lambda E, m=m, bs=bs, py=py: E.tensor_tensor(out=yb[:, m, :], in0=bs[:, :], in1=py[:, :], op=ALU.mult),
                     reads=[b_bs, b_py], writes=[b_yb[m], b_h2T])
            halo()
            release(3)

        def stage3(hp):
            hT = hTs[hp]
            b_hT = b_hTs[hp]
            for half in range(2):
                sga, b_sga = acquire(f"ga{half}")
                swa, b_swa = acquire(f"wa{half}")
                sgb, b_sgb = acquire(f"gb{half}")
                swb, b_swb = acquire(f"wb{half}")
                for mm_ in range(4):
                    m = half * 4 + mm_
                    pga, b_pga = proj_fm(sga, b_sga, 8, mm_ * 128, hT, [b_hT])
                    sa, b_sa = T_()
                    P.op("act", lambda E, pga=pga, sa=sa: E.activation(out=sa[:, :], in_=pga[:, :], func=AF.Sigmoid),
                         reads=[b_pga], writes=[b_sa])
                    pya, b_pya = proj_fm(swa, b_swa, 4, mm_ * 128, og, b_og)
                    ma, b_ma = T_()
                    P.op("dve", lambda E, sa=sa, pya=pya, ma=ma: E.tensor_tensor(out=ma[:, :], in0=sa[:, :], in1=pya[:, :], op=ALU.mult),
                         reads=[b_sa, b_pya], writes=[b_ma])
                    pgb, b_pgb = proj_fm(sgb, b_sgb, 8, mm_ * 128, hT, [b_hT])
                    sb_, b_sb = T_()
                    P.op("act", lambda E, pgb=pgb, sb_=sb_: E.activation(out=sb_[:, :], in_=pgb[:, :], func=AF.Sigmoid),
                         reads=[b_pgb], writes=[b_sb])
                    pyb, b_pyb = proj_fm(swb, b_swb, 4, mm_ * 128, yb, b_yb)
                    mb, b_mb = T_()
                    P.op("dve", lambda E, sb_=sb_, pyb=pyb, mb=mb: E.tensor_tensor(out=mb[:, :], in0=sb_[:, :], in1=pyb[:, :], op=ALU.mult),
                         reads=[b_sb, b_pyb], writes=[b_mb])
                    P.op("dve", lambda E, m=m, ma=ma, mb=mb: E.tensor_tensor(out=merged[:, m, :], in0=ma[:, :], in1=mb[:, :], op=ALU.add),
                         reads=[b_ma, b_mb], writes=[b_merged[m]])
                release(4)

        def stage4(row0):
            so = [acquire("wo0"), acquire("wo1")]
            ks = {}

            def wo(t):
                k = load_x(x_main, row0 + t * 128)
                for n in range(2):
                    px1, b_px1 = F_()

                    def mmo(E, n=n, px1=px1):
                        last = None
                        for c in range(8):
                            last = E.matmul(px1[:, :], lhsT=merged[:, c, t * 128:(t + 1) * 128], rhs=so[n][0][:, c, :],
                                            start=(c == 0), stop=(c == 7))
                        return last
                    P.op("pe", mmo, reads=b_merged + [so[n][1]], writes=[b_px1])
                    P.op("dve", lambda E, n=n, px1=px1: E.tensor_tensor(
                        out=x1[:, t, n * 512:(n + 1) * 512], in0=px1[:, :], in1=xt[k][:, n * 512:(n + 1) * 512], op=ALU.add),
                         reads=[b_px1, b_xt[k]], writes=[b_x1[t]])

            def stat_(t):
                ks[t] = norm_stat(x1[:, t, :], [b_x1[t]])

            def xs_(t):
                norm_xs(ks[t], x1[:, t, :], [b_x1[t]])

            def nb(t):
                norm_b(ks[t], G2, h2T, b_h2T, t, extra_w=B_CH)

            wo(0)
            stat_(0)
            wo(1)
            xs_(0)
            stat_(1)
            wo(2)
            xs_(1)
            stat_(2)
            nb(0)
            wo(3)
            release(2)
            xs_(2)
            stat_(3)
            nb(1)
            xs_(3)
            nb(2)
            nb(3)

        def stage5(row0, mid_hook=None):
            if mid_hook is not None and -1 in mid_hook:
                mid_hook[-1]()
            for j in range(6):
                sg_, b_sg = acquire(f"fg{j}")
                su_, b_su = acquire(f"fu{j}")
                for jj in range(4 if j < 5 else 2):
                    jc = j * 4 + jj
                    pg, b_pg = proj_fm(sg_, b_sg, 8, jj * 128, h2T, [b_h2T])
                    sl, b_sl = T_()
                    P.op("act", lambda E, pg=pg, sl=sl: E.activation(out=sl[:, :], in_=pg[:, :], func=AF.Silu),
                         reads=[b_pg], writes=[b_sl])
                    pu, b_pu = proj_fm(su_, b_su, 8, jj * 128, h2T, [b_h2T])
                    P.op("dve", lambda E, jc=jc, sl=sl, pu=pu: E.tensor_tensor(out=act[:, jc, :], in0=sl[:, :], in1=pu[:, :], op=ALU.mult),
                         reads=[b_sl, b_pu], writes=[b_act[jc]] + A_CH)
                release(2)
                if mid_hook is not None and j in mid_hook:
                    mid_hook[j]()
            for n in range(2):
                sd = [acquire(f"fd{n}{k}") for k in range(3)]
                for t in range(NT):
                    pd, b_pd = F_()

                    def mmd(E, t=t, pd=pd, sd=sd):
                        last = None
                        for jc in range(22):
                            last = E.matmul(pd[:, :], lhsT=act[:, jc, t * 128:(t + 1) * 128], rhs=sd[jc // 8][0][:, jc % 8, :],
                                            start=(jc == 0), stop=(jc == 21))
                        return last
                    P.op("pe", mmd, reads=b_act + [s_[1] for s_ in sd], writes=[b_pd])
                    P.op("dve", lambda E, t=t, n=n, pd=pd: E.tensor_tensor(
                        out=x1[:, t, n * 512:(n + 1) * 512], in0=pd[:, :], in1=x1[:, t, n * 512:(n + 1) * 512], op=ALU.add),
                         reads=[b_pd, b_x1[t]], writes=[b_x1[t]])
                    if n == 1:
                        k = cnt["st"] % NXS
                        cnt["st"] += 1
                        P.op("act", lambda E, t=t, k=k: E.activation(out=junk[:, :], in_=x1[:, t, :], func=AF.Square,
                                                                    accum_out=stat[k][:, 2:3]),
                             reads=[b_x1[t]], writes=[b_stat[k]])
                        P.op("act", lambda E, k=k: E.activation(out=stat[k][:, 3:4], in_=stat[k][:, 2:3], func=AF.Ln, scale=1.0 / D,
                                                                bias=cst[:, EPS:EPS + 1]),
                             reads=[b_stat[k], b_cst], writes=[b_stat[k]])
                        P.op("act", lambda E, k=k: E.activation(out=stat[k][:, 3:4], in_=stat[k][:, 3:4], func=AF.Exp, scale=-0.5),
                             reads=[b_stat[k]], writes=[b_stat[k]])
                        P.op("dve", lambda E, t=t, k=k: E.scalar_tensor_tensor(
                            out=x1[:, t, :], in0=x1[:, t, :], scalar=stat[k][:, 3:4], in1=cst[:, GF:GF + D],
                            op0=ALU.mult, op1=ALU.mult),
                             reads=[b_x1[t], b_stat[k], b_cst], writes=[b_x1[t]])
                        P.dma("sp", lambda E, t=t: E.dma_start(out=out_d[row0 + t * 128:row0 + (t + 1) * 128, :], in_=x1[:, t, :]),
                              reads=[b_x1[t]], chan=b_x1[t])
                release(3)

        blocks = [("pre", pb) for pb in reversed(range(NPRE))] + [("main", mb) for mb in range(NMAIN)]

        def src_of(bi):
            kind, idx = blocks[bi]
            return (x_prev if kind == "pre" else x_main), idx * T

        s_, r_ = src_of(0)
        cnt["fbmod"] = 5
        ks0 = stage0_a(s_, r_)
        for _ in range(R):
            issue_next()
        stage0_b(ks0, 0)
        for bi, (kind, idx) in enumerate(blocks):
            hp = bi % 2
            nxt = {"xk": {}, "ks": {}}
            has_next = bi + 1 < len(blocks)

            def pre_load(tiles, bi=bi, nxt=nxt, has_next=has_next):
                if has_next:
                    s2, r2 = src_of(bi + 1)
                    stage0_load(s2, r2, tiles, nxt)

            def pre_norm(tiles, nxt=nxt, has_next=has_next):
                if has_next:
                    stage0_norm(tiles, nxt)

            def pre_b(bi=bi, nxt=nxt, has_next=has_next):
                if has_next:
                    stage0_b([nxt["ks"][t] for t in range(NT)], (bi + 1) % 2)

            def h_start(pre_load=pre_load):
                pre_load([0, 1])

            def h_mid1(pre_load=pre_load, pre_norm=pre_norm):
                pre_norm([0, 1])
                pre_load([2, 3])

            def h_mid2(pre_norm=pre_norm):
                pre_norm([2, 3])

            if kind == "pre":
                stage1_pre(hp, first=(bi == 0), hooks={"start": h_start, "mid1": h_mid1, "mid2": h_mid2, "end": pre_b})
                if bi == 0:
                    conv_u(hp)
                    halo()
                    release(2)
                precache(NPC)
                if bi == NPRE - 1:
                    prefix_finish()
                    cnt["fbmod"] = 6
            else:
                stage1(bi % 2, hp, True)
                core(bi % 2, True, False)
                stage2(hp)
                stage3(hp)
                stage4(idx * T)
                stage5(idx * T, mid_hook={-1: h_start, 0: h_mid1, 2: h_mid2, 4: pre_b})
        P.final_wait("sp", b_x1)
        P.emit(block)
    return nc


DEBUG = False
_NC_CACHE = {}


def _get_nc():
    if "nc" not in _NC_CACHE:
        _NC_CACHE["nc"] = build_nc()
    return _NC_CACHE["nc"]


def kernel(x, norm_mix_g, w_in, lower_bounds, hg_norm_g, conv_w, w_branch_a, w_branch_b,
           w_out, norm_ffn_g, w_ffn_gate, w_ffn_up, w_ffn_down, norm_final_g):
    f32 = np.float32
    x = np.asarray(x, f32)
    cst = np.zeros((128, NCST), f32)
    cst[:, G1:G1 + 8] = np.asarray(norm_mix_g, f32)[0].reshape(8, 128).T
    cst[:, G2:G2 + 8] = np.asarray(norm_ffn_g, f32)[0].reshape(8, 128).T
    lbs = np.asarray(lower_bounds, f32)
    cst[:, L0:L0 + 4] = lbs[0].reshape(4, 128).T
    cst[:, L1:L1 + 4] = lbs[1].reshape(4, 128).T
    cst[:, HGN] = np.asarray(hg_norm_g, f32)[0]
    cw = np.asarray(conv_w, f32)[0]
    for k in range(3):
        cst[:, CW + k * 4:CW + k * 4 + 4] = cw[k].reshape(4, 128).T
    cst[:, EPS] = 1e-6
    cst[:, ONE] = 1.0
    cst[:, GF:GF + D] = np.asarray(norm_final_g, f32)[None, :]
    weights = {
        "w_in": np.ascontiguousarray(np.asarray(w_in, f32)[0]),
        "w_a": np.ascontiguousarray(np.asarray(w_branch_a, f32)[0]),
        "w_b": np.ascontiguousarray(np.asarray(w_branch_b, f32)[0]),
        "w_out": np.ascontiguousarray(np.asarray(w_out, f32)[0]),
        "w_g": np.ascontiguousarray(np.asarray(w_ffn_gate, f32)[0]),
        "w_u": np.ascontiguousarray(np.asarray(w_ffn_up, f32)[0]),
        "w_d": np.ascontiguousarray(np.asarray(w_ffn_down, f32)[0]),
    }
    in_maps = []
    zeros = np.zeros((NPRE * T, D), f32)
    for i in range(8):
        b, half = i // 2, i % 2
        m = {"x_main": np.ascontiguousarray(x[b, half * TOK:(half + 1) * TOK]),
             "x_prev": np.ascontiguousarray(x[b, 0:TOK]) if half == 1 else zeros,
             "cst": cst}
        m.update(weights)
        in_maps.append(m)
    nc = _get_nc()
    res = run_bass_kernel_spmd(nc, in_maps, core_ids=list(range(8)))
    out = np.empty((4, 2 * TOK, D), f32)
    for i in range(8):
        b, half = i // 2, i % 2
        out[b, half * TOK:(half + 1) * TOK] = res.results[i]["out"]
    return out
```

```python
import numpy as np
from contextlib import ExitStack
import concourse.bass as bass
import concourse.mybir as mybir
from concourse.bass_utils import run_bass_kernel_spmd

F32 = mybir.dt.float32
BF16 = mybir.dt.bfloat16
ALU = mybir.AluOpType
AF = mybir.ActivationFunctionType

ENGS = ("pe", "act", "dve", "pool", "sp")

D = 1024
NIN = 5632
DFF = 2816
T = 512
NT = T // 128
CH = 64
NCH = T // CH
NMAIN = 4
NPRE = 4
TOK = NMAIN * T
R = 8
NTMP = 14
NXS = 4

G1, G2, L0, L1, HGN, CW, EPS, ONE, GF = 0, 8, 16, 20, 24, 25, 37, 38, 39
NCST = GF + D


class Buf:
    __slots__ = ("name", "w", "r", "dsem")

    def __init__(self, name):
        self.name = name
        self.w = None
        self.r = {}
        self.dsem = None


class Prog:
    def __init__(self, nc, stack):
        self.nc = nc
        self.stack = stack
        self.streams = {e: [] for e in ENGS}
        self.sems = {}
        self.count = {}
        self.known = {e: {} for e in ENGS}
        for e in ENGS:
            self._newsem("P_" + e)

    def _newsem(self, key):
        h = self.stack.enter_context(self.nc.semaphore(key))
        self.sems[key] = h
        self.count[key] = 0
        return h

    def _deps(self, eng, reads, writes):
        deps = {}
        for b in reads:
            if b.w is not None:
                k, v = b.w
                if deps.get(k, 0) < v:
                    deps[k] = v
        for b in writes:
            if b.w is not None:
                k, v = b.w
                if deps.get(k, 0) < v:
                    deps[k] = v
            for k, v in b.r.items():
                if deps.get(k, 0) < v:
                    deps[k] = v
        own = "P_" + eng
        kn = self.known[eng]
        for k, v in deps.items():
            if k == own and eng == "pe":
                continue
            if kn.get(k, 0) >= v:
                continue
            kn[k] = v
            self.streams[eng].append(("wait", self.sems[k], v))

    def op(self, eng, fn, reads=(), writes=()):
        self._deps(eng, reads, writes)
        key = "P_" + eng
        self.count[key] += 1
        t = self.count[key]
        self.streams[eng].append(("inc", fn, self.sems[key]))
        for b in reads:
            if b.r.get(key, 0) < t:
                b.r[key] = t
        for b in writes:
            b.w = (key, t)
            b.r = {}
        return t

    def dma(self, eng, fn, reads=(), writes=(), chan=None):
        self._deps(eng, reads, writes)
        if chan.dsem is None:
            chan.dsem = "D_" + chan.name
            self._newsem(chan.dsem)
        key = chan.dsem
        self.count[key] += 16
        t = self.count[key]
        self.streams[eng].append(("dma", fn, self.sems[key]))
        for b in reads:
            if b.r.get(key, 0) < t:
                b.r[key] = t
        for b in writes:
            b.w = (key, t)
            b.r = {}
        return t

    def final_wait(self, eng, bufs):
        self._deps(eng, (), bufs)

    def emit(self, block):
        def run(E, stream):
            for item in stream:
                kind = item[0]
                if kind == "wait":
                    E.wait_ge(item[1], item[2])
                elif kind == "inc":
                    item[1](E).then_inc(item[2], 1)
                else:
                    item[1](E).then_inc(item[2], 16)

        @block.tensor
        def _(E):
            run(E, self.streams["pe"])

        @block.scalar
        def _(E):
            run(E, self.streams["act"])

        @block.vector
        def _(E):
            run(E, self.streams["dve"])

        @block.gpsimd
        def _(E):
            run(E, self.streams["pool"])

        @block.sync
        def _(E):
            run(E, self.streams["sp"])


def slab_specs():
    sp = {}
    names = ["q", "f", "i", "g", "c", "bg", "xb", "ga0", "ga1", "gb0", "gb1"]
    for n, nm in enumerate(names):
        sp[nm] = ("w_in", 0, 8, n * 512, 512)
    for n in range(2):
        sp[f"wa{n}"] = ("w_a", 0, 4, n * 512, 512)
        sp[f"wb{n}"] = ("w_b", 0, 4, n * 512, 512)
        sp[f"wo{n}"] = ("w_out", 0, 8, n * 512, 512)
    for j in range(6):
        nc_ = 512 if j < 5 else 256
        sp[f"fg{j}"] = ("w_g", 0, 8, j * 512, nc_)
        sp[f"fu{j}"] = ("w_u", 0, 8, j * 512, nc_)
    for n in range(2):
        for k in range(3):
            kc = 8 if k < 2 else 6
            sp[f"fd{n}{k}"] = ("w_d", k * 1024, kc, n * 512, 512)
    return sp


MAIN_ORDER = (["f", "i", "q", "g", "c", "xb", "bg",
               "ga0", "wa0", "gb0", "wb0", "ga1", "wa1", "gb1", "wb1", "wo0", "wo1"]
              + [x for j in range(6) for x in (f"fg{j}", f"fu{j}")]
              + [f"fd{n}{k}" for n in range(2) for k in range(3)])


def build_nc():
    nc = bass.Bass("TRN2", target_bir_lowering=False)
    x_main = nc.dram_tensor("x_main", [TOK, D], F32, kind="ExternalInput").ap()
    x_prev = nc.dram_tensor("x_prev", [NPRE * T, D], F32, kind="ExternalInput").ap()
    cst_d = nc.dram_tensor("cst", [128, NCST], F32, kind="ExternalInput").ap()
    W = {
        "w_in": nc.dram_tensor("w_in", [D, NIN], F32, kind="ExternalInput").ap(),
        "w_a": nc.dram_tensor("w_a", [512, D], F32, kind="ExternalInput").ap(),
        "w_b": nc.dram_tensor("w_b", [512, D], F32, kind="ExternalInput").ap(),
        "w_out": nc.dram_tensor("w_out", [D, D], F32, kind="ExternalInput").ap(),
        "w_g": nc.dram_tensor("w_g", [D, DFF], F32, kind="ExternalInput").ap(),
        "w_u": nc.dram_tensor("w_u", [D, DFF], F32, kind="ExternalInput").ap(),
        "w_d": nc.dram_tensor("w_d", [DFF, D], F32, kind="ExternalInput").ap(),
    }
    out_d = nc.dram_tensor("out", [TOK, D], F32, kind="ExternalOutput").ap()
    dbg_out = {}
    if DEBUG:
        for nm, shp, dt in [("d_qin", [128, 4 * T], BF16), ("d_khat", [128, 4 * T], BF16), ("d_v", [128, NT * 512], BF16),
                            ("d_og", [128, 4 * T], BF16), ("d_gsil", [128, 4 * T], BF16), ("d_kout", [128, NT * 512], BF16),
                            ("d_hT", [128, 8 * T], BF16), ("d_S", [128, 2 * 4 * 128], F32), ("d_yb", [128, 4 * T], BF16),
                            ("d_merged", [128, 8 * T], BF16)]:
            dbg_out[nm] = nc.dram_tensor(nm, shp, dt, kind="ExternalOutput").ap()
    SPECS = slab_specs()
    cache_names = list(SPECS.keys())
    wcache = nc.dram_tensor("wcache", [len(cache_names), 128, 8 * 512], BF16).ap()
    cache_idx = {n: i for i, n in enumerate(cache_names)}

    with ExitStack() as st:
        def sb(name, shape, dt):
            return st.enter_context(nc.sbuf_tensor(name, shape, dt))

        def ps(name, shape, dt):
            return st.enter_context(nc.psum_tensor(name, shape, dt))

        cst = sb("cst_sb", [128, NCST], F32)
        identf = sb("identf", [128, 128], F32)
        ident = sb("ident", [128, 128], BF16)
        ones_f = sb("ones_f", [128, 128], F32)
        maskT = sb("maskT", [128, 128], F32)
        smask = sb("smask", [128, T], F32)
        diagw = sb("diagw", [128, 12, 128], BF16)
        sv = sb("sv", [128, 4, 4], F32)
        S = sb("S", [128, 2, 4, 128], F32)
        Sb = sb("Sb", [128, 2, 4, 128], BF16)
        decs = sb("decs", [128, 2, 4, NCH], F32)
        xt = [sb(f"xt{i}", [128, D], F32) for i in range(2)]
        xs = [sb(f"xs{i}", [128, D], BF16) for i in range(NXS)]
        stat = [sb(f"stat{i}", [128, 4], F32) for i in range(NXS)]
        junk = sb("junk", [128, 2 * T], BF16)
        omask = sb("omask", [128, T], F32)
        gdec = sb("gdec", [128, 4, 4], F32)
        arenaA = sb("arenaA", [128, 22 * T], BF16)
        act = arenaA[:, :].rearrange("p (j t) -> p j t", t=T)
        qin = arenaA[:, 0:4 * T].rearrange("p (h t) -> p h t", t=T)
        khat = arenaA[:, 4 * T:8 * T].rearrange("p (h t) -> p h t", t=T)
        kout = arenaA[:, 8 * T:12 * T].rearrange("p (t d) -> p t d", d=512)
        v = arenaA[:, 12 * T:16 * T].rearrange("p (t d) -> p t d", d=512)
        gsil = arenaA[:, 16 * T:20 * T].rearrange("p (h t) -> p h t", t=T)
        koutT = [arenaA[:, 20 * T:21 * T], arenaA[:, 21 * T:22 * T]]
        arenaB = sb("arenaB", [128, 8 * T], BF16)
        h2T = arenaB[:, :].rearrange("p (c t) -> p c t", t=T)
        og = arenaB[:, 0:4 * T].rearrange("p (h t) -> p h t", t=T)
        yb = arenaB[:, 4 * T:8 * T].rearrange("p (h t) -> p h t", t=T)
        hTs = [sb(f"hT{i}", [128, 8, T], BF16) for i in range(2)]
        u = sb("u", [128, 4, T + 2], BF16)
        merged = sb("merged", [128, 8, T], BF16)
        x1 = sb("x1", [128, NT, D], F32)
        AT = [sb(f"AT{i}", [128, 128], BF16) for i in range(4)]
        tmp = [sb(f"tmp{i}", [128, T], F32) for i in range(NTMP)]
        ring = [sb(f"ring{i}", [128, 8, 512], BF16) for i in range(R)]
        FB = [ps(f"pf{i}", [128, 512], F32) for i in range(6)]
        TBs = [ps(f"ptb{i}", [128, 1024], BF16) for i in range(2)]

        block = st.enter_context(nc.Block())
        P = Prog(nc, st)

        b_cst, b_ident, b_identf, b_ones, b_maskT, b_smask, b_diagw, b_sv = (Buf(n) for n in
            ["cst", "ident", "identf", "ones", "maskT", "smask", "diagw", "sv"])
        b_S = [[Buf(f"S{p}{h}") for h in range(4)] for p in range(2)]
        b_Sb = [[Buf(f"Sb{p}{h}") for h in range(4)] for p in range(2)]
        b_decs = [[Buf(f"decs{p}{h}") for h in range(4)] for p in range(2)]
        b_xt = [Buf(f"xt{i}") for i in range(2)]
        b_xs = [Buf(f"xs{i}") for i in range(NXS)]
        b_stat = [Buf(f"stat{i}") for i in range(NXS)]
        b_act = [Buf(f"act{j}") for j in range(22)]
        b_hTs = [Buf("hT0"), Buf("hT1")]
        b_qin = [Buf(f"qin{h}") for h in range(4)]
        b_khat = [Buf(f"khat{h}") for h in range(4)]
        b_kout = [Buf(f"kout{h}") for h in range(4)]
        b_koutT = [Buf(f"koutT{i}") for i in range(2)]
        b_v = [Buf(f"v{t}") for t in range(NT)]
        b_gsil = [Buf(f"gsil{h}") for h in range(4)]
        b_og = [Buf(f"og{h}") for h in range(4)]
        b_u = Buf("u")
        b_omask = Buf("omask")
        b_gdecs = [Buf(f"gdec{h}") for h in range(4)]
        b_yb = [Buf(f"yb{m}") for m in range(4)]
        b_merged = [Buf(f"merged{m}") for m in range(8)]
        b_x1 = [Buf(f"x1{t}") for t in range(NT)]
        b_h2T = Buf("h2T")
        b_AT = [Buf(f"AT{i}") for i in range(4)]
        b_tmp = [Buf(f"tmp{i}") for i in range(NTMP)]
        b_ring = [Buf(f"ring{i}") for i in range(R)]
        b_FB = [Buf(f"FB{i}") for i in range(6)]
        b_pS = [[Buf(f"pS{k}{h}") for h in range(4)] for k in range(2)]
        b_TBs = [Buf("TB0"), Buf("TB1")]
        b_cache = {n: Buf("wc_" + n) for n in cache_names}
        A_CH = b_qin + b_khat + b_kout + b_koutT + b_v + b_gsil
        B_CH = b_og + b_yb

        cnt = {"tmp": 0, "fb": 0, "xt": 0, "at": 0, "kt": 0, "st": 0, "os": 0, "tb": 0, "fbmod": 6}

        def TB_():
            i = cnt["tb"] % 2
            cnt["tb"] += 1
            return TBs[i], b_TBs[i]

        def T_():
            i = cnt["tmp"] % NTMP
            cnt["tmp"] += 1
            return tmp[i], b_tmp[i]

        def F_():
            i = cnt["fb"] % cnt["fbmod"]
            cnt["fb"] += 1
            return FB[i], b_FB[i]

        NPC = 4
        pc_list = [n for n in MAIN_ORDER if n not in ("f", "i", "c", "xb")][:NPC * NPRE]
        sched = []
        for pb in range(NPRE):
            sched += ["f", "i"]
            if pb == 0:
                sched += ["c", "xb"]
            sched += ["PC:" + n for n in pc_list[pb * NPC:(pb + 1) * NPC]]
        for mb in range(NMAIN):
            sched += MAIN_ORDER
        ring_state = {"issued": 0, "acq": 0}
        cached = set()
        ch_sw = [Buf(f"rsw{i}") for i in range(R)]
        ch_hw = [Buf(f"rhw{i}") for i in range(R)]

        pending = {"store": None}

        def flush_store():
            if pending["store"] is not None:
                fn_, reads_, writes_, chan_ = pending["store"]
                P.dma("pool", fn_, reads=reads_, writes=writes_, chan=chan_)
                pending["store"] = None

        def issue_next():
            g = ring_state["issued"]
            if g >= len(sched):
                flush_store()
                return
            ring_state["issued"] += 1
            name = sched[g]
            if name.startswith("PC:"):
                name = name[3:]
            s = g % R
            wkey, r0, kc, c0, ncols = SPECS[name]
            dst = ring[s][:, 0:kc, 0:ncols]
            ci = cache_idx[name]
            cview = wcache[ci].rearrange("p (c n) -> p c n", n=512)[:, 0:kc, 0:ncols]
            if name in cached:
                flush_store()
                P.dma("sp", lambda E: E.dma_start(out=dst, in_=cview), reads=[b_cache[name]],
                      writes=[b_ring[s]], chan=ch_hw[s])
            else:
                src = W[wkey][r0:r0 + kc * 128, :].rearrange("(c p) n -> p c n", p=128)[:, :, c0:c0 + ncols]
                P.dma("pool", lambda E: E.dma_start(out=dst, in_=src), writes=[b_ring[s]], chan=ch_sw[s])
                cached.add(name)
                flush_store()
                pending["store"] = (lambda E: E.dma_start(out=cview, in_=dst), [b_ring[s]], [b_cache[name]], ch_sw[s])

        def acquire(name):
            g = ring_state["acq"]
            assert sched[g] == name, (g, sched[g], name)
            ring_state["acq"] += 1
            assert g < ring_state["issued"], "ring underflow"
            return ring[g % R], b_ring[g % R]

        def precache(n):
            for _ in range(n):
                g = ring_state["acq"]
                assert sched[g].startswith("PC:"), (g, sched[g])
                ring_state["acq"] += 1
                issue_next()

        def release(n=1):
            for _ in range(n):
                issue_next()

        P.dma("sp", lambda E: E.dma_start(out=cst[:], in_=cst_d), writes=[b_cst], chan=b_cst)
        P.op("dve", lambda E: E.memset(identf[:], 0.0), writes=[b_identf])
        P.op("pool", lambda E: E.affine_select(out=identf[:], in_=identf[:], pattern=[[-1, 128]],
                                               compare_op=ALU.not_equal, fill=1.0, base=0, channel_multiplier=1),
             reads=[b_identf], writes=[b_identf])
        P.op("dve", lambda E: E.tensor_copy(out=ident[:], in_=identf[:]), reads=[b_identf], writes=[b_ident])
        P.op("dve", lambda E: E.memset(ones_f[:], 1.0), writes=[b_ones])
        P.op("dve", lambda E: E.memset(maskT[:], 1.0), writes=[b_maskT])
        P.op("pool", lambda E: E.affine_select(out=maskT[:], in_=maskT[:], pattern=[[1, 128]],
                                               compare_op=ALU.is_ge, fill=0.0, base=0, channel_multiplier=-1),
             reads=[b_maskT], writes=[b_maskT])
        P.op("dve", lambda E: E.memset(maskT[0:64, 64:128], 0.0), reads=[b_maskT], writes=[b_maskT])
        P.op("dve", lambda E: E.memset(smask[:], 1.0), writes=[b_smask])
        P.op("dve", lambda E: E.memset(smask[:].rearrange("p (c j) -> p c j", j=CH)[:, :, 0:1], 0.0),
             reads=[b_smask], writes=[b_smask])
        for k in range(12):
            P.op("dve", lambda E, k=k: E.tensor_scalar(out=diagw[:, k, :], in0=identf[:], scalar1=cst[:, CW + k:CW + k + 1],
                                                       scalar2=0.0, op0=ALU.mult, op1=ALU.add),
                 reads=[b_identf, b_cst], writes=[b_diagw])
        P.op("dve", lambda E: E.tensor_tensor(out=sv[:, :, 3], in0=cst[:, L0:L0 + 4], in1=cst[:, L1:L1 + 4], op=ALU.subtract),
             reads=[b_cst], writes=[b_sv])
        P.op("act", lambda E: E.activation(out=sv[:, :, 1], in_=sv[:, :, 3], func=AF.Sigmoid), reads=[b_sv], writes=[b_sv])
        P.op("act", lambda E: E.activation(out=sv[:, :, 0], in_=sv[:, :, 3], func=AF.Sigmoid, scale=-1.0),
             reads=[b_sv], writes=[b_sv])
        P.op("dve", lambda E: E.tensor_scalar(out=sv[:, :, 2], in0=sv[:, :, 0], scalar1=-1.0, scalar2=0.0, op0=ALU.mult, op1=ALU.add),
             reads=[b_sv], writes=[b_sv])
        P.op("dve", lambda E: E.memset(omask[:], 1.0), writes=[b_omask])
        P.op("dve", lambda E: E.memset(gdec[:], 1.0), writes=b_gdecs)
        P.op("dve", lambda E: E.memset(S[:, 0, :, :], 0.0), writes=b_S[0])
        P.op("dve", lambda E: E.memset(Sb[:, 0, :, :], 0.0), writes=b_Sb[0])
        P.op("dve", lambda E: E.memset(u[:], 0.0), writes=[b_u])
        chain = {"step": 0}

        def norm_stat(src_tile_ap, src_reads):
            k = cnt["st"] % NXS
            cnt["st"] += 1
            P.op("act", lambda E: E.activation(out=junk[:, :], in_=src_tile_ap, func=AF.Square, accum_out=stat[k][:, 2:3]),
                 reads=src_reads, writes=[b_stat[k]])
            P.op("act", lambda E: E.activation(out=stat[k][:, 3:4], in_=stat[k][:, 2:3], func=AF.Ln, scale=1.0 / D,
                                               bias=cst[:, EPS:EPS + 1]),
                 reads=[b_stat[k], b_cst], writes=[b_stat[k]])
            P.op("act", lambda E: E.activation(out=stat[k][:, 3:4], in_=stat[k][:, 3:4], func=AF.Exp, scale=-0.5),
                 reads=[b_stat[k]], writes=[b_stat[k]])
            return k

        def norm_xs(k, src_tile_ap, src_reads):
            P.op("dve", lambda E: E.tensor_scalar(out=xs[k][:], in0=src_tile_ap, scalar1=stat[k][:, 3:4], scalar2=0.0,
                                                  op0=ALU.mult, op1=ALU.add),
                 reads=list(src_reads) + [b_stat[k]], writes=[b_xs[k]])

        def norm_a(src_tile_ap, src_reads):
            k = norm_stat(src_tile_ap, src_reads)
            norm_xs(k, src_tile_ap, src_reads)
            return k

        def norm_b(k, gcol, dstT, b_dst, t, extra_w=()):
            TB, b_TB = TB_()

            def tr(E):
                last = None
                for c in range(8):
                    last = E.transpose(out=TB[:, c * 128:(c + 1) * 128], in_=xs[k][:, c * 128:(c + 1) * 128], identity=ident[:])
                return last
            P.op("pe", tr, reads=[b_xs[k], b_ident], writes=[b_TB])
            P.op("dve", lambda E: E.tensor_tensor(
                out=dstT[:, :, t * 128:(t + 1) * 128],
                in0=TB[:].rearrange("p (c t) -> p c t", t=128),
                in1=cst[:, gcol:gcol + 8].rearrange("p (c o) -> p c o", o=1).to_broadcast([128, 8, 128]),
                op=ALU.mult), reads=[b_TB, b_cst], writes=[b_dst] + list(extra_w))

        def load_x(src, r0):
            k = cnt["xt"] % 2
            cnt["xt"] += 1
            P.dma("sp", lambda E: E.dma_start(out=xt[k][:], in_=src[r0:r0 + 128, :]), writes=[b_xt[k]], chan=b_xt[k])
            return k

        def proj_fm(slab, b_slab, kc, col0, rhsT, b_rhs):
            pf, b_pf = F_()

            def mm(E):
                last = None
                for c in range(kc):
                    last = E.matmul(pf[:, :], lhsT=slab[:, c, col0:col0 + 128], rhs=rhsT[:, c, :],
                                    start=(c == 0), stop=(c == kc - 1))
                return last
            P.op("pe", mm, reads=[b_slab] + list(b_rhs), writes=[b_pf])
            return pf, b_pf

        def stage0_load(src, row0, tiles, st):
            for t in tiles:
                st["xk"][t] = load_x(src, row0 + t * 128)

        def stage0_norm(tiles, st):
            for t in tiles:
                k = st["xk"][t]
                st["ks"][t] = norm_a(xt[k][:, :], [b_xt[k]])

        def stage0_a(src, row0):
            st = {"xk": {}, "ks": {}}
            stage0_load(src, row0, [0, 1], st)
            stage0_norm([0, 1], st)
            stage0_load(src, row0, [2, 3], st)
            stage0_norm([2, 3], st)
            return [st["ks"][t] for t in range(NT)]

        def stage0_b(ks, hp):
            for t in range(NT):
                norm_b(ks[t], G1, hTs[hp], b_hTs[hp], t)

        def lockstep(*lists):
            n = max(len(l) for l in lists)
            for i in range(n):
                for l in lists:
                    if i < len(l):
                        l[i]()

        def make_vtile(hT, b_hT, si, b_si, on_dve=False):
            def vtile(t):
                pv, b_pv = F_()

                def mmv(E):
                    last = None
                    for c in range(8):
                        last = E.matmul(pv[:, :], lhsT=hT[:, c, t * 128:(t + 1) * 128], rhs=si[:, c, :],
                                        start=(c == 0), stop=(c == 7))
                    return last
                P.op("pe", mmv, reads=[b_si, b_hT], writes=[b_pv])
                if on_dve:
                    P.op("dve", lambda E: E.tensor_copy(out=v[:, t, :], in_=pv[:, :]), reads=[b_pv], writes=[b_v[t]] + b_act)
                else:
                    P.op("act", lambda E: E.activation(out=v[:, t, :], in_=pv[:, :], func=AF.Copy),
                         reads=[b_pv], writes=[b_v[t]] + b_act)
            return vtile

        def gate_p1(h, pf, b_pf, r_e, rb_e, r_sg, rb_sg, r_km, rb_km, r_lf, rb_lf):
            return [
                lambda: P.op("act", lambda E: E.activation(out=r_e[:, :], in_=pf[:, :], func=AF.Exp, scale=-1.0),
                             reads=[b_pf], writes=[rb_e]),
                lambda: P.op("act", lambda E: E.activation(out=r_e[:, :], in_=r_e[:, :], func=AF.Ln, bias=cst[:, ONE:ONE + 1]),
                             reads=[rb_e, b_cst], writes=[rb_e]),
                lambda: P.op("act", lambda E: E.activation(out=r_sg[:, :], in_=r_e[:, :], func=AF.Exp, scale=-1.0),
                             reads=[rb_e], writes=[rb_sg]),
                lambda: P.op("act", lambda E: E.activation(out=r_km[:, :], in_=r_sg[:, :], func=AF.Identity, scale=sv[:, h, 2:3],
                                                           bias=sv[:, h, 0:1]),
                             reads=[rb_sg, b_sv], writes=[rb_km]),
                lambda: P.op("act", lambda E: E.activation(out=r_lf[:, :], in_=r_sg[:, :], func=AF.Ln, scale=sv[:, h, 0:1],
                                                           bias=sv[:, h, 1:2]),
                             reads=[rb_sg, b_sv], writes=[rb_lf]),
            ]

        def kout_tail(h, kk):
            hc = h * 128
            st_ = {}

            def t1():
                TB, b_TB = TB_()
                st_["tb"] = (TB, b_TB)

                def trk(E):
                    last = None
                    for t in range(NT):
                        last = E.transpose(out=TB[:, t * 128:(t + 1) * 128], in_=koutT[kk][:, t * 128:(t + 1) * 128],
                                           identity=ident[:])
                    return last
                P.op("pe", trk, reads=[b_koutT[kk], b_ident], writes=[b_TB])

            def t2():
                TB, b_TB = st_["tb"]
                P.op("act", lambda E: E.activation(out=kout[:, :, hc:hc + 128],
                                                   in_=TB[:, 0:512].rearrange("p (t d) -> p t d", d=128), func=AF.Copy),
                     reads=[b_TB], writes=[b_kout[h]] + b_act)
            return [t1, t2]

        def stage1(blk_par, hp, main, hooks=None):
            hT = hTs[hp]
            b_hT = b_hTs[hp]
            sf, b_sf = acquire("f")
            si, b_si = acquire("i")
            vtile = make_vtile(hT, b_hT, si, b_si)
            r_e, rb_e = [tmp[0], tmp[1]], [b_tmp[0], b_tmp[1]]
            r_sg, rb_sg = [tmp[2], tmp[3]], [b_tmp[2], b_tmp[3]]
            r_km, rb_km = [tmp[4], tmp[5]], [b_tmp[4], b_tmp[5]]
            r_lf, rb_lf = [tmp[6], tmp[7]], [b_tmp[6], b_tmp[7]]
            r_ee, rb_ee = tmp[8:12], b_tmp[8:12]
            pfs = [proj_fm(sf, b_sf, 8, h * 128, hT, [b_hT]) for h in range(4)]

            def p1(h):
                p = h % 2
                return gate_p1(h, pfs[h][0], pfs[h][1], r_e[p], rb_e[p], r_sg[p], rb_sg[p], r_km[p], rb_km[p], r_lf[p], rb_lf[p])

            def p2(h):
                p = h % 2
                bb, b_bb = r_e[p], rb_e[p]
                ei, b_ei = r_sg[p], rb_sg[p]
                ee, b_ee = r_ee[h], rb_ee[h]
                kk = p
                ops = [
                    lambda: P.op("dve", lambda E: E.tensor_tensor_scan(out=bb[:, :], data0=smask[:, :], data1=r_lf[p][:, :], initial=0.0,
                                                                       op0=ALU.mult, op1=ALU.add),
                                 reads=[b_smask, rb_lf[p]], writes=[b_bb]),
                    lambda: P.op("act", lambda E: E.activation(out=ee[:, :], in_=bb[:, :], func=AF.Exp), reads=[b_bb], writes=[b_ee]),
                    lambda: P.op("act", lambda E: E.activation(out=ei[:, :], in_=bb[:, :], func=AF.Exp, scale=-1.0),
                                 reads=[b_bb], writes=[b_ei]),
                    lambda: P.op("dve", lambda E: E.tensor_tensor(out=khat[:, h, :], in0=r_km[p][:, :], in1=ei[:, :], op=ALU.mult),
                                 reads=[rb_km[p], b_ei], writes=[b_khat[h]] + b_act),
                    lambda: P.op("dve", lambda E: E.tensor_copy(
                        out=decs[:, blk_par, h, :],
                        in_=ee[:, :].rearrange("p (c j) -> p c j", j=CH)[:, :, CH - 1]),
                        reads=[b_ee], writes=[b_decs[blk_par][h]]),
                    lambda: P.op("dve", lambda E: E.tensor_tensor(
                        out=koutT[kk].rearrange("p (c j) -> p c j", j=CH),
                        in0=khat[:, h, :].rearrange("p (c j) -> p c j", j=CH),
                        in1=decs[:, blk_par, h, :].rearrange("p (c o) -> p c o", o=1).to_broadcast([128, NCH, CH]),
                        op=ALU.mult), reads=[b_khat[h], b_decs[blk_par][h]], writes=[b_koutT[kk]] + b_act),
                ]
                return ops + kout_tail(h, kk)

            hooks = hooks or {}
            hk = lambda name: hooks[name]() if name in hooks else None
            hk("start")
            lockstep(p1(0), p1(1))
            vtile(0)
            lockstep(p2(0), p2(1))
            hk("mid1")
            lockstep(p1(2), p1(3))
            vtile(1)
            vtile(2)
            hk("mid2")
            lockstep(p2(2), p2(3))
            vtile(3)
            release(2)
            hk("end")
            if main:
                sq_, b_sq = acquire("q")
                sg_, b_sg = acquire("g")

                def QG(h):
                    hc = h * 128
                    qs, b_qs = (tmp[0], b_tmp[0]) if h % 2 == 0 else (tmp[1], b_tmp[1])
                    pq, b_pq = proj_fm(sq_, b_sq, 8, hc, hT, [b_hT])
                    P.op("act", lambda E: E.activation(out=qs[:, :], in_=pq[:, :], func=AF.Silu), reads=[b_pq], writes=[b_qs])
                    pg, b_pg = proj_fm(sg_, b_sg, 8, hc, hT, [b_hT])
                    P.op("act", lambda E: E.activation(out=gsil[:, h, :], in_=pg[:, :], func=AF.Silu),
                         reads=[b_pg], writes=[b_gsil[h]] + b_act)
                    P.op("dve", lambda E: E.scalar_tensor_tensor(out=qin[:, h, :], in0=qs[:, :], scalar=float(128 ** -0.5),
                                                                 in1=r_ee[h][:, :], op0=ALU.mult, op1=ALU.mult),
                         reads=[b_qs, rb_ee[h]], writes=[b_qin[h]] + b_act)
                for h in range(4):
                    QG(h)
                release(2)

        def stage1_pre(hp, first, hooks=None):
            hT = hTs[hp]
            b_hT = b_hTs[hp]
            sf, b_sf = acquire("f")
            si, b_si = acquire("i")
            vtile = make_vtile(hT, b_hT, si, b_si, on_dve=True)
            ie, ikm, ilf = [0, 1, 8, 9], [4, 5, 10, 11], [6, 7, 12, 13]
            r_e, rb_e = [tmp[i] for i in ie], [b_tmp[i] for i in ie]
            r_sg, rb_sg = [tmp[2], tmp[3]], [b_tmp[2], b_tmp[3]]
            r_km, rb_km = [tmp[i] for i in ikm], [b_tmp[i] for i in ikm]
            r_lf, rb_lf = [tmp[i] for i in ilf], [b_tmp[i] for i in ilf]
            pS_, b_pS_ = FB[5], b_FB[5]
            pfs = [proj_fm(sf, b_sf, 8, h * 128, hT, [b_hT]) for h in range(4)]

            def p1(h):
                p = h % 2
                return gate_p1(h, pfs[h][0], pfs[h][1], r_e[h], rb_e[h], r_sg[p], rb_sg[p], r_km[h], rb_km[h], r_lf[h], rb_lf[h])

            def p2(h):
                p = h % 2
                bb, b_bb = r_e[h], rb_e[h]
                lf, b_lf = r_lf[h], rb_lf[h]
                bg = b_gdecs[h]
                kk = p
                ops = [
                    lambda: P.op("dve", lambda E: E.tensor_tensor_scan(out=bb[:, :], data0=omask[:, :], data1=lf[:, :], initial=0.0,
                                                                       op0=ALU.mult, op1=ALU.add),
                                 reads=[b_omask, b_lf], writes=[b_bb]),
                    lambda: P.op("dve", lambda E: E.tensor_scalar(out=gdec[:, h, 2:3], in0=bb[:, T - 1:T], scalar1=-40.0, scalar2=0.0,
                                                                  op0=ALU.max, op1=ALU.add), reads=[b_bb, bg], writes=[bg]),
                    lambda: P.op("dve", lambda E: E.tensor_scalar(out=lf[:, :], in0=bb[:, :], scalar1=bb[:, T - 1:T], scalar2=40.0,
                                                                  op0=ALU.subtract, op1=ALU.min), reads=[b_bb], writes=[b_lf]),
                    lambda: P.op("act", lambda E: E.activation(out=lf[:, :], in_=lf[:, :], func=AF.Exp, scale=-1.0),
                                 reads=[b_lf], writes=[b_lf]),
                    lambda: P.op("dve", lambda E: E.tensor_scalar(out=lf[:, :], in0=lf[:, :], scalar1=gdec[:, h, 0:1], scalar2=1e-30,
                                                                  op0=ALU.mult, op1=ALU.max),
                                 reads=[b_lf, bg], writes=[b_lf]),
                    lambda: P.op("dve", lambda E: E.tensor_tensor(out=koutT[kk], in0=r_km[h][:, :], in1=lf[:, :], op=ALU.mult),
                                 reads=[rb_km[h], b_lf], writes=[b_koutT[kk]] + b_act),
                    lambda: P.op("act", lambda E: E.activation(out=gdec[:, h, 1:2], in_=gdec[:, h, 2:3], func=AF.Exp),
                                 reads=[bg], writes=[bg]),
                    lambda: P.op("dve", lambda E: E.tensor_scalar(out=gdec[:, h, 0:1], in0=gdec[:, h, 0:1], scalar1=gdec[:, h, 1:2],
                                                                  scalar2=1e-30, op0=ALU.mult, op1=ALU.max),
                                 reads=[bg], writes=[bg]),
                ]
                return ops + kout_tail(h, kk)

            hooks = hooks or {}
            hk = lambda name: hooks[name]() if name in hooks else None
            hk("start")
            lockstep(p1(0), p1(1))
            vtile(0)
            hk("mid1")
            lockstep(p2(0), p2(1), p1(2), p1(3))
            vtile(1)
            vtile(2)
            hk("mid2")
            lockstep(p2(2), p2(3))
            vtile(3)
            release(2)
            hk("end")
            for h in range(4):
                hc = h * 128

                def mms(E, h=h, hc=hc):
                    last = None
                    for t in range(NT):
                        last = E.matmul(pS_[:, hc:hc + 128], lhsT=kout[:, t, hc:hc + 128], rhs=v[:, t, hc:hc + 128],
                                        start=(first and h == 0 and t == 0), stop=False, skip_group_check=True)
                    return last
                P.op("pe", mms, reads=[b_kout[h]] + b_v, writes=[b_pS_])

        def prefix_finish():
            pS_, b_pS_ = FB[5], b_FB[5]
            dS_t, dS_b = tmp[0], b_tmp[0]
            P.op("dve", lambda E: E.tensor_copy(out=dS_t[:, :], in_=pS_[:, :]), reads=[b_pS_], writes=[dS_b])
            P.op("dve", lambda E: E.tensor_copy(out=S[:, 0, :, :].rearrange("p h d -> p (h d)"), in_=dS_t[:, :]),
                 reads=[dS_b], writes=b_S[0])
            P.op("act", lambda E: E.activation(out=Sb[:, 0, :, :].rearrange("p h d -> p (h d)"), in_=dS_t[:, :], func=AF.Copy),
                 reads=[dS_b], writes=b_Sb[0])

        def core(blk_par, main, last_pre):
            po = FB[0:4]
            b_po = b_FB[0:4]
            if main:
                items = [(h, t) for h in range(4) for t in range(NT)]

                def emit_psc(i):
                    h, t = items[i]
                    tc_ = slice(t * 128, (t + 1) * 128)
                    psc, b_psc = FB[4 + i % 2], b_FB[4 + i % 2]
                    P.op("pe", lambda E: E.matmul(psc[:, 0:128], lhsT=khat[:, h, tc_], rhs=qin[:, h, tc_], start=True, stop=True),
                         reads=[b_khat[h], b_qin[h]], writes=[b_psc])

                def emit_po(i):
                    h, t = items[i]
                    hc = h * 128
                    tc_ = slice(t * 128, (t + 1) * 128)
                    psc, b_psc = FB[4 + i % 2], b_FB[4 + i % 2]
                    ai = i % 4
                    P.op("dve", lambda E: E.tensor_tensor(out=AT[ai][:, :], in0=psc[:, 0:128], in1=maskT[:, :], op=ALU.mult),
                         reads=[b_psc, b_maskT], writes=[b_AT[ai]])
                    P.op("pe", lambda E: E.matmul(po[h][:, tc_], lhsT=v[:, t, hc:hc + 128], rhs=AT[ai][:, :],
                                                  start=(t == 0), stop=False, skip_group_check=True),
                         reads=[b_v[t], b_AT[ai]], writes=[b_po[h]])
                emit_psc(0)
                for i in range(len(items)):
                    if i + 1 < len(items):
                        emit_psc(i + 1)
                    emit_po(i)

            def emit_pS(cc):
                t = cc // 2
                r0 = (cc % 2) * CH
                bank = 4 + cc % 2
                for h in range(4):
                    hc = h * 128
                    P.op("pe", lambda E, h=h, hc=hc: E.matmul(
                        FB[bank][:, hc:hc + 128], lhsT=kout[r0:r0 + CH, t, hc:hc + 128], rhs=v[r0:r0 + CH, t, hc:hc + 128],
                        start=True, stop=True),
                         reads=[b_kout[h], b_v[t]], writes=[b_pS[cc % 2][h], b_FB[bank]])
            emit_pS(0)
            for cc in range(NCH):
                if cc + 1 < NCH:
                    emit_pS(cc + 1)
                bank = 4 + cc % 2
                par = chain["step"] % 2
                npar = 1 - par
                need_bf = main or (last_pre and cc == NCH - 1)
                for h in range(4):
                    hc = h * 128
                    if main:
                        P.op("pe", lambda E, h=h, cc=cc, par=par: E.matmul(
                            po[h][:, cc * CH:(cc + 1) * CH], lhsT=Sb[:, par, h, :], rhs=qin[:, h, cc * CH:(cc + 1) * CH],
                            start=False, stop=True, skip_group_check=True),
                             reads=[b_Sb[par][h], b_qin[h]], writes=[b_po[h]])
                for h in range(4):
                    hc = h * 128
                    P.op("dve", lambda E, h=h, hc=hc, cc=cc, par=par, npar=npar, bank=bank: E.scalar_tensor_tensor(
                        out=S[:, npar, h, :], in0=S[:, par, h, :], scalar=decs[:, blk_par, h, cc:cc + 1],
                        in1=FB[bank][:, hc:hc + 128], op0=ALU.mult, op1=ALU.add),
                         reads=[b_S[par][h], b_decs[blk_par][h], b_pS[cc % 2][h], b_FB[bank]], writes=[b_S[npar][h]])
                    if need_bf:
                        P.op("act", lambda E, h=h, npar=npar: E.activation(out=Sb[:, npar, h, :], in_=S[:, npar, h, :], func=AF.Copy),
                             reads=[b_S[npar][h]], writes=[b_Sb[npar][h]])
                chain["step"] += 1
            if main:
                sq2s = [T_() for _ in range(4)]
                rss = [T_() for _ in range(4)]
                pS_deps = [x for k3 in range(2) for x in b_pS[k3]]
                for h in range(4):
                    sq2, b_sq2 = sq2s[h]
                    P.op("act", lambda E, h=h, sq2=sq2: E.activation(out=sq2[:, :], in_=po[h][:, :], func=AF.Square),
                         reads=[b_po[h]], writes=[b_sq2])
                for hh in range(2):
                    for h in (2 * hh, 2 * hh + 1):
                        sq2, b_sq2 = sq2s[h]
                        rs, b_rs = rss[h]
                        pss, b_pss = (FB[4], b_FB[4]) if (h % 2 == 0) else (FB[5], b_FB[5])
                        P.op("pe", lambda E, pss=pss, sq2=sq2: E.matmul(pss[:, :], lhsT=ones_f[:, :], rhs=sq2[:, :], start=True, stop=True),
                             reads=[b_ones, b_sq2], writes=[b_pss] + pS_deps)
                        P.op("act", lambda E, pss=pss, rs=rs: E.activation(out=rs[:, :], in_=pss[:, :], func=AF.Ln, scale=1.0 / 128,
                                                                           bias=cst[:, EPS:EPS + 1]),
                             reads=[b_pss, b_cst], writes=[b_rs])
                    for h in (2 * hh, 2 * hh + 1):
                        rs, b_rs = rss[h]
                        P.op("act", lambda E, rs=rs: E.activation(out=rs[:, :], in_=rs[:, :], func=AF.Exp, scale=-0.5),
                             reads=[b_rs], writes=[b_rs])
                for h in range(4):
                    rs, b_rs = rss[h]
                    t1, b_t1 = sq2s[h]
                    P.op("dve", lambda E, h=h, t1=t1, rs=rs: E.scalar_tensor_tensor(
                        out=t1[:, :], in0=po[h][:, :], scalar=cst[:, HGN:HGN + 1], in1=rs[:, :], op0=ALU.mult, op1=ALU.mult),
                         reads=[b_po[h], b_cst, b_rs], writes=[b_t1])
                    P.op("dve", lambda E, h=h, t1=t1: E.tensor_tensor(out=og[:, h, :], in0=t1[:, :], in1=gsil[:, h, :], op=ALU.mult),
                         reads=[b_t1, b_gsil[h]], writes=[b_og[h], b_h2T])

        def conv_u(hp):
            hT = hTs[hp]
            b_hT = b_hTs[hp]
            sc_, b_sc = acquire("c")
            sx_, b_sx = acquire("xb")
            for m in range(4):
                pc, b_pc = proj_fm(sc_, b_sc, 8, m * 128, hT, [b_hT])
                cs, b_cs = T_()
                P.op("act", lambda E, pc=pc, cs=cs: E.activation(out=cs[:, :], in_=pc[:, :], func=AF.Copy),
                     reads=[b_pc], writes=[b_cs])
                px, b_px = proj_fm(sx_, b_sx, 8, m * 128, hT, [b_hT])
                P.op("dve", lambda E, m=m, cs=cs, px=px: E.tensor_tensor(out=u[:, m, 2:T + 2], in0=cs[:, :], in1=px[:, :], op=ALU.mult),
                     reads=[b_cs, b_px], writes=[b_u])

        def halo():
            P.op("dve", lambda E: E.tensor_copy(out=u[:, :, 0:2], in_=u[:, :, T:T + 2]), reads=[b_u], writes=[b_u])

        def stage2(hp):
            hT = hTs[hp]
            b_hT = b_hTs[hp]
            conv_u(hp)
            sbg, b_sbg = acquire("bg")
            for m in range(4):
                py, b_py = F_()

                def mmc(E, m=m, py=py):
                    last = None
                    for k in range(3):
                        last = E.matmul(py[:, :], lhsT=diagw[:, k * 4 + m, :], rhs=u[:, m, k:k + T], start=(k == 0), stop=(k == 2))
                    return last
                P.op("pe", mmc, reads=[b_diagw, b_u], writes=[b_py])
                pb, b_pb = proj_fm(sbg, b_sbg, 8, m * 128, hT, [b_hT])
                bs, b_bs = T_()
                P.op("act", lambda E, pb=pb, bs=bs: E.activation(out=bs[:, :], in_=pb[:, :], func=AF.Copy),
                     reads=[b_pb], writes=[b_bs])
                P.op("dve", lambda E, m=m, bs=bs, py=py: E.tensor_tensor(out=yb[:, m, :], in0=bs[:, :], in1=py[:, :], op=ALU.mult),
                     reads=[b_bs, b_py], writes=[b_yb[m], b_h2T])
            halo()
            release(3)

        def stage3(hp):
            hT = hTs[hp]
            b_hT = b_hTs[hp]
            for half in range(2):
                sga, b_sga = acquire(f"ga{half}")
                swa, b_swa = acquire(f"wa{half}")
                sgb, b_sgb = acquire(f"gb{half}")
                swb, b_swb = acquire(f"wb{half}")
                for mm_ in range(4):
                    m = half * 4 + mm_
                    pga, b_pga = proj_fm(sga, b_sga, 8, mm_ * 128, hT, [b_hT])
                    sa, b_sa = T_()
                    P.op("act", lambda E, pga=pga, sa=sa: E.activation(out=sa[:, :], in_=pga[:, :], func=AF.Sigmoid),
                         reads=[b_pga], writes=[b_sa])
                    pya, b_pya = proj_fm(swa, b_swa, 4, mm_ * 128, og, b_og)
                    ma, b_ma = T_()
                    P.op("dve", lambda E, sa=sa, pya=pya, ma=ma: E.tensor_tensor(out=ma[:, :], in0=sa[:, :], in1=pya[:, :], op=ALU.mult),
                         reads=[b_sa, b_pya], writes=[b_ma])
                    pgb, b_pgb = proj_fm(sgb, b_sgb, 8, mm_ * 128, hT, [b_hT])
                    sb_, b_sb = T_()
                    P.op("act", lambda E, pgb=pgb, sb_=sb_: E.activation(out=sb_[:, :], in_=pgb[:, :], func=AF.Sigmoid),
                         reads=[b_pgb], writes=[b_sb])
                    pyb, b_pyb = proj_fm(swb, b_swb, 4, mm_ * 128, yb, b_yb)
                    mb, b_mb = T_()
                    P.op("dve", lambda E, sb_=sb_, pyb=pyb, mb=mb: E.tensor_tensor(out=mb[:, :], in0=sb_[:, :], in1=pyb[:, :], op=ALU.mult),
                         reads=[b_sb, b_pyb], writes=[b_mb])
                    P.op("dve", lambda E, m=m, ma=ma, mb=mb: E.tensor_tensor(out=merged[:, m, :], in0=ma[:, :], in1=mb[:, :], op=ALU.add),
                         reads=[b_ma, b_mb], writes=[b_merged[m]])
                release(4)

        def stage4_preload(row0):
            return {t: load_x(x_main, row0 + t * 128) for t in (0, 1)}

        def stage4(row0, xk):
            so = [acquire("wo0"), acquire("wo1")]
            ks = {}

            def wo(t):
                k = xk[t] if t in xk else load_x(x_main, row0 + t * 128)
                for n in range(2):
                    px1, b_px1 = F_()

                    def mmo(E, n=n, px1=px1):
                        last = None
                        for c in range(8):
                            last = E.matmul(px1[:, :], lhsT=merged[:, c, t * 128:(t + 1) * 128], rhs=so[n][0][:, c, :],
                                            start=(c == 0), stop=(c == 7))
                        return last
                    P.op("pe", mmo, reads=b_merged + [so[n][1]], writes=[b_px1])
                    P.op("dve", lambda E, n=n, px1=px1: E.tensor_tensor(
                        out=x1[:, t, n * 512:(n + 1) * 512], in0=px1[:, :], in1=xt[k][:, n * 512:(n + 1) * 512], op=ALU.add),
                         reads=[b_px1, b_xt[k]], writes=[b_x1[t]])

            def stat_(t):
                ks[t] = norm_stat(x1[:, t, :], [b_x1[t]])

            def xs_(t):
                norm_xs(ks[t], x1[:, t, :], [b_x1[t]])

            def nb(t):
                norm_b(ks[t], G2, h2T, b_h2T, t, extra_w=B_CH)

            wo(0)
            stat_(0)
            wo(1)
            xs_(0)
            stat_(1)
            wo(2)
            xs_(1)
            stat_(2)
            nb(0)
            wo(3)
            release(2)
            xs_(2)
            stat_(3)
            nb(1)
            xs_(3)
            nb(2)
            nb(3)

        def stage5(row0, mid_hook=None):
            if mid_hook is not None and -1 in mid_hook:
                mid_hook[-1]()
            for j in range(6):
                sg_, b_sg = acquire(f"fg{j}")
                su_, b_su = acquire(f"fu{j}")
                for jj in range(4 if j < 5 else 2):
                    jc = j * 4 + jj
                    pg, b_pg = proj_fm(sg_, b_sg, 8, jj * 128, h2T, [b_h2T])
                    sl, b_sl = T_()
                    P.op("act", lambda E, pg=pg, sl=sl: E.activation(out=sl[:, :], in_=pg[:, :], func=AF.Silu),
                         reads=[b_pg], writes=[b_sl])
                    pu, b_pu = proj_fm(su_, b_su, 8, jj * 128, h2T, [b_h2T])
                    P.op("dve", lambda E, jc=jc, sl=sl, pu=pu: E.tensor_tensor(out=act[:, jc, :], in0=sl[:, :], in1=pu[:, :], op=ALU.mult),
                         reads=[b_sl, b_pu], writes=[b_act[jc]] + A_CH)
                release(2)
                if mid_hook is not None and j in mid_hook:
                    mid_hook[j]()
            for n in range(2):
                sd = [acquire(f"fd{n}{k}") for k in range(3)]
                for t in range(NT):
                    pd, b_pd = F_()

                    def mmd(E, t=t, pd=pd, sd=sd):
                        last = None
                        for jc in range(22):
                            last = E.matmul(pd[:, :], lhsT=act[:, jc, t * 128:(t + 1) * 128], rhs=sd[jc // 8][0][:, jc % 8, :],
                                            start=(jc == 0), stop=(jc == 21))
                        return last
                    P.op("pe", mmd, reads=b_act + [s_[1] for s_ in sd], writes=[b_pd])
                    P.op("dve", lambda E, t=t, n=n, pd=pd: E.tensor_tensor(
                        out=x1[:, t, n * 512:(n + 1) * 512], in0=pd[:, :], in1=x1[:, t, n * 512:(n + 1) * 512], op=ALU.add),
                         reads=[b_pd, b_x1[t]], writes=[b_x1[t]])
                    if n == 1:
                        k = cnt["st"] % NXS
                        cnt["st"] += 1
                        P.op("act", lambda E, t=t, k=k: E.activation(out=junk[:, :], in_=x1[:, t, :], func=AF.Square,
                                                                    accum_out=stat[k][:, 2:3]),
                             reads=[b_x1[t]], writes=[b_stat[k]])
                        P.op("act", lambda E, k=k: E.activation(out=stat[k][:, 3:4], in_=stat[k][:, 2:3], func=AF.Ln, scale=1.0 / D,
                                                                bias=cst[:, EPS:EPS + 1]),
                             reads=[b_stat[k], b_cst], writes=[b_stat[k]])
                        P.op("act", lambda E, k=k: E.activation(out=stat[k][:, 3:4], in_=stat[k][:, 3:4], func=AF.Exp, scale=-0.5),
                             reads=[b_stat[k]], writes=[b_stat[k]])
                        P.op("dve", lambda E, t=t, k=k: E.scalar_tensor_tensor(
                            out=x1[:, t, :], in0=x1[:, t, :], scalar=stat[k][:, 3:4], in1=cst[:, GF:GF + D],
                            op0=ALU.mult, op1=ALU.mult),
                             reads=[b_x1[t], b_stat[k], b_cst], writes=[b_x1[t]])
                        P.dma("sp", lambda E, t=t: E.dma_start(out=out_d[row0 + t * 128:row0 + (t + 1) * 128, :], in_=x1[:, t, :]),
                              reads=[b_x1[t]], chan=b_x1[t])
                release(3)

        blocks = [("pre", pb) for pb in reversed(range(NPRE))] + [("main", mb) for mb in range(NMAIN)]

        def src_of(bi):
            kind, idx = blocks[bi]
            return (x_prev if kind == "pre" else x_main), idx * T

        s_, r_ = src_of(0)
        cnt["fbmod"] = 5
        ks0 = stage0_a(s_, r_)
        for _ in range(R):
            issue_next()
        stage0_b(ks0, 0)
        for bi, (kind, idx) in enumerate(blocks):
            hp = bi % 2
            nxt = {"xk": {}, "ks": {}}
            has_next = bi + 1 < len(blocks)

            def pre_load(tiles, bi=bi, nxt=nxt, has_next=has_next):
                if has_next:
                    s2, r2 = src_of(bi + 1)
                    stage0_load(s2, r2, tiles, nxt)

            def pre_norm(tiles, nxt=nxt, has_next=has_next):
                if has_next:
                    stage0_norm(tiles, nxt)

            def pre_b(bi=bi, nxt=nxt, has_next=has_next):
                if has_next:
                    stage0_b([nxt["ks"][t] for t in range(NT)], (bi + 1) % 2)

            def h_start(pre_load=pre_load):
                pre_load([0, 1])

            def h_mid1(pre_load=pre_load, pre_norm=pre_norm):
                pre_norm([0, 1])
                pre_load([2, 3])

            def h_mid2(pre_norm=pre_norm):
                pre_norm([2, 3])

            if kind == "pre":
                stage1_pre(hp, first=(bi == 0), hooks={"start": h_start, "mid1": h_mid1, "mid2": h_mid2, "end": pre_b})
                if bi == 0:
                    conv_u(hp)
                    halo()
                    release(2)
                precache(NPC)
                if bi == NPRE - 1:
                    prefix_finish()
                    cnt["fbmod"] = 6
            else:
                stage1(bi % 2, hp, True)
                core(bi % 2, True, False)
                stage2(hp)
                xk4 = stage4_preload(idx * T)
                stage3(hp)
                stage4(idx * T, xk4)
                stage5(idx * T, mid_hook={-1: h_start, 0: h_mid1, 2: h_mid2, 4: pre_b})
        P.final_wait("sp", b_x1)
        P.emit(block)
    return nc


DEBUG = False
_NC_CACHE = {}


def _get_nc():
    if "nc" not in _NC_CACHE:
        _NC_CACHE["nc"] = build_nc()
    return _NC_CACHE["nc"]


def kernel(x, norm_mix_g, w_in, lower_bounds, hg_norm_g, conv_w, w_branch_a, w_branch_b,
           w_out, norm_ffn_g, w_ffn_gate, w_ffn_up, w_ffn_down, norm_final_g):
    f32 = np.float32
    x = np.asarray(x, f32)
    cst = np.zeros((128, NCST), f32)
    cst[:, G1:G1 + 8] = np.asarray(norm_mix_g, f32)[0].reshape(8, 128).T
    cst[:, G2:G2 + 8] = np.asarray(norm_ffn_g, f32)[0].reshape(8, 128).T
    lbs = np.asarray(lower_bounds, f32)
    cst[:, L0:L0 + 4] = lbs[0].reshape(4, 128).T
    cst[:, L1:L1 + 4] = lbs[1].reshape(4, 128).T
    cst[:, HGN] = np.asarray(hg_norm_g, f32)[0]
    cw = np.asarray(conv_w, f32)[0]
    for k in range(3):
        cst[:, CW + k * 4:CW + k * 4 + 4] = cw[k].reshape(4, 128).T
    cst[:, EPS] = 1e-6
    cst[:, ONE] = 1.0
    cst[:, GF:GF + D] = np.asarray(norm_final_g, f32)[None, :]
    weights = {
        "w_in": np.ascontiguousarray(np.asarray(w_in, f32)[0]),
        "w_a": np.ascontiguousarray(np.asarray(w_branch_a, f32)[0]),
        "w_b": np.ascontiguousarray(np.asarray(w_branch_b, f32)[0]),
        "w_out": np.ascontiguousarray(np.asarray(w_out, f32)[0]),
        "w_g": np.ascontiguousarray(np.asarray(w_ffn_gate, f32)[0]),
        "w_u": np.ascontiguousarray(np.asarray(w_ffn_up, f32)[0]),
        "w_d": np.ascontiguousarray(np.asarray(w_ffn_down, f32)[0]),
    }
    in_maps = []
    zeros = np.zeros((NPRE * T, D), f32)
    for i in range(8):
        b, half = i // 2, i % 2
        m = {"x_main": np.ascontiguousarray(x[b, half * TOK:(half + 1) * TOK]),
             "x_prev": np.ascontiguousarray(x[b, 0:TOK]) if half == 1 else zeros,
             "cst": cst}
        m.update(weights)
        in_maps.append(m)
    nc = _get_nc()
    res = run_bass_kernel_spmd(nc, in_maps, core_ids=list(range(8)))
    out = np.empty((4, 2 * TOK, D), f32)
    for i in range(8):
        b, half = i // 2, i % 2
        out[b, half * TOK:(half + 1) * TOK] = res.results[i]["out"]
    return out
```

```python
import numpy as np
from contextlib import ExitStack
import concourse.bass as bass
import concourse.mybir as mybir
from concourse.bass_utils import run_bass_kernel_spmd

F32 = mybir.dt.float32
BF16 = mybir.dt.bfloat16
ALU = mybir.AluOpType
AF = mybir.ActivationFunctionType

ENGS = ("pe", "act", "dve", "pool", "sp")

D = 1024
NIN = 5632
DFF = 2816
T = 512
NT = T // 128
CH = 64
NCH = T // CH
NMAIN = 4
NPRE = 4
TOK = NMAIN * T
R = 8
NTMP = 14
NXS = 4

G1, G2, L0, L1, HGN, CW, EPS, ONE, GF = 0, 8, 16, 20, 24, 25, 37, 38, 39
NCST = GF + D


class Buf:
    __slots__ = ("name", "w", "r", "dsem")

    def __init__(self, name):
        self.name = name
        self.w = None
        self.r = {}
        self.dsem = None


class Prog:
    def __init__(self, nc, stack):
        self.nc = nc
        self.stack = stack
        self.streams = {e: [] for e in ENGS}
        self.sems = {}
        self.count = {}
        self.known = {e: {} for e in ENGS}
        for e in ENGS:
            self._newsem("P_" + e)

    def _newsem(self, key):
        h = self.stack.enter_context(self.nc.semaphore(key))
        self.sems[key] = h
        self.count[key] = 0
        return h

    def _deps(self, eng, reads, writes):
        deps = {}
        for b in reads:
            if b.w is not None:
                k, v = b.w
                if deps.get(k, 0) < v:
                    deps[k] = v
        for b in writes:
            if b.w is not None:
                k, v = b.w
                if deps.get(k, 0) < v:
                    deps[k] = v
            for k, v in b.r.items():
                if deps.get(k, 0) < v:
                    deps[k] = v
        own = "P_" + eng
        kn = self.known[eng]
        for k, v in deps.items():
            if k == own and eng == "pe":
                continue
            if kn.get(k, 0) >= v:
                continue
            kn[k] = v
            self.streams[eng].append(("wait", self.sems[k], v))

    def op(self, eng, fn, reads=(), writes=()):
        self._deps(eng, reads, writes)
        key = "P_" + eng
        self.count[key] += 1
        t = self.count[key]
        self.streams[eng].append(("inc", fn, self.sems[key]))
        for b in reads:
            if b.r.get(key, 0) < t:
                b.r[key] = t
        for b in writes:
            b.w = (key, t)
            b.r = {}
        return t

    def dma(self, eng, fn, reads=(), writes=(), chan=None):
        self._deps(eng, reads, writes)
        if chan.dsem is None:
            chan.dsem = "D_" + chan.name
            self._newsem(chan.dsem)
        key = chan.dsem
        self.count[key] += 16
        t = self.count[key]
        self.streams[eng].append(("dma", fn, self.sems[key]))
        for b in reads:
            if b.r.get(key, 0) < t:
                b.r[key] = t
        for b in writes:
            b.w = (key, t)
            b.r = {}
        return t

    def final_wait(self, eng, bufs):
        self._deps(eng, (), bufs)

    def emit(self, block):
        def run(E, stream):
            for item in stream:
                kind = item[0]
                if kind == "wait":
                    E.wait_ge(item[1], item[2])
                elif kind == "inc":
                    item[1](E).then_inc(item[2], 1)
                else:
                    item[1](E).then_inc(item[2], 16)

        @block.tensor
        def _(E):
            run(E, self.streams["pe"])

        @block.scalar
        def _(E):
            run(E, self.streams["act"])

        @block.vector
        def _(E):
            run(E, self.streams["dve"])

        @block.gpsimd
        def _(E):
            run(E, self.streams["pool"])

        @block.sync
        def _(E):
            run(E, self.streams["sp"])


def slab_specs():
    sp = {}
    names = ["q", "f", "i", "g", "c", "bg", "xb", "ga0", "ga1", "gb0", "gb1"]
    for n, nm in enumerate(names):
        sp[nm] = ("w_in", 0, 8, n * 512, 512)
    for n in range(2):
        sp[f"wa{n}"] = ("w_a", 0, 4, n * 512, 512)
        sp[f"wb{n}"] = ("w_b", 0, 4, n * 512, 512)
        sp[f"wo{n}"] = ("w_out", 0, 8, n * 512, 512)
    for j in range(6):
        nc_ = 512 if j < 5 else 256
        sp[f"fg{j}"] = ("w_g", 0, 8, j * 512, nc_)
        sp[f"fu{j}"] = ("w_u", 0, 8, j * 512, nc_)
    for n in range(2):
        for k in range(3):
            kc = 8 if k < 2 else 6
            sp[f"fd{n}{k}"] = ("w_d", k * 1024, kc, n * 512, 512)
    return sp


MAIN_ORDER = (["f", "i", "q", "g", "c", "xb", "bg",
               "ga0", "wa0", "gb0", "wb0", "ga1", "wa1", "gb1", "wb1", "wo0", "wo1"]
              + [x for j in range(6) for x in (f"fg{j}", f"fu{j}")]
              + [f"fd{n}{k}" for n in range(2) for k in range(3)])


def build_nc():
    nc = bass.Bass("TRN2", target_bir_lowering=False)
    x_main = nc.dram_tensor("x_main", [TOK, D], F32, kind="ExternalInput").ap()
    x_prev = nc.dram_tensor("x_prev", [NPRE * T, D], F32, kind="ExternalInput").ap()
    cst_d = nc.dram_tensor("cst", [128, NCST], F32, kind="ExternalInput").ap()
    W = {
        "w_in": nc.dram_tensor("w_in", [D, NIN], F32, kind="ExternalInput").ap(),
        "w_a": nc.dram_tensor("w_a", [512, D], F32, kind="ExternalInput").ap(),
        "w_b": nc.dram_tensor("w_b", [512, D], F32, kind="ExternalInput").ap(),
        "w_out": nc.dram_tensor("w_out", [D, D], F32, kind="ExternalInput").ap(),
        "w_g": nc.dram_tensor("w_g", [D, DFF], F32, kind="ExternalInput").ap(),
        "w_u": nc.dram_tensor("w_u", [D, DFF], F32, kind="ExternalInput").ap(),
        "w_d": nc.dram_tensor("w_d", [DFF, D], F32, kind="ExternalInput").ap(),
    }
    out_d = nc.dram_tensor("out", [TOK, D], F32, kind="ExternalOutput").ap()
    dbg_out = {}
    if DEBUG:
        for nm, shp, dt in [("d_qin", [128, 4 * T], BF16), ("d_khat", [128, 4 * T], BF16), ("d_v", [128, NT * 512], BF16),
                            ("d_og", [128, 4 * T], BF16), ("d_gsil", [128, 4 * T], BF16), ("d_kout", [128, NT * 512], BF16),
                            ("d_hT", [128, 8 * T], BF16), ("d_S", [128, 2 * 4 * 128], F32), ("d_yb", [128, 4 * T], BF16),
                            ("d_merged", [128, 8 * T], BF16)]:
            dbg_out[nm] = nc.dram_tensor(nm, shp, dt, kind="ExternalOutput").ap()
    SPECS = slab_specs()
    cache_names = list(SPECS.keys())
    wcache = nc.dram_tensor("wcache", [len(cache_names), 128, 8 * 512], BF16).ap()
    cache_idx = {n: i for i, n in enumerate(cache_names)}

    with ExitStack() as st:
        def sb(name, shape, dt):
            return st.enter_context(nc.sbuf_tensor(name, shape, dt))

        def ps(name, shape, dt):
            return st.enter_context(nc.psum_tensor(name, shape, dt))

        cst = sb("cst_sb", [128, NCST], F32)
        identf = sb("identf", [128, 128], F32)
        ident = sb("ident", [128, 128], BF16)
        ones_f = sb("ones_f", [128, 128], F32)
        maskT = sb("maskT", [128, 128], F32)
        smask = sb("smask", [128, T], F32)
        diagw = sb("diagw", [128, 12, 128], BF16)
        sv = sb("sv", [128, 4, 4], F32)
        S = sb("S", [128, 2, 4, 128], F32)
        Sb = sb("Sb", [128, 2, 4, 128], BF16)
        decs = sb("decs", [128, 2, 4, NCH], F32)
        xt = [sb(f"xt{i}", [128, D], F32) for i in range(2)]
        xs = [sb(f"xs{i}", [128, D], BF16) for i in range(NXS)]
        stat = [sb(f"stat{i}", [128, 4], F32) for i in range(NXS)]
        junk = sb("junk", [128, 2 * T], BF16)
        omask = sb("omask", [128, T], F32)
        gdec = sb("gdec", [128, 4, 4], F32)
        arenaA = sb("arenaA", [128, 22 * T], BF16)
        act = arenaA[:, :].rearrange("p (j t) -> p j t", t=T)
        qin = arenaA[:, 0:4 * T].rearrange("p (h t) -> p h t", t=T)
        khat = arenaA[:, 4 * T:8 * T].rearrange("p (h t) -> p h t", t=T)
        kout = arenaA[:, 8 * T:12 * T].rearrange("p (t d) -> p t d", d=512)
        v = arenaA[:, 12 * T:16 * T].rearrange("p (t d) -> p t d", d=512)
        gsil = arenaA[:, 16 * T:20 * T].rearrange("p (h t) -> p h t", t=T)
        koutT = [arenaA[:, 20 * T:21 * T], arenaA[:, 21 * T:22 * T]]
        arenaB = sb("arenaB", [128, 8 * T], BF16)
        h2T = arenaB[:, :].rearrange("p (c t) -> p c t", t=T)
        og = arenaB[:, 0:4 * T].rearrange("p (h t) -> p h t", t=T)
        yb = arenaB[:, 4 * T:8 * T].rearrange("p (h t) -> p h t", t=T)
        hTs = [sb(f"hT{i}", [128, 8, T], BF16) for i in range(2)]
        u = sb("u", [128, 4, T + 2], BF16)
        merged = sb("merged", [128, 8, T], BF16)
        x1 = sb("x1", [128, NT, D], F32)
        AT = [sb(f"AT{i}", [128, 128], BF16) for i in range(4)]
        tmp = [sb(f"tmp{i}", [128, T], F32) for i in range(NTMP)]
        ring = [sb(f"ring{i}", [128, 8, 512], BF16) for i in range(R)]
        FB = [ps(f"pf{i}", [128, 512], F32) for i in range(6)]
        TBs = [ps(f"ptb{i}", [128, 1024], BF16) for i in range(2)]

        block = st.enter_context(nc.Block())
        P = Prog(nc, st)

        b_cst, b_ident, b_identf, b_ones, b_maskT, b_smask, b_diagw, b_sv = (Buf(n) for n in
            ["cst", "ident", "identf", "ones", "maskT", "smask", "diagw", "sv"])
        b_S = [[Buf(f"S{p}{h}") for h in range(4)] for p in range(2)]
        b_Sb = [[Buf(f"Sb{p}{h}") for h in range(4)] for p in range(2)]
        b_decs = [[Buf(f"decs{p}{h}") for h in range(4)] for p in range(2)]
        b_xt = [Buf(f"xt{i}") for i in range(2)]
        b_xs = [Buf(f"xs{i}") for i in range(NXS)]
        b_stat = [Buf(f"stat{i}") for i in range(NXS)]
        b_act = [Buf(f"act{j}") for j in range(22)]
        b_hTs = [Buf("hT0"), Buf("hT1")]
        b_qin = [Buf(f"qin{h}") for h in range(4)]
        b_khat = [Buf(f"khat{h}") for h in range(4)]
        b_kout = [Buf(f"kout{h}") for h in range(4)]
        b_koutT = [Buf(f"koutT{i}") for i in range(2)]
        b_v = [Buf(f"v{t}") for t in range(NT)]
        b_gsil = [Buf(f"gsil{h}") for h in range(4)]
        b_og = [Buf(f"og{h}") for h in range(4)]
        b_u = Buf("u")
        b_omask = Buf("omask")
        b_gdecs = [Buf(f"gdec{h}") for h in range(4)]
        b_yb = [Buf(f"yb{m}") for m in range(4)]
        b_merged = [Buf(f"merged{m}") for m in range(8)]
        b_x1 = [Buf(f"x1{t}") for t in range(NT)]
        b_h2T = Buf("h2T")
        b_AT = [Buf(f"AT{i}") for i in range(4)]
        b_tmp = [Buf(f"tmp{i}") for i in range(NTMP)]
        b_ring = [Buf(f"ring{i}") for i in range(R)]
        b_FB = [Buf(f"FB{i}") for i in range(6)]
        b_pS = [[Buf(f"pS{k}{h}") for h in range(4)] for k in range(2)]
        b_TBs = [Buf("TB0"), Buf("TB1")]
        b_cache = {n: Buf("wc_" + n) for n in cache_names}
        A_CH = b_qin + b_khat + b_kout + b_koutT + b_v + b_gsil
        B_CH = b_og + b_yb

        cnt = {"tmp": 0, "fb": 0, "xt": 0, "at": 0, "kt": 0, "st": 0, "os": 0, "tb": 0, "fbmod": 6}

        def TB_():
            i = cnt["tb"] % 2
            cnt["tb"] += 1
            return TBs[i], b_TBs[i]

        def T_():
            i = cnt["tmp"] % NTMP
            cnt["tmp"] += 1
            return tmp[i], b_tmp[i]

        def F_():
            i = cnt["fb"] % cnt["fbmod"]
            cnt["fb"] += 1
            return FB[i], b_FB[i]

        NPC = 4
        pc_list = [n for n in MAIN_ORDER if n not in ("f", "i", "c", "xb")][:NPC * NPRE]
        sched = []
        for pb in range(NPRE):
            sched += ["f", "i"]
            if pb == 0:
                sched += ["c", "xb"]
            sched += ["PC:" + n for n in pc_list[pb * NPC:(pb + 1) * NPC]]
        for mb in range(NMAIN):
            sched += MAIN_ORDER
        ring_state = {"issued": 0, "acq": 0}
        cached = set()
        ch_sw = [Buf(f"rsw{i}") for i in range(R)]
        ch_hw = [Buf(f"rhw{i}") for i in range(R)]

        pending = {"store": None}

        def flush_store():
            if pending["store"] is not None:
                fn_, reads_, writes_, chan_ = pending["store"]
                P.dma("pool", fn_, reads=reads_, writes=writes_, chan=chan_)
                pending["store"] = None

        def issue_next():
            g = ring_state["issued"]
            if g >= len(sched):
                flush_store()
                return
            ring_state["issued"] += 1
            name = sched[g]
            if name.startswith("PC:"):
                name = name[3:]
            s = g % R
            wkey, r0, kc, c0, ncols = SPECS[name]
            dst = ring[s][:, 0:kc, 0:ncols]
            ci = cache_idx[name]
            cview = wcache[ci].rearrange("p (c n) -> p c n", n=512)[:, 0:kc, 0:ncols]
            if name in cached:
                flush_store()
                P.dma("sp", lambda E: E.dma_start(out=dst, in_=cview), reads=[b_cache[name]],
                      writes=[b_ring[s]], chan=ch_hw[s])
            else:
                src = W[wkey][r0:r0 + kc * 128, :].rearrange("(c p) n -> p c n", p=128)[:, :, c0:c0 + ncols]
                P.dma("pool", lambda E: E.dma_start(out=dst, in_=src), writes=[b_ring[s]], chan=ch_sw[s])
                cached.add(name)
                flush_store()
                pending["store"] = (lambda E: E.dma_start(out=cview, in_=dst), [b_ring[s]], [b_cache[name]], ch_sw[s])

        def acquire(name):
            g = ring_state["acq"]
            assert sched[g] == name, (g, sched[g], name)
            ring_state["acq"] += 1
            assert g < ring_state["issued"], "ring underflow"
            return ring[g % R], b_ring[g % R]

        def precache(n):
            for _ in range(n):
                g = ring_state["acq"]
                assert sched[g].startswith("PC:"), (g, sched[g])
                ring_state["acq"] += 1
                issue_next()

        def release(n=1):
            for _ in range(n):
                issue_next()

        P.dma("sp", lambda E: E.dma_start(out=cst[:], in_=cst_d), writes=[b_cst], chan=b_cst)
        P.op("dve", lambda E: E.memset(identf[:], 0.0), writes=[b_identf])
        P.op("pool", lambda E: E.affine_select(out=identf[:], in_=identf[:], pattern=[[-1, 128]],
                                               compare_op=ALU.not_equal, fill=1.0, base=0, channel_multiplier=1),
             reads=[b_identf], writes=[b_identf])
        P.op("dve", lambda E: E.tensor_copy(out=ident[:], in_=identf[:]), reads=[b_identf], writes=[b_ident])
        P.op("dve", lambda E: E.memset(ones_f[:], 1.0), writes=[b_ones])
        P.op("dve", lambda E: E.memset(maskT[:], 1.0), writes=[b_maskT])
        P.op("pool", lambda E: E.affine_select(out=maskT[:], in_=maskT[:], pattern=[[1, 128]],
                                               compare_op=ALU.is_ge, fill=0.0, base=0, channel_multiplier=-1),
             reads=[b_maskT], writes=[b_maskT])
        P.op("dve", lambda E: E.memset(maskT[0:64, 64:128], 0.0), reads=[b_maskT], writes=[b_maskT])
        P.op("dve", lambda E: E.memset(smask[:], 1.0), writes=[b_smask])
        P.op("dve", lambda E: E.memset(smask[:].rearrange("p (c j) -> p c j", j=CH)[:, :, 0:1], 0.0),
             reads=[b_smask], writes=[b_smask])
        for k in range(12):
            P.op("dve", lambda E, k=k: E.tensor_scalar(out=diagw[:, k, :], in0=identf[:], scalar1=cst[:, CW + k:CW + k + 1],
                                                       scalar2=0.0, op0=ALU.mult, op1=ALU.add),
                 reads=[b_identf, b_cst], writes=[b_diagw])
        P.op("dve", lambda E: E.tensor_tensor(out=sv[:, :, 3], in0=cst[:, L0:L0 + 4], in1=cst[:, L1:L1 + 4], op=ALU.subtract),
             reads=[b_cst], writes=[b_sv])
        P.op("act", lambda E: E.activation(out=sv[:, :, 1], in_=sv[:, :, 3], func=AF.Sigmoid), reads=[b_sv], writes=[b_sv])
        P.op("act", lambda E: E.activation(out=sv[:, :, 0], in_=sv[:, :, 3], func=AF.Sigmoid, scale=-1.0),
             reads=[b_sv], writes=[b_sv])
        P.op("dve", lambda E: E.tensor_scalar(out=sv[:, :, 2], in0=sv[:, :, 0], scalar1=-1.0, scalar2=0.0, op0=ALU.mult, op1=ALU.add),
             reads=[b_sv], writes=[b_sv])
        P.op("dve", lambda E: E.memset(omask[:], 1.0), writes=[b_omask])
        P.op("dve", lambda E: E.memset(gdec[:], 1.0), writes=b_gdecs)
        P.op("dve", lambda E: E.memset(S[:, 0, :, :], 0.0), writes=b_S[0])
        P.op("dve", lambda E: E.memset(Sb[:, 0, :, :], 0.0), writes=b_Sb[0])
        P.op("dve", lambda E: E.memset(u[:], 0.0), writes=[b_u])
        chain = {"step": 0}

        def norm_stat(src_tile_ap, src_reads):
            k = cnt["st"] % NXS
            cnt["st"] += 1
            P.op("act", lambda E: E.activation(out=junk[:, :], in_=src_tile_ap, func=AF.Square, accum_out=stat[k][:, 2:3]),
                 reads=src_reads, writes=[b_stat[k]])
            P.op("act", lambda E: E.activation(out=stat[k][:, 3:4], in_=stat[k][:, 2:3], func=AF.Ln, scale=1.0 / D,
                                               bias=cst[:, EPS:EPS + 1]),
                 reads=[b_stat[k], b_cst], writes=[b_stat[k]])
            P.op("act", lambda E: E.activation(out=stat[k][:, 3:4], in_=stat[k][:, 3:4], func=AF.Exp, scale=-0.5),
                 reads=[b_stat[k]], writes=[b_stat[k]])
            return k

        def norm_xs(k, src_tile_ap, src_reads):
            P.op("dve", lambda E: E.tensor_scalar(out=xs[k][:], in0=src_tile_ap, scalar1=stat[k][:, 3:4], scalar2=0.0,
                                                  op0=ALU.mult, op1=ALU.add),
                 reads=list(src_reads) + [b_stat[k]], writes=[b_xs[k]])

        def norm_a(src_tile_ap, src_reads):
            k = norm_stat(src_tile_ap, src_reads)
            norm_xs(k, src_tile_ap, src_reads)
            return k

        def norm_b(k, gcol, dstT, b_dst, t, extra_w=()):
            TB, b_TB = TB_()

            def tr(E):
                last = None
                for c in range(8):
                    last = E.transpose(out=TB[:, c * 128:(c + 1) * 128], in_=xs[k][:, c * 128:(c + 1) * 128], identity=ident[:])
                return last
            P.op("pe", tr, reads=[b_xs[k], b_ident], writes=[b_TB])
            P.op("dve", lambda E: E.tensor_tensor(
                out=dstT[:, :, t * 128:(t + 1) * 128],
                in0=TB[:].rearrange("p (c t) -> p c t", t=128),
                in1=cst[:, gcol:gcol + 8].rearrange("p (c o) -> p c o", o=1).to_broadcast([128, 8, 128]),
                op=ALU.mult), reads=[b_TB, b_cst], writes=[b_dst] + list(extra_w))

        def load_x(src, r0):
            k = cnt["xt"] % 2
            cnt["xt"] += 1
            P.dma("sp", lambda E: E.dma_start(out=xt[k][:], in_=src[r0:r0 + 128, :]), writes=[b_xt[k]], chan=b_xt[k])
            return k

        def proj_fm(slab, b_slab, kc, col0, rhsT, b_rhs):
            pf, b_pf = F_()

            def mm(E):
                last = None
                for c in range(kc):
                    last = E.matmul(pf[:, :], lhsT=slab[:, c, col0:col0 + 128], rhs=rhsT[:, c, :],
                                    start=(c == 0), stop=(c == kc - 1))
                return last
            P.op("pe", mm, reads=[b_slab] + list(b_rhs), writes=[b_pf])
            return pf, b_pf

        def stage0_load(src, row0, tiles, st):
            for t in tiles:
                st["xk"][t] = load_x(src, row0 + t * 128)

        def stage0_norm(tiles, st):
            for t in tiles:
                k = st["xk"][t]
                st["ks"][t] = norm_a(xt[k][:, :], [b_xt[k]])

        def stage0_a(src, row0):
            st = {"xk": {}, "ks": {}}
            stage0_load(src, row0, [0, 1], st)
            stage0_norm([0, 1], st)
            stage0_load(src, row0, [2, 3], st)
            stage0_norm([2, 3], st)
            return [st["ks"][t] for t in range(NT)]

        def stage0_b(ks, hp):
            for t in range(NT):
                norm_b(ks[t], G1, hTs[hp], b_hTs[hp], t)

        def lockstep(*lists):
            n = max(len(l) for l in lists)
            for i in range(n):
                for l in lists:
                    if i < len(l):
                        l[i]()

        def make_vtile(hT, b_hT, si, b_si, on_dve=False):
            def vtile(t):
                pv, b_pv = F_()

                def mmv(E):
                    last = None
                    for c in range(8):
                        last = E.matmul(pv[:, :], lhsT=hT[:, c, t * 128:(t + 1) * 128], rhs=si[:, c, :],
                                        start=(c == 0), stop=(c == 7))
                    return last
                P.op("pe", mmv, reads=[b_si, b_hT], writes=[b_pv])
                if on_dve:
                    P.op("dve", lambda E: E.tensor_copy(out=v[:, t, :], in_=pv[:, :]), reads=[b_pv], writes=[b_v[t]] + b_act)
                else:
                    P.op("act", lambda E: E.activation(out=v[:, t, :], in_=pv[:, :], func=AF.Copy),
                         reads=[b_pv], writes=[b_v[t]] + b_act)
            return vtile

        def gate_p1(h, pf, b_pf, r_e, rb_e, r_sg, rb_sg, r_km, rb_km, r_lf, rb_lf):
            return [
                lambda: P.op("act", lambda E: E.activation(out=r_e[:, :], in_=pf[:, :], func=AF.Exp, scale=-1.0),
                             reads=[b_pf], writes=[rb_e]),
                lambda: P.op("act", lambda E: E.activation(out=r_e[:, :], in_=r_e[:, :], func=AF.Ln, bias=cst[:, ONE:ONE + 1]),
                             reads=[rb_e, b_cst], writes=[rb_e]),
                lambda: P.op("act", lambda E: E.activation(out=r_sg[:, :], in_=r_e[:, :], func=AF.Exp, scale=-1.0),
                             reads=[rb_e], writes=[rb_sg]),
                lambda: P.op("act", lambda E: E.activation(out=r_km[:, :], in_=r_sg[:, :], func=AF.Identity, scale=sv[:, h, 2:3],
                                                           bias=sv[:, h, 0:1]),
                             reads=[rb_sg, b_sv], writes=[rb_km]),
                lambda: P.op("act", lambda E: E.activation(out=r_lf[:, :], in_=r_sg[:, :], func=AF.Ln, scale=sv[:, h, 0:1],
                                                           bias=sv[:, h, 1:2]),
                             reads=[rb_sg, b_sv], writes=[rb_lf]),
            ]

        def kout_tail(h, kk):
            hc = h * 128
            st_ = {}

            def t1():
                TB, b_TB = TB_()
                st_["tb"] = (TB, b_TB)

                def trk(E):
                    last = None
                    for t in range(NT):
                        last = E.transpose(out=TB[:, t * 128:(t + 1) * 128], in_=koutT[kk][:, t * 128:(t + 1) * 128],
                                           identity=ident[:])
                    return last
                P.op("pe", trk, reads=[b_koutT[kk], b_ident], writes=[b_TB])

            def t2():
                TB, b_TB = st_["tb"]
                P.op("act", lambda E: E.activation(out=kout[:, :, hc:hc + 128],
                                                   in_=TB[:, 0:512].rearrange("p (t d) -> p t d", d=128), func=AF.Copy),
                     reads=[b_TB], writes=[b_kout[h]] + b_act)
            return [t1, t2]

        def stage1(blk_par, hp, main, hooks=None):
            hT = hTs[hp]
            b_hT = b_hTs[hp]
            sf, b_sf = acquire("f")
            si, b_si = acquire("i")
            vtile = make_vtile(hT, b_hT, si, b_si)
            r_e, rb_e = [tmp[0], tmp[1]], [b_tmp[0], b_tmp[1]]
            r_sg, rb_sg = [tmp[2], tmp[3]], [b_tmp[2], b_tmp[3]]
            r_km, rb_km = [tmp[4], tmp[5]], [b_tmp[4], b_tmp[5]]
            r_lf, rb_lf = [tmp[6], tmp[7]], [b_tmp[6], b_tmp[7]]
            r_ee, rb_ee = tmp[8:12], b_tmp[8:12]
            pfs = [proj_fm(sf, b_sf, 8, h * 128, hT, [b_hT]) for h in range(4)]

            def p1(h):
                p = h % 2
                return gate_p1(h, pfs[h][0], pfs[h][1], r_e[p], rb_e[p], r_sg[p], rb_sg[p], r_km[p], rb_km[p], r_lf[p], rb_lf[p])

            def p2(h):
                p = h % 2
                bb, b_bb = r_e[p], rb_e[p]
                ei, b_ei = r_sg[p], rb_sg[p]
                ee, b_ee = r_ee[h], rb_ee[h]
                kk = p
                ops = [
                    lambda: P.op("dve", lambda E: E.tensor_tensor_scan(out=bb[:, :], data0=smask[:, :], data1=r_lf[p][:, :], initial=0.0,
                                                                       op0=ALU.mult, op1=ALU.add),
                                 reads=[b_smask, rb_lf[p]], writes=[b_bb]),
                    lambda: P.op("act", lambda E: E.activation(out=ee[:, :], in_=bb[:, :], func=AF.Exp), reads=[b_bb], writes=[b_ee]),
                    lambda: P.op("act", lambda E: E.activation(out=ei[:, :], in_=bb[:, :], func=AF.Exp, scale=-1.0),
                                 reads=[b_bb], writes=[b_ei]),
                    lambda: P.op("dve", lambda E: E.tensor_tensor(out=khat[:, h, :], in0=r_km[p][:, :], in1=ei[:, :], op=ALU.mult),
                                 reads=[rb_km[p], b_ei], writes=[b_khat[h]] + b_act),
                    lambda: P.op("dve", lambda E: E.tensor_copy(
                        out=decs[:, blk_par, h, :],
                        in_=ee[:, :].rearrange("p (c j) -> p c j", j=CH)[:, :, CH - 1]),
                        reads=[b_ee], writes=[b_decs[blk_par][h]]),
                    lambda: P.op("dve", lambda E: E.tensor_tensor(
                        out=koutT[kk].rearrange("p (c j) -> p c j", j=CH),
                        in0=khat[:, h, :].rearrange("p (c j) -> p c j", j=CH),
                        in1=decs[:, blk_par, h, :].rearrange("p (c o) -> p c o", o=1).to_broadcast([128, NCH, CH]),
                        op=ALU.mult), reads=[b_khat[h], b_decs[blk_par][h]], writes=[b_koutT[kk]] + b_act),
                ]
                return ops

            hooks = hooks or {}
            hk = lambda name: hooks[name]() if name in hooks else None
            hk("start")

            def tails(h):
                for th in kout_tail(h, h % 2):
                    th()

            lockstep(p1(0), p1(1))
            vtile(0)
            lockstep(p2(0), p2(1))
            hk("mid1")
            lockstep(p1(2), p1(3))
            vtile(1)
            vtile(2)
            tails(0)
            tails(1)
            hk("mid2")
            lockstep(p2(2), p2(3))
            vtile(3)
            release(2)
            hk("end")
            sq_, b_sq = acquire("q")
            sg_, b_sg = acquire("g")

            def QG(h):
                hc = h * 128
                qs, b_qs = (tmp[0], b_tmp[0]) if h % 2 == 0 else (tmp[1], b_tmp[1])
                pq, b_pq = proj_fm(sq_, b_sq, 8, hc, hT, [b_hT])
                P.op("act", lambda E: E.activation(out=qs[:, :], in_=pq[:, :], func=AF.Silu), reads=[b_pq], writes=[b_qs])
                pg, b_pg = proj_fm(sg_, b_sg, 8, hc, hT, [b_hT])
                P.op("act", lambda E: E.activation(out=gsil[:, h, :], in_=pg[:, :], func=AF.Silu),
                     reads=[b_pg], writes=[b_gsil[h]] + b_act)
                P.op("dve", lambda E: E.scalar_tensor_tensor(out=qin[:, h, :], in0=qs[:, :], scalar=float(128 ** -0.5),
                                                             in1=r_ee[h][:, :], op0=ALU.mult, op1=ALU.mult),
                     reads=[b_qs, rb_ee[h]], writes=[b_qin[h]] + b_act)
            for h in range(4):
                QG(h)
            tails(2)
            tails(3)
            release(2)

        def stage1_pre(hp, first, hooks=None):
            hT = hTs[hp]
            b_hT = b_hTs[hp]
            sf, b_sf = acquire("f")
            si, b_si = acquire("i")
            vtile = make_vtile(hT, b_hT, si, b_si, on_dve=True)
            ie, ikm, ilf = [0, 1, 8, 9], [4, 5, 10, 11], [6, 7, 12, 13]
            r_e, rb_e = [tmp[i] for i in ie], [b_tmp[i] for i in ie]
            r_sg, rb_sg = [tmp[2], tmp[3]], [b_tmp[2], b_tmp[3]]
            r_km, rb_km = [tmp[i] for i in ikm], [b_tmp[i] for i in ikm]
            r_lf, rb_lf = [tmp[i] for i in ilf], [b_tmp[i] for i in ilf]
            pS_, b_pS_ = FB[5], b_FB[5]
            pfs = [proj_fm(sf, b_sf, 8, h * 128, hT, [b_hT]) for h in range(4)]

            def p1(h):
                p = h % 2
                return gate_p1(h, pfs[h][0], pfs[h][1], r_e[h], rb_e[h], r_sg[p], rb_sg[p], r_km[h], rb_km[h], r_lf[h], rb_lf[h])

            def p2(h):
                p = h % 2
                bb, b_bb = r_e[h], rb_e[h]
                lf, b_lf = r_lf[h], rb_lf[h]
                bg = b_gdecs[h]
                kk = p
                ops = [
                    lambda: P.op("dve", lambda E: E.tensor_tensor_scan(out=bb[:, :], data0=omask[:, :], data1=lf[:, :], initial=0.0,
                                                                       op0=ALU.mult, op1=ALU.add),
                                 reads=[b_omask, b_lf], writes=[b_bb]),
                    lambda: P.op("dve", lambda E: E.tensor_scalar(out=gdec[:, h, 2:3], in0=bb[:, T - 1:T], scalar1=-40.0, scalar2=0.0,
                                                                  op0=ALU.max, op1=ALU.add), reads=[b_bb, bg], writes=[bg]),
                    lambda: P.op("dve", lambda E: E.tensor_scalar(out=lf[:, :], in0=bb[:, :], scalar1=bb[:, T - 1:T], scalar2=40.0,
                                                                  op0=ALU.subtract, op1=ALU.min), reads=[b_bb], writes=[b_lf]),
                    lambda: P.op("act", lambda E: E.activation(out=lf[:, :], in_=lf[:, :], func=AF.Exp, scale=-1.0),
                                 reads=[b_lf], writes=[b_lf]),
                    lambda: P.op("dve", lambda E: E.tensor_scalar(out=lf[:, :], in0=lf[:, :], scalar1=gdec[:, h, 0:1], scalar2=1e-30,
                                                                  op0=ALU.mult, op1=ALU.max),
                                 reads=[b_lf, bg], writes=[b_lf]),
                    lambda: P.op("dve", lambda E: E.tensor_tensor(out=koutT[kk], in0=r_km[h][:, :], in1=lf[:, :], op=ALU.mult),
                                 reads=[rb_km[h], b_lf], writes=[b_koutT[kk]] + b_act),
                    lambda: P.op("act", lambda E: E.activation(out=gdec[:, h, 1:2], in_=gdec[:, h, 2:3], func=AF.Exp),
                                 reads=[bg], writes=[bg]),
                    lambda: P.op("dve", lambda E: E.tensor_scalar(out=gdec[:, h, 0:1], in0=gdec[:, h, 0:1], scalar1=gdec[:, h, 1:2],
                                                                  scalar2=1e-30, op0=ALU.mult, op1=ALU.max),
                                 reads=[bg], writes=[bg]),
                ]
                return ops + kout_tail(h, kk)

            hooks = hooks or {}
            hk = lambda name: hooks[name]() if name in hooks else None
            hk("start")
            lockstep(p1(0), p1(1))
            vtile(0)
            hk("mid1")
            lockstep(p2(0), p2(1), p1(2), p1(3))
            vtile(1)
            vtile(2)
            hk("mid2")
            lockstep(p2(2), p2(3))
            vtile(3)
            release(2)
            hk("end")
            for h in range(4):
                hc = h * 128

                def mms(E, h=h, hc=hc):
                    last = None
                    for t in range(NT):
                        last = E.matmul(pS_[:, hc:hc + 128], lhsT=kout[:, t, hc:hc + 128], rhs=v[:, t, hc:hc + 128],
                                        start=(first and h == 0 and t == 0), stop=False, skip_group_check=True)
                    return last
                P.op("pe", mms, reads=[b_kout[h]] + b_v, writes=[b_pS_])

        def prefix_finish():
            pS_, b_pS_ = FB[5], b_FB[5]
            dS_t, dS_b = tmp[0], b_tmp[0]
            P.op("dve", lambda E: E.tensor_copy(out=dS_t[:, :], in_=pS_[:, :]), reads=[b_pS_], writes=[dS_b])
            P.op("dve", lambda E: E.tensor_copy(out=S[:, 0, :, :].rearrange("p h d -> p (h d)"), in_=dS_t[:, :]),
                 reads=[dS_b], writes=b_S[0])
            P.op("act", lambda E: E.activation(out=Sb[:, 0, :, :].rearrange("p h d -> p (h d)"), in_=dS_t[:, :], func=AF.Copy),
                 reads=[dS_b], writes=b_Sb[0])

        def core(blk_par, main, last_pre):
            po = FB[0:4]
            b_po = b_FB[0:4]
            if main:
                items = [(h, t) for h in range(4) for t in range(NT)]

                def emit_psc(i):
                    h, t = items[i]
                    tc_ = slice(t * 128, (t + 1) * 128)
                    psc, b_psc = FB[4 + i % 2], b_FB[4 + i % 2]
                    P.op("pe", lambda E: E.matmul(psc[:, 0:128], lhsT=khat[:, h, tc_], rhs=qin[:, h, tc_], start=True, stop=True),
                         reads=[b_khat[h], b_qin[h]], writes=[b_psc])

                def emit_po(i):
                    h, t = items[i]
                    hc = h * 128
                    tc_ = slice(t * 128, (t + 1) * 128)
                    psc, b_psc = FB[4 + i % 2], b_FB[4 + i % 2]
                    ai = i % 4
                    P.op("dve", lambda E: E.tensor_tensor(out=AT[ai][:, :], in0=psc[:, 0:128], in1=maskT[:, :], op=ALU.mult),
                         reads=[b_psc, b_maskT], writes=[b_AT[ai]])
                    P.op("pe", lambda E: E.matmul(po[h][:, tc_], lhsT=v[:, t, hc:hc + 128], rhs=AT[ai][:, :],
                                                  start=(t == 0), stop=False, skip_group_check=True),
                         reads=[b_v[t], b_AT[ai]], writes=[b_po[h]])
                emit_psc(0)
                for i in range(len(items)):
                    if i + 1 < len(items):
                        emit_psc(i + 1)
                    emit_po(i)

            def emit_pS(cc):
                t = cc // 2
                r0 = (cc % 2) * CH
                bank = 4 + cc % 2
                for h in range(4):
                    hc = h * 128
                    P.op("pe", lambda E, h=h, hc=hc: E.matmul(
                        FB[bank][:, hc:hc + 128], lhsT=kout[r0:r0 + CH, t, hc:hc + 128], rhs=v[r0:r0 + CH, t, hc:hc + 128],
                        start=True, stop=True),
                         reads=[b_kout[h], b_v[t]], writes=[b_pS[cc % 2][h], b_FB[bank]])
            emit_pS(0)
            for cc in range(NCH):
                if cc + 1 < NCH:
                    emit_pS(cc + 1)
                bank = 4 + cc % 2
                par = chain["step"] % 2
                npar = 1 - par
                need_bf = main or (last_pre and cc == NCH - 1)
                for h in range(4):
                    hc = h * 128
                    if main:
                        P.op("pe", lambda E, h=h, cc=cc, par=par: E.matmul(
                            po[h][:, cc * CH:(cc + 1) * CH], lhsT=Sb[:, par, h, :], rhs=qin[:, h, cc * CH:(cc + 1) * CH],
                            start=False, stop=True, skip_group_check=True),
                             reads=[b_Sb[par][h], b_qin[h]], writes=[b_po[h]])
                for h in range(4):
                    hc = h * 128
                    P.op("dve", lambda E, h=h, hc=hc, cc=cc, par=par, npar=npar, bank=bank: E.scalar_tensor_tensor(
                        out=S[:, npar, h, :], in0=S[:, par, h, :], scalar=decs[:, blk_par, h, cc:cc + 1],
                        in1=FB[bank][:, hc:hc + 128], op0=ALU.mult, op1=ALU.add),
                         reads=[b_S[par][h], b_decs[blk_par][h], b_pS[cc % 2][h], b_FB[bank]], writes=[b_S[npar][h]])
                    if need_bf:
                        P.op("act", lambda E, h=h, npar=npar: E.activation(out=Sb[:, npar, h, :], in_=S[:, npar, h, :], func=AF.Copy),
                             reads=[b_S[npar][h]], writes=[b_Sb[npar][h]])
                chain["step"] += 1
            if main:
                sq2s = [T_() for _ in range(4)]
                rss = [T_() for _ in range(4)]
                pS_deps = [x for k3 in range(2) for x in b_pS[k3]]
                for h in range(4):
                    sq2, b_sq2 = sq2s[h]
                    P.op("act", lambda E, h=h, sq2=sq2: E.activation(out=sq2[:, :], in_=po[h][:, :], func=AF.Square),
                         reads=[b_po[h]], writes=[b_sq2])
                for hh in range(2):
                    for h in (2 * hh, 2 * hh + 1):
                        sq2, b_sq2 = sq2s[h]
                        rs, b_rs = rss[h]
                        pss, b_pss = (FB[4], b_FB[4]) if (h % 2 == 0) else (FB[5], b_FB[5])
                        P.op("pe", lambda E, pss=pss, sq2=sq2: E.matmul(pss[:, :], lhsT=ones_f[:, :], rhs=sq2[:, :], start=True, stop=True),
                             reads=[b_ones, b_sq2], writes=[b_pss] + pS_deps)
                        P.op("act", lambda E, pss=pss, rs=rs: E.activation(out=rs[:, :], in_=pss[:, :], func=AF.Ln, scale=1.0 / 128,
                                                                           bias=cst[:, EPS:EPS + 1]),
                             reads=[b_pss, b_cst], writes=[b_rs])
                    for h in (2 * hh, 2 * hh + 1):
                        rs, b_rs = rss[h]
                        P.op("act", lambda E, rs=rs: E.activation(out=rs[:, :], in_=rs[:, :], func=AF.Exp, scale=-0.5),
                             reads=[b_rs], writes=[b_rs])
                for h in range(4):
                    rs, b_rs = rss[h]
                    t1, b_t1 = sq2s[h]
                    P.op("dve", lambda E, h=h, t1=t1, rs=rs: E.scalar_tensor_tensor(
                        out=t1[:, :], in0=po[h][:, :], scalar=cst[:, HGN:HGN + 1], in1=rs[:, :], op0=ALU.mult, op1=ALU.mult),
                         reads=[b_po[h], b_cst, b_rs], writes=[b_t1])
                    P.op("dve", lambda E, h=h, t1=t1: E.tensor_tensor(out=og[:, h, :], in0=t1[:, :], in1=gsil[:, h, :], op=ALU.mult),
                         reads=[b_t1, b_gsil[h]], writes=[b_og[h], b_h2T])

        def conv_u(hp):
            hT = hTs[hp]
            b_hT = b_hTs[hp]
            sc_, b_sc = acquire("c")
            sx_, b_sx = acquire("xb")
            for m in range(4):
                pc, b_pc = proj_fm(sc_, b_sc, 8, m * 128, hT, [b_hT])
                cs, b_cs = T_()
                P.op("act", lambda E, pc=pc, cs=cs: E.activation(out=cs[:, :], in_=pc[:, :], func=AF.Copy),
                     reads=[b_pc], writes=[b_cs])
                px, b_px = proj_fm(sx_, b_sx, 8, m * 128, hT, [b_hT])
                P.op("dve", lambda E, m=m, cs=cs, px=px: E.tensor_tensor(out=u[:, m, 2:T + 2], in0=cs[:, :], in1=px[:, :], op=ALU.mult),
                     reads=[b_cs, b_px], writes=[b_u])

        def halo():
            P.op("dve", lambda E: E.tensor_copy(out=u[:, :, 0:2], in_=u[:, :, T:T + 2]), reads=[b_u], writes=[b_u])

        def stage2(hp):
            hT = hTs[hp]
            b_hT = b_hTs[hp]
            conv_u(hp)
            sbg, b_sbg = acquire("bg")
            for m in range(4):
                py, b_py = F_()

                def mmc(E, m=m, py=py):
                    last = None
                    for k in range(3):
                        last = E.matmul(py[:, :], lhsT=diagw[:, k * 4 + m, :], rhs=u[:, m, k:k + T], start=(k == 0), stop=(k == 2))
                    return last
                P.op("pe", mmc, reads=[b_diagw, b_u], writes=[b_py])
                pb, b_pb = proj_fm(sbg, b_sbg, 8, m * 128, hT, [b_hT])
                bs, b_bs = T_()
                P.op("act", lambda E, pb=pb, bs=bs: E.activation(out=bs[:, :], in_=pb[:, :], func=AF.Copy),
                     reads=[b_pb], writes=[b_bs])
                P.op("dve", lambda E, m=m, bs=bs, py=py: E.tensor_tensor(out=yb[:, m, :], in0=bs[:, :], in1=py[:, :], op=ALU.mult),
                     reads=[b_bs, b_py], writes=[b_yb[m], b_h2T])
            halo()
            release(3)

        def stage3(hp):
            hT = hTs[hp]
            b_hT = b_hTs[hp]
            for half in range(2):
                sga, b_sga = acquire(f"ga{half}")
                swa, b_swa = acquire(f"wa{half}")
                sgb, b_sgb = acquire(f"gb{half}")
                swb, b_swb = acquire(f"wb{half}")
                for mm_ in range(4):
                    m = half * 4 + mm_
                    pga, b_pga = proj_fm(sga, b_sga, 8, mm_ * 128, hT, [b_hT])
                    sa, b_sa = T_()
                    P.op("act", lambda E, pga=pga, sa=sa: E.activation(out=sa[:, :], in_=pga[:, :], func=AF.Sigmoid),
                         reads=[b_pga], writes=[b_sa])
                    pya, b_pya = proj_fm(swa, b_swa, 4, mm_ * 128, og, b_og)
                    ma, b_ma = T_()
                    P.op("dve", lambda E, sa=sa, pya=pya, ma=ma: E.tensor_tensor(out=ma[:, :], in0=sa[:, :], in1=pya[:, :], op=ALU.mult),
                         reads=[b_sa, b_pya], writes=[b_ma])
                    pgb, b_pgb = proj_fm(sgb, b_sgb, 8, mm_ * 128, hT, [b_hT])
                    sb_, b_sb = T_()
                    P.op("act", lambda E, pgb=pgb, sb_=sb_: E.activation(out=sb_[:, :], in_=pgb[:, :], func=AF.Sigmoid),
                         reads=[b_pgb], writes=[b_sb])
                    pyb, b_pyb = proj_fm(swb, b_swb, 4, mm_ * 128, yb, b_yb)
                    mb, b_mb = T_()
                    P.op("dve", lambda E, sb_=sb_, pyb=pyb, mb=mb: E.tensor_tensor(out=mb[:, :], in0=sb_[:, :], in1=pyb[:, :], op=ALU.mult),
                         reads=[b_sb, b_pyb], writes=[b_mb])
                    P.op("dve", lambda E, m=m, ma=ma, mb=mb: E.tensor_tensor(out=merged[:, m, :], in0=ma[:, :], in1=mb[:, :], op=ALU.add),
                         reads=[b_ma, b_mb], writes=[b_merged[m]])
                release(4)

        def stage4_preload(row0):
            return {t: load_x(x_main, row0 + t * 128) for t in (0, 1)}

        def stage4(row0, xk):
            so = [acquire("wo0"), acquire("wo1")]
            ks = {}

            def wo(t):
                k = xk[t] if t in xk else load_x(x_main, row0 + t * 128)
                for n in range(2):
                    px1, b_px1 = F_()

                    def mmo(E, n=n, px1=px1):
                        last = None
                        for c in range(8):
                            last = E.matmul(px1[:, :], lhsT=merged[:, c, t * 128:(t + 1) * 128], rhs=so[n][0][:, c, :],
                                            start=(c == 0), stop=(c == 7))
                        return last
                    P.op("pe", mmo, reads=b_merged + [so[n][1]], writes=[b_px1])
                    P.op("dve", lambda E, n=n, px1=px1: E.tensor_tensor(
                        out=x1[:, t, n * 512:(n + 1) * 512], in0=px1[:, :], in1=xt[k][:, n * 512:(n + 1) * 512], op=ALU.add),
                         reads=[b_px1, b_xt[k]], writes=[b_x1[t]])

            def stat_(t):
                ks[t] = norm_stat(x1[:, t, :], [b_x1[t]])

            def xs_(t):
                norm_xs(ks[t], x1[:, t, :], [b_x1[t]])

            def nb(t):
                norm_b(ks[t], G2, h2T, b_h2T, t, extra_w=B_CH)

            wo(0)
            stat_(0)
            wo(1)
            xs_(0)
            stat_(1)
            wo(2)
            xs_(1)
            stat_(2)
            nb(0)
            wo(3)
            release(2)
            xs_(2)
            stat_(3)
            nb(1)
            xs_(3)
            nb(2)
            nb(3)

        def stage5(row0, mid_hook=None):
            if mid_hook is not None and -1 in mid_hook:
                mid_hook[-1]()
            for j in range(6):
                sg_, b_sg = acquire(f"fg{j}")
                su_, b_su = acquire(f"fu{j}")
                for jj in range(4 if j < 5 else 2):
                    jc = j * 4 + jj
                    pg, b_pg = proj_fm(sg_, b_sg, 8, jj * 128, h2T, [b_h2T])
                    sl, b_sl = T_()
                    P.op("act", lambda E, pg=pg, sl=sl: E.activation(out=sl[:, :], in_=pg[:, :], func=AF.Silu),
                         reads=[b_pg], writes=[b_sl])
                    pu, b_pu = proj_fm(su_, b_su, 8, jj * 128, h2T, [b_h2T])
                    P.op("dve", lambda E, jc=jc, sl=sl, pu=pu: E.tensor_tensor(out=act[:, jc, :], in0=sl[:, :], in1=pu[:, :], op=ALU.mult),
                         reads=[b_sl, b_pu], writes=[b_act[jc]] + A_CH)
                release(2)
                if mid_hook is not None and j in mid_hook:
                    mid_hook[j]()
            for n in range(2):
                sd = [acquire(f"fd{n}{k}") for k in range(3)]
                for t in range(NT):
                    pd, b_pd = F_()

                    def mmd(E, t=t, pd=pd, sd=sd):
                        last = None
                        for jc in range(22):
                            last = E.matmul(pd[:, :], lhsT=act[:, jc, t * 128:(t + 1) * 128], rhs=sd[jc // 8][0][:, jc % 8, :],
                                            start=(jc == 0), stop=(jc == 21))
                        return last
                    P.op("pe", mmd, reads=b_act + [s_[1] for s_ in sd], writes=[b_pd])
                    P.op("dve", lambda E, t=t, n=n, pd=pd: E.tensor_tensor(
                        out=x1[:, t, n * 512:(n + 1) * 512], in0=pd[:, :], in1=x1[:, t, n * 512:(n + 1) * 512], op=ALU.add),
                         reads=[b_pd, b_x1[t]], writes=[b_x1[t]])
                    if n == 1:
                        k = cnt["st"] % NXS
                        cnt["st"] += 1
                        P.op("act", lambda E, t=t, k=k: E.activation(out=junk[:, :], in_=x1[:, t, :], func=AF.Square,
                                                                    accum_out=stat[k][:, 2:3]),
                             reads=[b_x1[t]], writes=[b_stat[k]])
                        P.op("act", lambda E, k=k: E.activation(out=stat[k][:, 3:4], in_=stat[k][:, 2:3], func=AF.Ln, scale=1.0 / D,
                                                                bias=cst[:, EPS:EPS + 1]),
                             reads=[b_stat[k], b_cst], writes=[b_stat[k]])
                        P.op("act", lambda E, k=k: E.activation(out=stat[k][:, 3:4], in_=stat[k][:, 3:4], func=AF.Exp, scale=-0.5),
                             reads=[b_stat[k]], writes=[b_stat[k]])
                        P.op("dve", lambda E, t=t, k=k: E.scalar_tensor_tensor(
                            out=x1[:, t, :], in0=x1[:, t, :], scalar=stat[k][:, 3:4], in1=cst[:, GF:GF + D],
                            op0=ALU.mult, op1=ALU.mult),
                             reads=[b_x1[t], b_stat[k], b_cst], writes=[b_x1[t]])
                        P.dma("sp", lambda E, t=t: E.dma_start(out=out_d[row0 + t * 128:row0 + (t + 1) * 128, :], in_=x1[:, t, :]),
                              reads=[b_x1[t]], chan=b_x1[t])
                release(3)

        blocks = [("pre", pb) for pb in reversed(range(NPRE))] + [("main", mb) for mb in range(NMAIN)]

        def src_of(bi):
            kind, idx = blocks[bi]
            return (x_prev if kind == "pre" else x_main), idx * T

        s_, r_ = src_of(0)
        cnt["fbmod"] = 5
        ks0 = stage0_a(s_, r_)
        for _ in range(R):
            issue_next()
        stage0_b(ks0, 0)
        for bi, (kind, idx) in enumerate(blocks):
            hp = bi % 2
            nxt = {"xk": {}, "ks": {}}
            has_next = bi + 1 < len(blocks)

            def pre_load(tiles, bi=bi, nxt=nxt, has_next=has_next):
                if has_next:
                    s2, r2 = src_of(bi + 1)
                    stage0_load(s2, r2, tiles, nxt)

            def pre_norm(tiles, nxt=nxt, has_next=has_next):
                if has_next:
                    stage0_norm(tiles, nxt)

            def pre_b(bi=bi, nxt=nxt, has_next=has_next):
                if has_next:
                    stage0_b([nxt["ks"][t] for t in range(NT)], (bi + 1) % 2)

            def h_start(pre_load=pre_load):
                pre_load([0, 1])

            def h_mid1(pre_load=pre_load, pre_norm=pre_norm):
                pre_norm([0, 1])
                pre_load([2, 3])

            def h_mid2(pre_norm=pre_norm):
                pre_norm([2, 3])

            if kind == "pre":
                stage1_pre(hp, first=(bi == 0), hooks={"start": h_start, "mid1": h_mid1, "mid2": h_mid2, "end": pre_b})
                if bi == 0:
                    conv_u(hp)
                    halo()
                    release(2)
                precache(NPC)
                if bi == NPRE - 1:
                    prefix_finish()
                    cnt["fbmod"] = 6
            else:
                stage1(bi % 2, hp, True)
                core(bi % 2, True, False)
                stage2(hp)
                xk4 = stage4_preload(idx * T)
                stage3(hp)
                stage4(idx * T, xk4)
                stage5(idx * T, mid_hook={-1: h_start, 0: h_mid1, 2: h_mid2, 4: pre_b})
        P.final_wait("sp", b_x1)
        P.emit(block)
    return nc


DEBUG = False
_NC_CACHE = {}


def _get_nc():
    if "nc" not in _NC_CACHE:
        _NC_CACHE["nc"] = build_nc()
    return _NC_CACHE["nc"]


def kernel(x, norm_mix_g, w_in, lower_bounds, hg_norm_g, conv_w, w_branch_a, w_branch_b,
           w_out, norm_ffn_g, w_ffn_gate, w_ffn_up, w_ffn_down, norm_final_g):
    f32 = np.float32
    x = np.asarray(x, f32)
    cst = np.zeros((128, NCST), f32)
    cst[:, G1:G1 + 8] = np.asarray(norm_mix_g, f32)[0].reshape(8, 128).T
    cst[:, G2:G2 + 8] = np.asarray(norm_ffn_g, f32)[0].reshape(8, 128).T
    lbs = np.asarray(lower_bounds, f32)
    cst[:, L0:L0 + 4] = lbs[0].reshape(4, 128).T
    cst[:, L1:L1 + 4] = lbs[1].reshape(4, 128).T
    cst[:, HGN] = np.asarray(hg_norm_g, f32)[0]
    cw = np.asarray(conv_w, f32)[0]
    for k in range(3):
        cst[:, CW + k * 4:CW + k * 4 + 4] = cw[k].reshape(4, 128).T
    cst[:, EPS] = 1e-6
    cst[:, ONE] = 1.0
    cst[:, GF:GF + D] = np.asarray(norm_final_g, f32)[None, :]
    weights = {
        "w_in": np.ascontiguousarray(np.asarray(w_in, f32)[0]),
        "w_a": np.ascontiguousarray(np.asarray(w_branch_a, f32)[0]),
        "w_b": np.ascontiguousarray(np.asarray(w_branch_b, f32)[0]),
        "w_out": np.ascontiguousarray(np.asarray(w_out, f32)[0]),
        "w_g": np.ascontiguousarray(np.asarray(w_ffn_gate, f32)[0]),
        "w_u": np.ascontiguousarray(np.asarray(w_ffn_up, f32)[0]),
        "w_d": np.ascontiguousarray(np.asarray(w_ffn_down, f32)[0]),
    }
    in_maps = []
    zeros = np.zeros((NPRE * T, D), f32)
    for i in range(8):
        b, half = i // 2, i % 2
        m = {"x_main": np.ascontiguousarray(x[b, half * TOK:(half + 1) * TOK]),
             "x_prev": np.ascontiguousarray(x[b, 0:TOK]) if half == 1 else zeros,
             "cst": cst}
        m.update(weights)
        in_maps.append(m)
    nc = _get_nc()
    res = run_bass_kernel_spmd(nc, in_maps, core_ids=list(range(8)))
    out = np.empty((4, 2 * TOK, D), f32)
    for i in range(8):
        b, half = i // 2, i % 2
        out[b, half * TOK:(half + 1) * TOK] = res.results[i]["out"]
    return out
```

```python
import numpy as np
from contextlib import ExitStack
import concourse.bass as bass
import concourse.mybir as mybir
from concourse.bass_utils import run_bass_kernel_spmd

F32 = mybir.dt.float32
BF16 = mybir.dt.bfloat16
ALU = mybir.AluOpType
AF = mybir.ActivationFunctionType

ENGS = ("pe", "act", "dve", "pool", "sp")

D = 1024
NIN = 5632
DFF = 2816
T = 512
NT = T // 128
CH = 64
NCH = T // CH
NMAIN = 4
NPRE = 4
TOK = NMAIN * T
R = 8
NTMP = 14
NXS = 4

G1, G2, L0, L1, HGN, CW, EPS, ONE, GF = 0, 8, 16, 20, 24, 25, 37, 38, 39
NCST = GF + D


class Buf:
    __slots__ = ("name", "w", "r", "dsem")

    def __init__(self, name):
        self.name = name
        self.w = None
        self.r = {}
        self.dsem = None


class Prog:
    def __init__(self, nc, stack):
        self.nc = nc
        self.stack = stack
        self.streams = {e: [] for e in ENGS}
        self.sems = {}
        self.count = {}
        self.known = {e: {} for e in ENGS}
        for e in ENGS:
            self._newsem("P_" + e)

    def _newsem(self, key):
        h = self.stack.enter_context(self.nc.semaphore(key))
        self.sems[key] = h
        self.count[key] = 0
        return h

    def _deps(self, eng, reads, writes):
        deps = {}
        for b in reads:
            if b.w is not None:
                k, v = b.w
                if deps.get(k, 0) < v:
                    deps[k] = v
        for b in writes:
            if b.w is not None:
                k, v = b.w
                if deps.get(k, 0) < v:
                    deps[k] = v
            for k, v in b.r.items():
                if deps.get(k, 0) < v:
                    deps[k] = v
        own = "P_" + eng
        kn = self.known[eng]
        for k, v in deps.items():
            if k == own and eng == "pe":
                continue
            if kn.get(k, 0) >= v:
                continue
            kn[k] = v
            self.streams[eng].append(("wait", self.sems[k], v))

    def op(self, eng, fn, reads=(), writes=()):
        self._deps(eng, reads, writes)
        key = "P_" + eng
        self.count[key] += 1
        t = self.count[key]
        self.streams[eng].append(("inc", fn, self.sems[key]))
        for b in reads:
            if b.r.get(key, 0) < t:
                b.r[key] = t
        for b in writes:
            b.w = (key, t)
            b.r = {}
        return t

    def dma(self, eng, fn, reads=(), writes=(), chan=None):
        self._deps(eng, reads, writes)
        if chan.dsem is None:
            chan.dsem = "D_" + chan.name
            self._newsem(chan.dsem)
        key = chan.dsem
        self.count[key] += 16
        t = self.count[key]
        self.streams[eng].append(("dma", fn, self.sems[key]))
        for b in reads:
            if b.r.get(key, 0) < t:
                b.r[key] = t
        for b in writes:
            b.w = (key, t)
            b.r = {}
        return t

    def final_wait(self, eng, bufs):
        self._deps(eng, (), bufs)

    def emit(self, block):
        def run(E, stream):
            for item in stream:
                kind = item[0]
                if kind == "wait":
                    E.wait_ge(item[1], item[2])
                elif kind == "inc":
                    item[1](E).then_inc(item[2], 1)
                else:
                    item[1](E).then_inc(item[2], 16)

        @block.tensor
        def _(E):
            run(E, self.streams["pe"])

        @block.scalar
        def _(E):
            run(E, self.streams["act"])

        @block.vector
        def _(E):
            run(E, self.streams["dve"])

        @block.gpsimd
        def _(E):
            run(E, self.streams["pool"])

        @block.sync
        def _(E):
            run(E, self.streams["sp"])


def slab_specs():
    sp = {}
    names = ["q", "f", "i", "g", "c", "bg", "xb", "ga0", "ga1", "gb0", "gb1"]
    for n, nm in enumerate(names):
        sp[nm] = ("w_in", 0, 8, n * 512, 512)
    for n in range(2):
        sp[f"wa{n}"] = ("w_a", 0, 4, n * 512, 512)
        sp[f"wb{n}"] = ("w_b", 0, 4, n * 512, 512)
        sp[f"wo{n}"] = ("w_out", 0, 8, n * 512, 512)
    for j in range(6):
        nc_ = 512 if j < 5 else 256
        sp[f"fg{j}"] = ("w_g", 0, 8, j * 512, nc_)
        sp[f"fu{j}"] = ("w_u", 0, 8, j * 512, nc_)
    for n in range(2):
        for k in range(3):
            kc = 8 if k < 2 else 6
            sp[f"fd{n}{k}"] = ("w_d", k * 1024, kc, n * 512, 512)
    return sp


MAIN_ORDER = (["f", "i", "q", "g", "c", "xb", "bg",
               "ga0", "wa0", "gb0", "wb0", "ga1", "wa1", "gb1", "wb1", "wo0", "wo1"]
              + [x for j in range(6) for x in (f"fg{j}", f"fu{j}")]
              + [f"fd{n}{k}" for n in range(2) for k in range(3)])


def build_nc():
    nc = bass.Bass("TRN2", target_bir_lowering=False)
    x_main = nc.dram_tensor("x_main", [TOK, D], F32, kind="ExternalInput").ap()
    x_prev = nc.dram_tensor("x_prev", [NPRE * T, D], F32, kind="ExternalInput").ap()
    cst_d = nc.dram_tensor("cst", [128, NCST], F32, kind="ExternalInput").ap()
    W = {
        "w_in": nc.dram_tensor("w_in", [D, NIN], F32, kind="ExternalInput").ap(),
        "w_a": nc.dram_tensor("w_a", [512, D], F32, kind="ExternalInput").ap(),
        "w_b": nc.dram_tensor("w_b", [512, D], F32, kind="ExternalInput").ap(),
        "w_out": nc.dram_tensor("w_out", [D, D], F32, kind="ExternalInput").ap(),
        "w_g": nc.dram_tensor("w_g", [D, DFF], F32, kind="ExternalInput").ap(),
        "w_u": nc.dram_tensor("w_u", [D, DFF], F32, kind="ExternalInput").ap(),
        "w_d": nc.dram_tensor("w_d", [DFF, D], F32, kind="ExternalInput").ap(),
    }
    out_d = nc.dram_tensor("out", [TOK, D], F32, kind="ExternalOutput").ap()
    dbg_out = {}
    if DEBUG:
        for nm, shp, dt in [("d_qin", [128, 4 * T], BF16), ("d_khat", [128, 4 * T], BF16), ("d_v", [128, NT * 512], BF16),
                            ("d_og", [128, 4 * T], BF16), ("d_gsil", [128, 4 * T], BF16), ("d_kout", [128, NT * 512], BF16),
                            ("d_hT", [128, 8 * T], BF16), ("d_S", [128, 2 * 4 * 128], F32), ("d_yb", [128, 4 * T], BF16),
                            ("d_merged", [128, 8 * T], BF16)]:
            dbg_out[nm] = nc.dram_tensor(nm, shp, dt, kind="ExternalOutput").ap()
    SPECS = slab_specs()
    cache_names = list(SPECS.keys())
    wcache = nc.dram_tensor("wcache", [len(cache_names), 128, 8 * 512], BF16).ap()
    cache_idx = {n: i for i, n in enumerate(cache_names)}

    with ExitStack() as st:
        def sb(name, shape, dt):
            return st.enter_context(nc.sbuf_tensor(name, shape, dt))

        def ps(name, shape, dt):
            return st.enter_context(nc.psum_tensor(name, shape, dt))

        cst = sb("cst_sb", [128, NCST], F32)
        identf = sb("identf", [128, 128], F32)
        ident = sb("ident", [128, 128], BF16)
        ones_f = sb("ones_f", [128, 128], F32)
        maskT = sb("maskT", [128, 128], F32)
        smask = sb("smask", [128, T], F32)
        diagw = sb("diagw", [128, 12, 128], BF16)
        sv = sb("sv", [128, 4, 4], F32)
        S = sb("S", [128, 2, 4, 128], F32)
        Sb = sb("Sb", [128, 2, 4, 128], BF16)
        decs = sb("decs", [128, 2, 4, NCH], F32)
        xt = [sb(f"xt{i}", [128, D], F32) for i in range(2)]
        xs = [sb(f"xs{i}", [128, D], BF16) for i in range(NXS)]
        stat = [sb(f"stat{i}", [128, 4], F32) for i in range(NXS)]
        junk = sb("junk", [128, 2 * T], BF16)
        omask = sb("omask", [128, T], F32)
        gdec = sb("gdec", [128, 4, 4], F32)
        arenaA = sb("arenaA", [128, 22 * T], BF16)
        act = arenaA[:, :].rearrange("p (j t) -> p j t", t=T)
        qin = arenaA[:, 0:4 * T].rearrange("p (h t) -> p h t", t=T)
        khat = arenaA[:, 4 * T:8 * T].rearrange("p (h t) -> p h t", t=T)
        kout = arenaA[:, 8 * T:12 * T].rearrange("p (t d) -> p t d", d=512)
        v = arenaA[:, 12 * T:16 * T].rearrange("p (t d) -> p t d", d=512)
        gsil = arenaA[:, 16 * T:20 * T].rearrange("p (h t) -> p h t", t=T)
        koutT = [arenaA[:, 20 * T:21 * T], arenaA[:, 21 * T:22 * T]]
        arenaB = sb("arenaB", [128, 8 * T], BF16)
        h2T = arenaB[:, :].rearrange("p (c t) -> p c t", t=T)
        og = arenaB[:, 0:4 * T].rearrange("p (h t) -> p h t", t=T)
        yb = arenaB[:, 4 * T:8 * T].rearrange("p (h t) -> p h t", t=T)
        hTs = [sb(f"hT{i}", [128, 8, T], BF16) for i in range(2)]
        u = sb("u", [128, 4, T + 2], BF16)
        merged = sb("merged", [128, 8, T], BF16)
        x1 = sb("x1", [128, NT, D], F32)
        AT = [sb(f"AT{i}", [128, 128], BF16) for i in range(4)]
        tmp = [sb(f"tmp{i}", [128, T], F32) for i in range(NTMP)]
        ring = [sb(f"ring{i}", [128, 8, 512], BF16) for i in range(R)]
        FB = [ps(f"pf{i}", [128, 512], F32) for i in range(6)]
        TBs = [ps(f"ptb{i}", [128, 1024], BF16) for i in range(2)]

        block = st.enter_context(nc.Block())
        P = Prog(nc, st)

        b_cst, b_ident, b_identf, b_ones, b_maskT, b_smask, b_diagw, b_sv = (Buf(n) for n in
            ["cst", "ident", "identf", "ones", "maskT", "smask", "diagw", "sv"])
        b_S = [[Buf(f"S{p}{h}") for h in range(4)] for p in range(2)]
        b_Sb = [[Buf(f"Sb{p}{h}") for h in range(4)] for p in range(2)]
        b_decs = [[Buf(f"decs{p}{h}") for h in range(4)] for p in range(2)]
        b_xt = [Buf(f"xt{i}") for i in range(2)]
        b_xs = [Buf(f"xs{i}") for i in range(NXS)]
        b_stat = [Buf(f"stat{i}") for i in range(NXS)]
        b_act = [Buf(f"act{j}") for j in range(22)]
        b_hTs = [Buf("hT0"), Buf("hT1")]
        b_qin = [Buf(f"qin{h}") for h in range(4)]
        b_khat = [Buf(f"khat{h}") for h in range(4)]
        b_kout = [Buf(f"kout{h}") for h in range(4)]
        b_koutT = [Buf(f"koutT{i}") for i in range(2)]
        b_v = [Buf(f"v{t}") for t in range(NT)]
        b_gsil = [Buf(f"gsil{h}") for h in range(4)]
        b_og = [Buf(f"og{h}") for h in range(4)]
        b_u = Buf("u")
        b_omask = Buf("omask")
        b_gdecs = [Buf(f"gdec{h}") for h in range(4)]
        b_yb = [Buf(f"yb{m}") for m in range(4)]
        b_merged = [Buf(f"merged{m}") for m in range(8)]
        b_x1 = [Buf(f"x1{t}") for t in range(NT)]
        b_h2T = Buf("h2T")
        b_AT = [Buf(f"AT{i}") for i in range(4)]
        b_tmp = [Buf(f"tmp{i}") for i in range(NTMP)]
        b_ring = [Buf(f"ring{i}") for i in range(R)]
        b_FB = [Buf(f"FB{i}") for i in range(6)]
        b_pS = [[Buf(f"pS{k}{h}") for h in range(4)] for k in range(2)]
        b_TBs = [Buf("TB0"), Buf("TB1")]
        b_cache = {n: Buf("wc_" + n) for n in cache_names}
        A_CH = b_qin + b_khat + b_kout + b_koutT + b_v + b_gsil
        B_CH = b_og + b_yb

        cnt = {"tmp": 0, "fb": 0, "xt": 0, "at": 0, "kt": 0, "st": 0, "os": 0, "tb": 0, "fbmod": 6}

        def TB_():
            i = cnt["tb"] % 2
            cnt["tb"] += 1
            return TBs[i], b_TBs[i]

        def T_():
            i = cnt["tmp"] % NTMP
            cnt["tmp"] += 1
            return tmp[i], b_tmp[i]

        def F_():
            i = cnt["fb"] % cnt["fbmod"]
            cnt["fb"] += 1
            return FB[i], b_FB[i]

        NPC = 4
        pc_list = [n for n in MAIN_ORDER if n not in ("f", "i", "c", "xb")][:NPC * NPRE]
        sched = []
        for pb in range(NPRE):
            sched += ["f", "i"]
            if pb == 0:
                sched += ["c", "xb"]
            sched += ["PC:" + n for n in pc_list[pb * NPC:(pb + 1) * NPC]]
        for mb in range(NMAIN):
            sched += MAIN_ORDER
        ring_state = {"issued": 0, "acq": 0}
        cached = set()
        ch_sw = [Buf(f"rsw{i}") for i in range(R)]
        ch_hw = [Buf(f"rhw{i}") for i in range(R)]

        pending = {"store": None}

        def flush_store():
            if pending["store"] is not None:
                fn_, reads_, writes_, chan_ = pending["store"]
                P.dma("pool", fn_, reads=reads_, writes=writes_, chan=chan_)
                pending["store"] = None

        def issue_next():
            g = ring_state["issued"]
            if g >= len(sched):
                flush_store()
                return
            ring_state["issued"] += 1
            name = sched[g]
            if name.startswith("PC:"):
                name = name[3:]
            s = g % R
            wkey, r0, kc, c0, ncols = SPECS[name]
            dst = ring[s][:, 0:kc, 0:ncols]
            ci = cache_idx[name]
            cview = wcache[ci].rearrange("p (c n) -> p c n", n=512)[:, 0:kc, 0:ncols]
            if name in cached:
                flush_store()
                P.dma("sp", lambda E: E.dma_start(out=dst, in_=cview), reads=[b_cache[name]],
                      writes=[b_ring[s]], chan=ch_hw[s])
            else:
                src = W[wkey][r0:r0 + kc * 128, :].rearrange("(c p) n -> p c n", p=128)[:, :, c0:c0 + ncols]
                P.dma("pool", lambda E: E.dma_start(out=dst, in_=src), writes=[b_ring[s]], chan=ch_sw[s])
                cached.add(name)
                flush_store()
                pending["store"] = (lambda E: E.dma_start(out=cview, in_=dst), [b_ring[s]], [b_cache[name]], ch_sw[s])

        def acquire(name):
            g = ring_state["acq"]
            assert sched[g] == name, (g, sched[g], name)
            ring_state["acq"] += 1
            assert g < ring_state["issued"], "ring underflow"
            return ring[g % R], b_ring[g % R]

        def precache(n):
            for _ in range(n):
                g = ring_state["acq"]
                assert sched[g].startswith("PC:"), (g, sched[g])
                ring_state["acq"] += 1
                issue_next()

        def release(n=1):
            for _ in range(n):
                issue_next()

        P.dma("sp", lambda E: E.dma_start(out=cst[:], in_=cst_d), writes=[b_cst], chan=b_cst)
        P.op("dve", lambda E: E.memset(identf[:], 0.0), writes=[b_identf])
        P.op("pool", lambda E: E.affine_select(out=identf[:], in_=identf[:], pattern=[[-1, 128]],
                                               compare_op=ALU.not_equal, fill=1.0, base=0, channel_multiplier=1),
             reads=[b_identf], writes=[b_identf])
        P.op("dve", lambda E: E.tensor_copy(out=ident[:], in_=identf[:]), reads=[b_identf], writes=[b_ident])
        P.op("dve", lambda E: E.memset(ones_f[:], 1.0), writes=[b_ones])
        P.op("dve", lambda E: E.memset(maskT[:], 1.0), writes=[b_maskT])
        P.op("pool", lambda E: E.affine_select(out=maskT[:], in_=maskT[:], pattern=[[1, 128]],
                                               compare_op=ALU.is_ge, fill=0.0, base=0, channel_multiplier=-1),
             reads=[b_maskT], writes=[b_maskT])
        P.op("dve", lambda E: E.memset(maskT[0:64, 64:128], 0.0), reads=[b_maskT], writes=[b_maskT])
        P.op("dve", lambda E: E.memset(smask[:], 1.0), writes=[b_smask])
        P.op("dve", lambda E: E.memset(smask[:].rearrange("p (c j) -> p c j", j=CH)[:, :, 0:1], 0.0),
             reads=[b_smask], writes=[b_smask])
        for k in range(12):
            P.op("dve", lambda E, k=k: E.tensor_scalar(out=diagw[:, k, :], in0=identf[:], scalar1=cst[:, CW + k:CW + k + 1],
                                                       scalar2=0.0, op0=ALU.mult, op1=ALU.add),
                 reads=[b_identf, b_cst], writes=[b_diagw])
        P.op("dve", lambda E: E.tensor_tensor(out=sv[:, :, 3], in0=cst[:, L0:L0 + 4], in1=cst[:, L1:L1 + 4], op=ALU.subtract),
             reads=[b_cst], writes=[b_sv])
        P.op("act", lambda E: E.activation(out=sv[:, :, 1], in_=sv[:, :, 3], func=AF.Sigmoid), reads=[b_sv], writes=[b_sv])
        P.op("act", lambda E: E.activation(out=sv[:, :, 0], in_=sv[:, :, 3], func=AF.Sigmoid, scale=-1.0),
             reads=[b_sv], writes=[b_sv])
        P.op("dve", lambda E: E.tensor_scalar(out=sv[:, :, 2], in0=sv[:, :, 0], scalar1=-1.0, scalar2=0.0, op0=ALU.mult, op1=ALU.add),
             reads=[b_sv], writes=[b_sv])
        P.op("dve", lambda E: E.memset(omask[:], 1.0), writes=[b_omask])
        P.op("dve", lambda E: E.memset(gdec[:], 1.0), writes=b_gdecs)
        P.op("dve", lambda E: E.memset(S[:, 0, :, :], 0.0), writes=b_S[0])
        P.op("dve", lambda E: E.memset(Sb[:, 0, :, :], 0.0), writes=b_Sb[0])
        P.op("dve", lambda E: E.memset(u[:], 0.0), writes=[b_u])
        chain = {"step": 0}

        def norm_stat(src_tile_ap, src_reads):
            k = cnt["st"] % NXS
            cnt["st"] += 1
            P.op("act", lambda E: E.activation(out=junk[:, :], in_=src_tile_ap, func=AF.Square, accum_out=stat[k][:, 2:3]),
                 reads=src_reads, writes=[b_stat[k]])
            P.op("act", lambda E: E.activation(out=stat[k][:, 3:4], in_=stat[k][:, 2:3], func=AF.Ln, scale=1.0 / D,
                                               bias=cst[:, EPS:EPS + 1]),
                 reads=[b_stat[k], b_cst], writes=[b_stat[k]])
            P.op("act", lambda E: E.activation(out=stat[k][:, 3:4], in_=stat[k][:, 3:4], func=AF.Exp, scale=-0.5),
                 reads=[b_stat[k]], writes=[b_stat[k]])
            return k

        def norm_xs(k, src_tile_ap, src_reads):
            P.op("dve", lambda E: E.tensor_scalar(out=xs[k][:], in0=src_tile_ap, scalar1=stat[k][:, 3:4], scalar2=0.0,
                                                  op0=ALU.mult, op1=ALU.add),
                 reads=list(src_reads) + [b_stat[k]], writes=[b_xs[k]])

        def norm_a(src_tile_ap, src_reads):
            k = norm_stat(src_tile_ap, src_reads)
            norm_xs(k, src_tile_ap, src_reads)
            return k

        def norm_b(k, gcol, dstT, b_dst, t, extra_w=()):
            TB, b_TB = TB_()

            def tr(E):
                last = None
                for c in range(8):
                    last = E.transpose(out=TB[:, c * 128:(c + 1) * 128], in_=xs[k][:, c * 128:(c + 1) * 128], identity=ident[:])
                return last
            P.op("pe", tr, reads=[b_xs[k], b_ident], writes=[b_TB])
            P.op("dve", lambda E: E.tensor_tensor(
                out=dstT[:, :, t * 128:(t + 1) * 128],
                in0=TB[:].rearrange("p (c t) -> p c t", t=128),
                in1=cst[:, gcol:gcol + 8].rearrange("p (c o) -> p c o", o=1).to_broadcast([128, 8, 128]),
                op=ALU.mult), reads=[b_TB, b_cst], writes=[b_dst] + list(extra_w))

        def load_x(src, r0):
            k = cnt["xt"] % 2
            cnt["xt"] += 1
            P.dma("sp", lambda E: E.dma_start(out=xt[k][:], in_=src[r0:r0 + 128, :]), writes=[b_xt[k]], chan=b_xt[k])
            return k

        def proj_fm(slab, b_slab, kc, col0, rhsT, b_rhs):
            pf, b_pf = F_()

            def mm(E):
                last = None
                for c in range(kc):
                    last = E.matmul(pf[:, :], lhsT=slab[:, c, col0:col0 + 128], rhs=rhsT[:, c, :],
                                    start=(c == 0), stop=(c == kc - 1))
                return last
            P.op("pe", mm, reads=[b_slab] + list(b_rhs), writes=[b_pf])
            return pf, b_pf

        def stage0_load(src, row0, tiles, st):
            for t in tiles:
                st["xk"][t] = load_x(src, row0 + t * 128)

        def stage0_norm(tiles, st):
            for t in tiles:
                k = st["xk"][t]
                st["ks"][t] = norm_a(xt[k][:, :], [b_xt[k]])

        def stage0_a(src, row0):
            st = {"xk": {}, "ks": {}}
            stage0_load(src, row0, [0, 1], st)
            stage0_norm([0, 1], st)
            stage0_load(src, row0, [2, 3], st)
            stage0_norm([2, 3], st)
            return [st["ks"][t] for t in range(NT)]

        def stage0_b(ks, hp):
            for t in range(NT):
                norm_b(ks[t], G1, hTs[hp], b_hTs[hp], t)

        def lockstep(*lists):
            n = max(len(l) for l in lists)
            for i in range(n):
                for l in lists:
                    if i < len(l):
                        l[i]()

        def make_vtile(hT, b_hT, si, b_si, on_dve=False):
            def vtile(t):
                pv, b_pv = F_()

                def mmv(E):
                    last = None
                    for c in range(8):
                        last = E.matmul(pv[:, :], lhsT=hT[:, c, t * 128:(t + 1) * 128], rhs=si[:, c, :],
                                        start=(c == 0), stop=(c == 7))
                    return last
                P.op("pe", mmv, reads=[b_si, b_hT], writes=[b_pv])
                if on_dve:
                    P.op("dve", lambda E: E.tensor_copy(out=v[:, t, :], in_=pv[:, :]), reads=[b_pv], writes=[b_v[t]] + b_act)
                else:
                    P.op("act", lambda E: E.activation(out=v[:, t, :], in_=pv[:, :], func=AF.Copy),
                         reads=[b_pv], writes=[b_v[t]] + b_act)
            return vtile

        def gate_p1(h, pf, b_pf, r_e, rb_e, r_sg, rb_sg, r_km, rb_km, r_lf, rb_lf):
            return [
                lambda: P.op("act", lambda E: E.activation(out=r_e[:, :], in_=pf[:, :], func=AF.Exp, scale=-1.0),
                             reads=[b_pf], writes=[rb_e]),
                lambda: P.op("act", lambda E: E.activation(out=r_e[:, :], in_=r_e[:, :], func=AF.Ln, bias=cst[:, ONE:ONE + 1]),
                             reads=[rb_e, b_cst], writes=[rb_e]),
                lambda: P.op("act", lambda E: E.activation(out=r_sg[:, :], in_=r_e[:, :], func=AF.Exp, scale=-1.0),
                             reads=[rb_e], writes=[rb_sg]),
                lambda: P.op("act", lambda E: E.activation(out=r_km[:, :], in_=r_sg[:, :], func=AF.Identity, scale=sv[:, h, 2:3],
                                                           bias=sv[:, h, 0:1]),
                             reads=[rb_sg, b_sv], writes=[rb_km]),
                lambda: P.op("act", lambda E: E.activation(out=r_lf[:, :], in_=r_sg[:, :], func=AF.Ln, scale=sv[:, h, 0:1],
                                                           bias=sv[:, h, 1:2]),
                             reads=[rb_sg, b_sv], writes=[rb_lf]),
            ]

        def kout_tail(h, kk):
            hc = h * 128
            st_ = {}

            def t1():
                TB, b_TB = TB_()
                st_["tb"] = (TB, b_TB)

                def trk(E):
                    last = None
                    for t in range(NT):
                        last = E.transpose(out=TB[:, t * 128:(t + 1) * 128], in_=koutT[kk][:, t * 128:(t + 1) * 128],
                                           identity=ident[:])
                    return last
                P.op("pe", trk, reads=[b_koutT[kk], b_ident], writes=[b_TB])

            def t2():
                TB, b_TB = st_["tb"]
                P.op("act", lambda E: E.activation(out=kout[:, :, hc:hc + 128],
                                                   in_=TB[:, 0:512].rearrange("p (t d) -> p t d", d=128), func=AF.Copy),
                     reads=[b_TB], writes=[b_kout[h]] + b_act)
            return [t1, t2]

        def stage1(blk_par, hp, main, hooks=None):
            hT = hTs[hp]
            b_hT = b_hTs[hp]
            sf, b_sf = acquire("f")
            si, b_si = acquire("i")
            vtile = make_vtile(hT, b_hT, si, b_si)
            r_e, rb_e = [tmp[0], tmp[1]], [b_tmp[0], b_tmp[1]]
            r_sg, rb_sg = [tmp[2], tmp[3]], [b_tmp[2], b_tmp[3]]
            r_km, rb_km = [tmp[4], tmp[5]], [b_tmp[4], b_tmp[5]]
            r_lf, rb_lf = [tmp[6], tmp[7]], [b_tmp[6], b_tmp[7]]
            r_ee, rb_ee = tmp[8:12], b_tmp[8:12]
            pfs = [proj_fm(sf, b_sf, 8, h * 128, hT, [b_hT]) for h in range(4)]

            def p1(h):
                p = h % 2
                return gate_p1(h, pfs[h][0], pfs[h][1], r_e[p], rb_e[p], r_sg[p], rb_sg[p], r_km[p], rb_km[p], r_lf[p], rb_lf[p])

            def p2(h):
                p = h % 2
                bb, b_bb = r_e[p], rb_e[p]
                ei, b_ei = r_sg[p], rb_sg[p]
                ee, b_ee = r_ee[h], rb_ee[h]
                kk = p
                ops = [
                    lambda: P.op("dve", lambda E: E.tensor_tensor_scan(out=bb[:, :], data0=smask[:, :], data1=r_lf[p][:, :], initial=0.0,
                                                                       op0=ALU.mult, op1=ALU.add),
                                 reads=[b_smask, rb_lf[p]], writes=[b_bb]),
                    lambda: P.op("act", lambda E: E.activation(out=ee[:, :], in_=bb[:, :], func=AF.Exp), reads=[b_bb], writes=[b_ee]),
                    lambda: P.op("act", lambda E: E.activation(out=ei[:, :], in_=bb[:, :], func=AF.Exp, scale=-1.0),
                                 reads=[b_bb], writes=[b_ei]),
                    lambda: P.op("dve", lambda E: E.tensor_tensor(out=khat[:, h, :], in0=r_km[p][:, :], in1=ei[:, :], op=ALU.mult),
                                 reads=[rb_km[p], b_ei], writes=[b_khat[h]] + b_act),
                    lambda: P.op("dve", lambda E: E.tensor_copy(
                        out=decs[:, blk_par, h, :],
                        in_=ee[:, :].rearrange("p (c j) -> p c j", j=CH)[:, :, CH - 1]),
                        reads=[b_ee], writes=[b_decs[blk_par][h]]),
                    lambda: P.op("dve", lambda E: E.tensor_tensor(
                        out=koutT[kk].rearrange("p (c j) -> p c j", j=CH),
                        in0=khat[:, h, :].rearrange("p (c j) -> p c j", j=CH),
                        in1=decs[:, blk_par, h, :].rearrange("p (c o) -> p c o", o=1).to_broadcast([128, NCH, CH]),
                        op=ALU.mult), reads=[b_khat[h], b_decs[blk_par][h]], writes=[b_koutT[kk]] + b_act),
                ]
                return ops

            hooks = hooks or {}
            hk = lambda name: hooks[name]() if name in hooks else None
            hk("start")

            def tails(h):
                for th in kout_tail(h, h % 2):
                    th()

            lockstep(p1(0), p1(1))
            vtile(0)
            lockstep(p2(0), p2(1))
            hk("mid1")
            lockstep(p1(2), p1(3))
            vtile(1)
            vtile(2)
            tails(0)
            tails(1)
            hk("mid2")
            lockstep(p2(2), p2(3))
            vtile(3)
            release(2)
            hk("end")
            sq_, b_sq = acquire("q")
            sg_, b_sg = acquire("g")

            def QG(h):
                hc = h * 128
                qs, b_qs = (tmp[0], b_tmp[0]) if h % 2 == 0 else (tmp[1], b_tmp[1])
                pq, b_pq = proj_fm(sq_, b_sq, 8, hc, hT, [b_hT])
                P.op("act", lambda E: E.activation(out=qs[:, :], in_=pq[:, :], func=AF.Silu), reads=[b_pq], writes=[b_qs])
                pg, b_pg = proj_fm(sg_, b_sg, 8, hc, hT, [b_hT])
                P.op("act", lambda E: E.activation(out=gsil[:, h, :], in_=pg[:, :], func=AF.Silu),
                     reads=[b_pg], writes=[b_gsil[h]] + b_act)
                P.op("dve", lambda E: E.scalar_tensor_tensor(out=qin[:, h, :], in0=qs[:, :], scalar=float(128 ** -0.5),
                                                             in1=r_ee[h][:, :], op0=ALU.mult, op1=ALU.mult),
                     reads=[b_qs, rb_ee[h]], writes=[b_qin[h]] + b_act)
            for h in range(4):
                QG(h)
            tails(2)
            tails(3)
            release(2)

        def stage1_pre(hp, first, hooks=None):
            hT = hTs[hp]
            b_hT = b_hTs[hp]
            sf, b_sf = acquire("f")
            si, b_si = acquire("i")
            vtile = make_vtile(hT, b_hT, si, b_si, on_dve=True)
            ie, ikm, ilf = [0, 1, 8, 9], [4, 5, 10, 11], [6, 7, 12, 13]
            r_e, rb_e = [tmp[i] for i in ie], [b_tmp[i] for i in ie]
            r_sg, rb_sg = [tmp[2], tmp[3]], [b_tmp[2], b_tmp[3]]
            r_km, rb_km = [tmp[i] for i in ikm], [b_tmp[i] for i in ikm]
            r_lf, rb_lf = [tmp[i] for i in ilf], [b_tmp[i] for i in ilf]
            pS_, b_pS_ = FB[5], b_FB[5]
            pfs = [proj_fm(sf, b_sf, 8, h * 128, hT, [b_hT]) for h in range(4)]

            def p1(h):
                p = h % 2
                return gate_p1(h, pfs[h][0], pfs[h][1], r_e[h], rb_e[h], r_sg[p], rb_sg[p], r_km[h], rb_km[h], r_lf[h], rb_lf[h])

            def p2(h):
                p = h % 2
                bb, b_bb = r_e[h], rb_e[h]
                lf, b_lf = r_lf[h], rb_lf[h]
                bg = b_gdecs[h]
                kk = p
                ops = [
                    lambda: P.op("dve", lambda E: E.tensor_tensor_scan(out=bb[:, :], data0=omask[:, :], data1=lf[:, :], initial=0.0,
                                                                       op0=ALU.mult, op1=ALU.add),
                                 reads=[b_omask, b_lf], writes=[b_bb]),
                    lambda: P.op("dve", lambda E: E.tensor_scalar(out=gdec[:, h, 2:3], in0=bb[:, T - 1:T], scalar1=-40.0, scalar2=0.0,
                                                                  op0=ALU.max, op1=ALU.add), reads=[b_bb, bg], writes=[bg]),
                    lambda: P.op("dve", lambda E: E.tensor_scalar(out=lf[:, :], in0=bb[:, :], scalar1=bb[:, T - 1:T], scalar2=40.0,
                                                                  op0=ALU.subtract, op1=ALU.min), reads=[b_bb], writes=[b_lf]),
                    lambda: P.op("act", lambda E: E.activation(out=lf[:, :], in_=lf[:, :], func=AF.Exp, scale=-1.0),
                                 reads=[b_lf], writes=[b_lf]),
                    lambda: P.op("dve", lambda E: E.tensor_scalar(out=lf[:, :], in0=lf[:, :], scalar1=gdec[:, h, 0:1], scalar2=1e-30,
                                                                  op0=ALU.mult, op1=ALU.max),
                                 reads=[b_lf, bg], writes=[b_lf]),
                    lambda: P.op("dve", lambda E: E.tensor_tensor(out=koutT[kk], in0=r_km[h][:, :], in1=lf[:, :], op=ALU.mult),
                                 reads=[rb_km[h], b_lf], writes=[b_koutT[kk]] + b_act),
                    lambda: P.op("act", lambda E: E.activation(out=gdec[:, h, 1:2], in_=gdec[:, h, 2:3], func=AF.Exp),
                                 reads=[bg], writes=[bg]),
                    lambda: P.op("dve", lambda E: E.tensor_scalar(out=gdec[:, h, 0:1], in0=gdec[:, h, 0:1], scalar1=gdec[:, h, 1:2],
                                                                  scalar2=1e-30, op0=ALU.mult, op1=ALU.max),
                                 reads=[bg], writes=[bg]),
                ]
                return ops

            hooks = hooks or {}
            hk = lambda name: hooks[name]() if name in hooks else None
            hk("start")

            def tails(h):
                for th in kout_tail(h, h % 2):
                    th()

            lockstep(p1(0), p1(1))
            vtile(0)
            hk("mid1")
            lockstep(p2(0), p2(1), p1(2), p1(3))
            vtile(1)
            vtile(2)
            tails(0)
            tails(1)
            hk("mid2")
            lockstep(p2(2), p2(3))
            vtile(3)
            tails(2)
            tails(3)
            release(2)
            hk("end")
            for h in range(4):
                hc = h * 128

                def mms(E, h=h, hc=hc):
                    last = None
                    for t in range(NT):
                        last = E.matmul(pS_[:, hc:hc + 128], lhsT=kout[:, t, hc:hc + 128], rhs=v[:, t, hc:hc + 128],
                                        start=(first and h == 0 and t == 0), stop=False, skip_group_check=True)
                    return last
                P.op("pe", mms, reads=[b_kout[h]] + b_v, writes=[b_pS_])

        def prefix_finish():
            pS_, b_pS_ = FB[5], b_FB[5]
            dS_t, dS_b = tmp[0], b_tmp[0]
            P.op("dve", lambda E: E.tensor_copy(out=dS_t[:, :], in_=pS_[:, :]), reads=[b_pS_], writes=[dS_b])
            P.op("dve", lambda E: E.tensor_copy(out=S[:, 0, :, :].rearrange("p h d -> p (h d)"), in_=dS_t[:, :]),
                 reads=[dS_b], writes=b_S[0])
            P.op("act", lambda E: E.activation(out=Sb[:, 0, :, :].rearrange("p h d -> p (h d)"), in_=dS_t[:, :], func=AF.Copy),
                 reads=[dS_b], writes=b_Sb[0])

        def core(blk_par, main, last_pre):
            po = FB[0:4]
            b_po = b_FB[0:4]
            if main:
                items = [(h, t) for h in range(4) for t in range(NT)]

                def emit_psc(i):
                    h, t = items[i]
                    tc_ = slice(t * 128, (t + 1) * 128)
                    psc, b_psc = FB[4 + i % 2], b_FB[4 + i % 2]
                    P.op("pe", lambda E: E.matmul(psc[:, 0:128], lhsT=khat[:, h, tc_], rhs=qin[:, h, tc_], start=True, stop=True),
                         reads=[b_khat[h], b_qin[h]], writes=[b_psc])

                def emit_po(i):
                    h, t = items[i]
                    hc = h * 128
                    tc_ = slice(t * 128, (t + 1) * 128)
                    psc, b_psc = FB[4 + i % 2], b_FB[4 + i % 2]
                    ai = i % 4
                    P.op("dve", lambda E: E.tensor_tensor(out=AT[ai][:, :], in0=psc[:, 0:128], in1=maskT[:, :], op=ALU.mult),
                         reads=[b_psc, b_maskT], writes=[b_AT[ai]])
                    P.op("pe", lambda E: E.matmul(po[h][:, tc_], lhsT=v[:, t, hc:hc + 128], rhs=AT[ai][:, :],
                                                  start=(t == 0), stop=False, skip_group_check=True),
                         reads=[b_v[t], b_AT[ai]], writes=[b_po[h]])
                emit_psc(0)
                for i in range(len(items)):
                    if i + 1 < len(items):
                        emit_psc(i + 1)
                    emit_po(i)

            def emit_pS(cc):
                t = cc // 2
                r0 = (cc % 2) * CH
                bank = 4 + cc % 2
                for h in range(4):
                    hc = h * 128
                    P.op("pe", lambda E, h=h, hc=hc: E.matmul(
                        FB[bank][:, hc:hc + 128], lhsT=kout[r0:r0 + CH, t, hc:hc + 128], rhs=v[r0:r0 + CH, t, hc:hc + 128],
                        start=True, stop=True),
                         reads=[b_kout[h], b_v[t]], writes=[b_pS[cc % 2][h], b_FB[bank]])
            emit_pS(0)
            for cc in range(NCH):
                if cc + 1 < NCH:
                    emit_pS(cc + 1)
                bank = 4 + cc % 2
                par = chain["step"] % 2
                npar = 1 - par
                need_bf = main or (last_pre and cc == NCH - 1)
                for h in range(4):
                    hc = h * 128
                    if main:
                        P.op("pe", lambda E, h=h, cc=cc, par=par: E.matmul(
                            po[h][:, cc * CH:(cc + 1) * CH], lhsT=Sb[:, par, h, :], rhs=qin[:, h, cc * CH:(cc + 1) * CH],
                            start=False, stop=True, skip_group_check=True),
                             reads=[b_Sb[par][h], b_qin[h]], writes=[b_po[h]])
                for h in range(4):
                    hc = h * 128
                    P.op("dve", lambda E, h=h, hc=hc, cc=cc, par=par, npar=npar, bank=bank: E.scalar_tensor_tensor(
                        out=S[:, npar, h, :], in0=S[:, par, h, :], scalar=decs[:, blk_par, h, cc:cc + 1],
                        in1=FB[bank][:, hc:hc + 128], op0=ALU.mult, op1=ALU.add),
                         reads=[b_S[par][h], b_decs[blk_par][h], b_pS[cc % 2][h], b_FB[bank]], writes=[b_S[npar][h]])
                    if need_bf:
                        P.op("act", lambda E, h=h, npar=npar: E.activation(out=Sb[:, npar, h, :], in_=S[:, npar, h, :], func=AF.Copy),
                             reads=[b_S[npar][h]], writes=[b_Sb[npar][h]])
                chain["step"] += 1
            if main:
                sq2s = [T_() for _ in range(4)]
                rss = [T_() for _ in range(4)]
                pS_deps = [x for k3 in range(2) for x in b_pS[k3]]
                for h in range(4):
                    sq2, b_sq2 = sq2s[h]
                    P.op("act", lambda E, h=h, sq2=sq2: E.activation(out=sq2[:, :], in_=po[h][:, :], func=AF.Square),
                         reads=[b_po[h]], writes=[b_sq2])
                for hh in range(2):
                    for h in (2 * hh, 2 * hh + 1):
                        sq2, b_sq2 = sq2s[h]
                        rs, b_rs = rss[h]
                        pss, b_pss = (FB[4], b_FB[4]) if (h % 2 == 0) else (FB[5], b_FB[5])
                        P.op("pe", lambda E, pss=pss, sq2=sq2: E.matmul(pss[:, :], lhsT=ones_f[:, :], rhs=sq2[:, :], start=True, stop=True),
                             reads=[b_ones, b_sq2], writes=[b_pss] + pS_deps)
                        P.op("act", lambda E, pss=pss, rs=rs: E.activation(out=rs[:, :], in_=pss[:, :], func=AF.Ln, scale=1.0 / 128,
                                                                           bias=cst[:, EPS:EPS + 1]),
                             reads=[b_pss, b_cst], writes=[b_rs])
                    for h in (2 * hh, 2 * hh + 1):
                        rs, b_rs = rss[h]
                        P.op("act", lambda E, rs=rs: E.activation(out=rs[:, :], in_=rs[:, :], func=AF.Exp, scale=-0.5),
                             reads=[b_rs], writes=[b_rs])
                for h in range(4):
                    rs, b_rs = rss[h]
                    t1, b_t1 = sq2s[h]
                    P.op("dve", lambda E, h=h, t1=t1, rs=rs: E.scalar_tensor_tensor(
                        out=t1[:, :], in0=po[h][:, :], scalar=cst[:, HGN:HGN + 1], in1=rs[:, :], op0=ALU.mult, op1=ALU.mult),
                         reads=[b_po[h], b_cst, b_rs], writes=[b_t1])
                    P.op("dve", lambda E, h=h, t1=t1: E.tensor_tensor(out=og[:, h, :], in0=t1[:, :], in1=gsil[:, h, :], op=ALU.mult),
                         reads=[b_t1, b_gsil[h]], writes=[b_og[h], b_h2T])

        def conv_u(hp):
            hT = hTs[hp]
            b_hT = b_hTs[hp]
            sc_, b_sc = acquire("c")
            sx_, b_sx = acquire("xb")
            for m in range(4):
                pc, b_pc = proj_fm(sc_, b_sc, 8, m * 128, hT, [b_hT])
                cs, b_cs = T_()
                P.op("act", lambda E, pc=pc, cs=cs: E.activation(out=cs[:, :], in_=pc[:, :], func=AF.Copy),
                     reads=[b_pc], writes=[b_cs])
                px, b_px = proj_fm(sx_, b_sx, 8, m * 128, hT, [b_hT])
                P.op("dve", lambda E, m=m, cs=cs, px=px: E.tensor_tensor(out=u[:, m, 2:T + 2], in0=cs[:, :], in1=px[:, :], op=ALU.mult),
                     reads=[b_cs, b_px], writes=[b_u])

        def halo():
            P.op("dve", lambda E: E.tensor_copy(out=u[:, :, 0:2], in_=u[:, :, T:T + 2]), reads=[b_u], writes=[b_u])

        def stage2(hp):
            hT = hTs[hp]
            b_hT = b_hTs[hp]
            conv_u(hp)
            sbg, b_sbg = acquire("bg")
            for m in range(4):
                py, b_py = F_()

                def mmc(E, m=m, py=py):
                    last = None
                    for k in range(3):
                        last = E.matmul(py[:, :], lhsT=diagw[:, k * 4 + m, :], rhs=u[:, m, k:k + T], start=(k == 0), stop=(k == 2))
                    return last
                P.op("pe", mmc, reads=[b_diagw, b_u], writes=[b_py])
                pb, b_pb = proj_fm(sbg, b_sbg, 8, m * 128, hT, [b_hT])
                bs, b_bs = T_()
                P.op("act", lambda E, pb=pb, bs=bs: E.activation(out=bs[:, :], in_=pb[:, :], func=AF.Copy),
                     reads=[b_pb], writes=[b_bs])
                P.op("dve", lambda E, m=m, bs=bs, py=py: E.tensor_tensor(out=yb[:, m, :], in0=bs[:, :], in1=py[:, :], op=ALU.mult),
                     reads=[b_bs, b_py], writes=[b_yb[m], b_h2T])
            halo()
            release(3)

        def stage3(hp):
            hT = hTs[hp]
            b_hT = b_hTs[hp]
            for half in range(2):
                sga, b_sga = acquire(f"ga{half}")
                swa, b_swa = acquire(f"wa{half}")
                sgb, b_sgb = acquire(f"gb{half}")
                swb, b_swb = acquire(f"wb{half}")
                for mm_ in range(4):
                    m = half * 4 + mm_
                    pga, b_pga = proj_fm(sga, b_sga, 8, mm_ * 128, hT, [b_hT])
                    sa, b_sa = T_()
                    P.op("act", lambda E, pga=pga, sa=sa: E.activation(out=sa[:, :], in_=pga[:, :], func=AF.Sigmoid),
                         reads=[b_pga], writes=[b_sa])
                    pya, b_pya = proj_fm(swa, b_swa, 4, mm_ * 128, og, b_og)
                    ma, b_ma = T_()
                    P.op("dve", lambda E, sa=sa, pya=pya, ma=ma: E.tensor_tensor(out=ma[:, :], in0=sa[:, :], in1=pya[:, :], op=ALU.mult),
                         reads=[b_sa, b_pya], writes=[b_ma])
                    pgb, b_pgb = proj_fm(sgb, b_sgb, 8, mm_ * 128, hT, [b_hT])
                    sb_, b_sb = T_()
                    P.op("act", lambda E, pgb=pgb, sb_=sb_: E.activation(out=sb_[:, :], in_=pgb[:, :], func=AF.Sigmoid),
                         reads=[b_pgb], writes=[b_sb])
                    pyb, b_pyb = proj_fm(swb, b_swb, 4, mm_ * 128, yb, b_yb)
                    mb, b_mb = T_()
                    P.op("dve", lambda E, sb_=sb_, pyb=pyb, mb=mb: E.tensor_tensor(out=mb[:, :], in0=sb_[:, :], in1=pyb[:, :], op=ALU.mult),
                         reads=[b_sb, b_pyb], writes=[b_mb])
                    P.op("dve", lambda E, m=m, ma=ma, mb=mb: E.tensor_tensor(out=merged[:, m, :], in0=ma[:, :], in1=mb[:, :], op=ALU.add),
                         reads=[b_ma, b_mb], writes=[b_merged[m]])
                release(4)

        def stage4_preload(row0):
            return {t: load_x(x_main, row0 + t * 128) for t in (0, 1)}

        def stage4(row0, xk):
            so = [acquire("wo0"), acquire("wo1")]
            ks = {}

            def wo(t):
                k = xk[t] if t in xk else load_x(x_main, row0 + t * 128)
                for n in range(2):
                    px1, b_px1 = F_()

                    def mmo(E, n=n, px1=px1):
                        last = None
                        for c in range(8):
                            last = E.matmul(px1[:, :], lhsT=merged[:, c, t * 128:(t + 1) * 128], rhs=so[n][0][:, c, :],
                                            start=(c == 0), stop=(c == 7))
                        return last
                    P.op("pe", mmo, reads=b_merged + [so[n][1]], writes=[b_px1])
                    P.op("dve", lambda E, n=n, px1=px1: E.tensor_tensor(
                        out=x1[:, t, n * 512:(n + 1) * 512], in0=px1[:, :], in1=xt[k][:, n * 512:(n + 1) * 512], op=ALU.add),
                         reads=[b_px1, b_xt[k]], writes=[b_x1[t]])

            def stat_(t):
                ks[t] = norm_stat(x1[:, t, :], [b_x1[t]])

            def xs_(t):
                norm_xs(ks[t], x1[:, t, :], [b_x1[t]])

            def nb(t):
                norm_b(ks[t], G2, h2T, b_h2T, t, extra_w=B_CH)

            wo(0)
            stat_(0)
            wo(1)
            xs_(0)
            stat_(1)
            wo(2)
            xs_(1)
            stat_(2)
            nb(0)
            wo(3)
            release(2)
            xs_(2)
            stat_(3)
            nb(1)
            xs_(3)
            nb(2)
            nb(3)

        def stage5(row0, mid_hook=None):
            if mid_hook is not None and -1 in mid_hook:
                mid_hook[-1]()
            for j in range(6):
                sg_, b_sg = acquire(f"fg{j}")
                su_, b_su = acquire(f"fu{j}")
                for jj in range(4 if j < 5 else 2):
                    jc = j * 4 + jj
                    pg, b_pg = proj_fm(sg_, b_sg, 8, jj * 128, h2T, [b_h2T])
                    sl, b_sl = T_()
                    P.op("act", lambda E, pg=pg, sl=sl: E.activation(out=sl[:, :], in_=pg[:, :], func=AF.Silu),
                         reads=[b_pg], writes=[b_sl])
                    pu, b_pu = proj_fm(su_, b_su, 8, jj * 128, h2T, [b_h2T])
                    P.op("dve", lambda E, jc=jc, sl=sl, pu=pu: E.tensor_tensor(out=act[:, jc, :], in0=sl[:, :], in1=pu[:, :], op=ALU.mult),
                         reads=[b_sl, b_pu], writes=[b_act[jc]] + A_CH)
                release(2)
                if mid_hook is not None and j in mid_hook:
                    mid_hook[j]()
            for n in range(2):
                sd = [acquire(f"fd{n}{k}") for k in range(3)]
                for t in range(NT):
                    pd, b_pd = F_()

                    def mmd(E, t=t, pd=pd, sd=sd):
                        last = None
                        for jc in range(22):
                            last = E.matmul(pd[:, :], lhsT=act[:, jc, t * 128:(t + 1) * 128], rhs=sd[jc // 8][0][:, jc % 8, :],
                                            start=(jc == 0), stop=(jc == 21))
                        return last
                    P.op("pe", mmd, reads=b_act + [s_[1] for s_ in sd], writes=[b_pd])
                    P.op("dve", lambda E, t=t, n=n, pd=pd: E.tensor_tensor(
                        out=x1[:, t, n * 512:(n + 1) * 512], in0=pd[:, :], in1=x1[:, t, n * 512:(n + 1) * 512], op=ALU.add),
                         reads=[b_pd, b_x1[t]], writes=[b_x1[t]])
                    if n == 1:
                        k = cnt["st"] % NXS
                        cnt["st"] += 1
                        P.op("act", lambda E, t=t, k=k: E.activation(out=junk[:, :], in_=x1[:, t, :], func=AF.Square,
                                                                    accum_out=stat[k][:, 2:3]),
                             reads=[b_x1[t]], writes=[b_stat[k]])
                        P.op("act", lambda E, k=k: E.activation(out=stat[k][:, 3:4], in_=stat[k][:, 2:3], func=AF.Ln, scale=1.0 / D,
                                                                bias=cst[:, EPS:EPS + 1]),
                             reads=[b_stat[k], b_cst], writes=[b_stat[k]])
                        P.op("act", lambda E, k=k: E.activation(out=stat[k][:, 3:4], in_=stat[k][:, 3:4], func=AF.Exp, scale=-0.5),
                             reads=[b_stat[k]], writes=[b_stat[k]])
                        P.op("dve", lambda E, t=t, k=k: E.scalar_tensor_tensor(
                            out=x1[:, t, :], in0=x1[:, t, :], scalar=stat[k][:, 3:4], in1=cst[:, GF:GF + D],
                            op0=ALU.mult, op1=ALU.mult),
                             reads=[b_x1[t], b_stat[k], b_cst], writes=[b_x1[t]])
                        P.dma("sp", lambda E, t=t: E.dma_start(out=out_d[row0 + t * 128:row0 + (t + 1) * 128, :], in_=x1[:, t, :]),
                              reads=[b_x1[t]], chan=b_x1[t])
                release(3)

        blocks = [("pre", pb) for pb in reversed(range(NPRE))] + [("main", mb) for mb in range(NMAIN)]

        def src_of(bi):
            kind, idx = blocks[bi]
            return (x_prev if kind == "pre" else x_main), idx * T

        s_, r_ = src_of(0)
        cnt["fbmod"] = 5
        ks0 = stage0_a(s_, r_)
        for _ in range(R):
            issue_next()
        stage0_b(ks0, 0)
        for bi, (kind, idx) in enumerate(blocks):
            hp = bi % 2
            nxt = {"xk": {}, "ks": {}}
            has_next = bi + 1 < len(blocks)

            def pre_load(tiles, bi=bi, nxt=nxt, has_next=has_next):
                if has_next:
                    s2, r2 = src_of(bi + 1)
                    stage0_load(s2, r2, tiles, nxt)

            def pre_norm(tiles, nxt=nxt, has_next=has_next):
                if has_next:
                    stage0_norm(tiles, nxt)

            def pre_b(bi=bi, nxt=nxt, has_next=has_next):
                if has_next:
                    stage0_b([nxt["ks"][t] for t in range(NT)], (bi + 1) % 2)

            def h_start(pre_load=pre_load):
                pre_load([0, 1])

            def h_mid1(pre_load=pre_load, pre_norm=pre_norm):
                pre_norm([0, 1])
                pre_load([2, 3])

            def h_mid2(pre_norm=pre_norm):
                pre_norm([2, 3])

            if kind == "pre":
                stage1_pre(hp, first=(bi == 0), hooks={"start": h_start, "mid1": h_mid1, "mid2": h_mid2, "end": pre_b})
                if bi == 0:
                    conv_u(hp)
                    halo()
                    release(2)
                precache(NPC)
                if bi == NPRE - 1:
                    prefix_finish()
                    cnt["fbmod"] = 6
            else:
                stage1(bi % 2, hp, True)
                core(bi % 2, True, False)
                stage2(hp)
                xk4 = stage4_preload(idx * T)
                stage3(hp)
                stage4(idx * T, xk4)
                stage5(idx * T, mid_hook={-1: h_start, 0: h_mid1, 2: h_mid2, 4: pre_b})
        P.final_wait("sp", b_x1)
        P.emit(block)
    return nc


DEBUG = False
_NC_CACHE = {}


def _get_nc():
    if "nc" not in _NC_CACHE:
        _NC_CACHE["nc"] = build_nc()
    return _NC_CACHE["nc"]


def kernel(x, norm_mix_g, w_in, lower_bounds, hg_norm_g, conv_w, w_branch_a, w_branch_b,
           w_out, norm_ffn_g, w_ffn_gate, w_ffn_up, w_ffn_down, norm_final_g):
    f32 = np.float32
    x = np.asarray(x, f32)
    cst = np.zeros((128, NCST), f32)
    cst[:, G1:G1 + 8] = np.asarray(norm_mix_g, f32)[0].reshape(8, 128).T
    cst[:, G2:G2 + 8] = np.asarray(norm_ffn_g, f32)[0].reshape(8, 128).T
    lbs = np.asarray(lower_bounds, f32)
    cst[:, L0:L0 + 4] = lbs[0].reshape(4, 128).T
    cst[:, L1:L1 + 4] = lbs[1].reshape(4, 128).T
    cst[:, HGN] = np.asarray(hg_norm_g, f32)[0]
    cw = np.asarray(conv_w, f32)[0]
    for k in range(3):
        cst[:, CW + k * 4:CW + k * 4 + 4] = cw[k].reshape(4, 128).T
    cst[:, EPS] = 1e-6
    cst[:, ONE] = 1.0
    cst[:, GF:GF + D] = np.asarray(norm_final_g, f32)[None, :]
    weights = {
        "w_in": np.ascontiguousarray(np.asarray(w_in, f32)[0]),
        "w_a": np.ascontiguousarray(np.asarray(w_branch_a, f32)[0]),
        "w_b": np.ascontiguousarray(np.asarray(w_branch_b, f32)[0]),
        "w_out": np.ascontiguousarray(np.asarray(w_out, f32)[0]),
        "w_g": np.ascontiguousarray(np.asarray(w_ffn_gate, f32)[0]),
        "w_u": np.ascontiguousarray(np.asarray(w_ffn_up, f32)[0]),
        "w_d": np.ascontiguousarray(np.asarray(w_ffn_down, f32)[0]),
    }
    in_maps = []
    zeros = np.zeros((NPRE * T, D), f32)
    for i in range(8):
        b, half = i // 2, i % 2
        m = {"x_main": np.ascontiguousarray(x[b, half * TOK:(half + 1) * TOK]),
             "x_prev": np.ascontiguousarray(x[b, 0:TOK]) if half == 1 else zeros,
             "cst": cst}
        m.update(weights)
        in_maps.append(m)
    nc = _get_nc()
    res = run_bass_kernel_spmd(nc, in_maps, core_ids=list(range(8)))
    out = np.empty((4, 2 * TOK, D), f32)
    for i in range(8):
        b, half = i // 2, i % 2
        out[b, half * TOK:(half + 1) * TOK] = res.results[i]["out"]
    return out
```
